# Optimizing a Trainium2 kernel written in Bass

```python
import math
import jax, jax.numpy as jnp
from jax import lax
import numpy as np

D_MODEL = 1024
BATCH = 16
SEQ = 2048
DEPTH = 4
DEC_BATCH = 8
DEC_SEQ = 64
PAST_LEN = 2048

CHUNK = 64
HEAD_DIM = 64
N_HEADS = D_MODEL // HEAD_DIM
H_GROUP = N_HEADS // 4
G_WIDTH = H_GROUP * HEAD_DIM
CONV_W = 4
SB_QBLOCK = 128
D_FF = 4 * D_MODEL
DN_ALPHA = (2 * DEPTH) ** 0.25
DN_BETA = (8 * DEPTH) ** -0.25
NORM_EPS = 1e-5
NEG_BIG = -1e30
SPLIT_SIZES = (3 * G_WIDTH, G_WIDTH, H_GROUP, H_GROUP,
               3 * G_WIDTH,
               3 * G_WIDTH, G_WIDTH, H_GROUP, H_GROUP,
               G_WIDTH, G_WIDTH, G_WIDTH, G_WIDTH)
D_IN = sum(SPLIT_SIZES)

kernel_name = "hybrid_streaming_encoder_step"


def _rmsnorm(x, g):
    x32 = x.astype(jnp.float32)
    return x32 * lax.rsqrt(jnp.mean(x32 * x32, axis=-1, keepdims=True) + NORM_EPS) * g.astype(jnp.float32)


def _layernorm(x, g, b):
    x32 = x.astype(jnp.float32)
    mu = jnp.mean(x32, axis=-1, keepdims=True)
    var = jnp.mean(jnp.square(x32 - mu), axis=-1, keepdims=True)
    return (x32 - mu) * lax.rsqrt(var + NORM_EPS) * g.astype(jnp.float32) + b.astype(jnp.float32)


def _l2norm(x):
    return x * lax.rsqrt(jnp.sum(x * x, axis=-1, keepdims=True) + 1e-6)


def _masked_exp(mask, logits):
    return jnp.where(mask, jnp.exp(jnp.where(mask, logits, 0.0)), 0.0)


def _to_chunks(t, c):
    b, n_t, h = t.shape[:3]
    t = t.reshape((b, n_t // c, c, h) + t.shape[3:])
    return jnp.moveaxis(t, (1, 3), (0, 2))


def _from_chunks(t):
    t = jnp.moveaxis(t, (0, 2), (1, 3))
    b, n, c, h, d = t.shape
    return t.reshape(b, n * c, h, d)


def _causal_conv(u_ext, w):
    ch = u_ext.shape[-1]
    return lax.conv_general_dilated(u_ext, w[:, None, :], window_strides=(1,), padding='VALID',
                                    dimension_numbers=('NWC', 'WIO', 'NWC'), feature_group_count=ch)


def _gated_delta(q, k, v, beta, g, s0):
    c = min(CHUNK, q.shape[1])
    q = _l2norm(q) * HEAD_DIM ** -0.5
    k = _l2norm(k)
    tri = jnp.tril(jnp.ones((c, c), bool))
    tri_s = jnp.tril(jnp.ones((c, c), bool), -1)
    eye = jnp.eye(c, dtype=jnp.float32)

    def step(s, inp):
        qc, kc, vc, bc, gc = inp
        gcum = jnp.cumsum(gc, axis=-1)
        gam = _masked_exp(tri, gcum[..., :, None] - gcum[..., None, :])
        kb = kc * bc[..., None]
        lm = jnp.where(tri_s, jnp.einsum('bhtd,bhsd->bhts', kb, kc) * gam, 0.0)
        rhs = jnp.concatenate([vc * bc[..., None], kb * jnp.exp(gcum)[..., None]], axis=-1)
        sol = lax.linalg.triangular_solve(eye + lm, rhs, left_side=True, lower=True, unit_diagonal=True)
        u, w = sol[..., :HEAD_DIM], sol[..., HEAD_DIM:]
        v_new = u - jnp.einsum('bhtk,bhkv->bhtv', w, s)
        attn = jnp.einsum('bhtd,bhsd->bhts', qc, kc) * gam
        o = jnp.einsum('bhtk,bhkv->bhtv', qc * jnp.exp(gcum)[..., None], s) + jnp.einsum('bhts,bhsv->bhtv', attn, v_new)
        g_last = gcum[..., -1:]
        s = s * jnp.exp(g_last)[..., None] + jnp.einsum('bhsk,bhsv->bhkv', kc * jnp.exp(g_last - gcum)[..., None], v_new)
        return s, o

    s, o = lax.scan(step, s0, tuple(_to_chunks(t, c) for t in (q, k, v, beta, g)))
    return _from_chunks(o), s


def _stick_breaking(q, k, v, q_offset):
    b, tq, h, d = q.shape
    qb_len = min(SB_QBLOCK, tq)
    nb = tq // qb_len
    qblocks = jnp.moveaxis(q.reshape(b, nb, qb_len, h, d), 1, 0)
    kpos = jnp.arange(k.shape[1])

    def block(args):
        qi, bi = args
        qpos = q_offset + bi * qb_len + jnp.arange(qb_len)
        z = jnp.einsum('bqhd,bkhd->bhqk', qi, k) * d ** -0.5
        mask = kpos[None, :] < qpos[:, None]
        log1m = jnp.where(mask, jax.nn.log_sigmoid(-z), 0.0)
        after = lax.cumsum(log1m, axis=3, reverse=True) - log1m
        a = jnp.where(mask, jnp.exp(jax.nn.log_sigmoid(z) + after), 0.0)
        return jnp.einsum('bhqk,bkhd->bqhd', a, v)

    o = lax.map(block, (qblocks, jnp.arange(nb)))
    return jnp.moveaxis(o, 0, 1).reshape(b, tq, h, d)


def _mlstm(q, k, v, i_pre, log_f, c0, n0, m0):
    c = min(CHUNK, q.shape[1])
    k = k * HEAD_DIM ** -0.5
    tri = jnp.tril(jnp.ones((c, c), bool))

    def step(carry, inp):
        cm, nv, m = carry
        qc, kc, vc, ic, fc = inp
        bcum = jnp.cumsum(fc, axis=-1)
        dmat = jnp.where(tri, bcum[..., :, None] - bcum[..., None, :] + ic[..., None, :], NEG_BIG)
        inter = bcum + m[..., None]
        mt = jnp.maximum(inter, jnp.max(dmat, axis=-1))
        a_prev = jnp.exp(inter - mt)
        qk = jnp.einsum('bhtd,bhsd->bhts', qc, kc) * _masked_exp(tri, dmat - mt[..., None])
        num = a_prev[..., None] * jnp.einsum('bhtk,bhkv->bhtv', qc, cm) + jnp.einsum('bhts,bhsv->bhtv', qk, vc)
        den = a_prev * jnp.einsum('bhtk,bhk->bht', qc, nv) + jnp.sum(qk, axis=-1)
        hc = num / jnp.maximum(jnp.abs(den), jnp.exp(-mt))[..., None]
        b_last = bcum[..., -1]
        dl = b_last[..., None] - bcum + ic
        m_new = jnp.maximum(b_last + m, jnp.max(dl, axis=-1))
        ws = jnp.exp(dl - m_new[..., None])
        dec = jnp.exp(b_last + m - m_new)
        cm = dec[..., None, None] * cm + jnp.einsum('bhs,bhsk,bhsv->bhkv', ws, kc, vc)
        nv = dec[..., None] * nv + jnp.einsum('bhs,bhsk->bhk', ws, kc)
        return (cm, nv, m_new), hc

    (cm, nv, m), hs = lax.scan(step, (c0, n0, m0), tuple(_to_chunks(t, c) for t in (q, k, v, i_pre, log_f)))
    return _from_chunks(hs), cm, nv, m


def _hgrn2(q, k, v, log_f, s0):
    c = min(CHUNK, q.shape[1])
    tri = jnp.tril(jnp.ones((c, c), bool))[..., None]

    def step(s, inp):
        qc, kc, vc, fc = inp
        lcum = jnp.cumsum(fc, axis=-2)
        dec = _masked_exp(tri, lcum[..., :, None, :] - lcum[..., None, :, :])
        a = jnp.einsum('bhtk,bhsk,bhtsk->bhts', qc, kc, dec)
        o = jnp.einsum('bhtk,bhkv->bhtv', qc * jnp.exp(lcum), s) + jnp.einsum('bhts,bhsv->bhtv', a, vc)
        l_last = lcum[..., -1:, :]
        s = jnp.exp(l_last)[..., 0, :, None] * s + jnp.einsum('bhsk,bhsv->bhkv', kc * jnp.exp(l_last - lcum), vc)
        return s, o

    s, o = lax.scan(step, s0, tuple(_to_chunks(t, c) for t in (q, k, v, log_f)))
    return _from_chunks(o), s


def _layer(x, past, weights):
    (sb_k_past, sb_v_past, conv_buf, gdn_s, ml_c, ml_n, ml_m, hg_s) = past
    (w_in, conv_w, a_log, dt_bias, b_i, b_f, lb, out_g, w_out,
     ln1_g, ln1_b, w1, b1, w2, b2, ln2_g, ln2_b) = weights
    f32 = jnp.float32
    dtype = x.dtype
    b, t, _ = x.shape
    heads = lambda a: a.reshape(b, t, H_GROUP, HEAD_DIM)
    proj = jnp.matmul(x, w_in).astype(f32)
    (gdn_qkv, gdn_gate, gdn_beta, gdn_a, sb_qkv, ml_qkv, ml_o, ml_i, ml_f,
     hg_q, hg_f, hg_i, hg_g) = jnp.split(proj, np.cumsum(SPLIT_SIZES)[:-1].tolist(), axis=-1)

    conv_ext = jnp.concatenate([conv_buf.astype(f32), gdn_qkv], axis=1)
    qkv_a = jax.nn.silu(_causal_conv(conv_ext, conv_w.astype(f32)))
    qa, ka, va = (heads(a) for a in jnp.split(qkv_a, 3, axis=-1))
    beta = jax.nn.sigmoid(gdn_beta)
    g = -jnp.exp(a_log.astype(f32)) * jax.nn.softplus(gdn_a + dt_bias.astype(f32))
    o_a, gdn_s_new = _gated_delta(qa, ka, va, beta, g, gdn_s.astype(f32))
    conv_new = conv_ext[:, -(CONV_W - 1):]

    qb, kb, vb = (heads(a) for a in jnp.split(sb_qkv, 3, axis=-1))
    k_all = jnp.concatenate([sb_k_past.astype(f32), kb], axis=1)
    v_all = jnp.concatenate([sb_v_past.astype(f32), vb], axis=1)
    o_b = _stick_breaking(qb, k_all, v_all, sb_k_past.shape[1])

    qc, kc, vc = (heads(a) for a in jnp.split(ml_qkv, 3, axis=-1))
    h_c, ml_c_new, ml_n_new, ml_m_new = _mlstm(qc, kc, vc, ml_i + b_i.astype(f32),
                                                 jax.nn.log_sigmoid(ml_f + b_f.astype(f32)),
                                                 ml_c.astype(f32), ml_n.astype(f32), ml_m.astype(f32))
    o_c = jax.nn.sigmoid(heads(ml_o)) * h_c

    lb_h = lb.astype(f32).reshape(H_GROUP, HEAD_DIM)
    z = heads(hg_f)
    log_f = jnp.log(lb_h + (1.0 - lb_h) * jax.nn.sigmoid(z))
    k_d = (1.0 - lb_h) * jax.nn.sigmoid(-z)
    o_d, hg_s_new = _hgrn2(jax.nn.silu(heads(hg_q)), k_d, heads(hg_i), log_f, hg_s.astype(f32))

    o = _rmsnorm(jnp.concatenate([o_a, o_b, o_c, o_d], axis=2), out_g.reshape(N_HEADS, HEAD_DIM))
    n_a, n_b, n_c, n_d = jnp.split(o, 4, axis=2)
    o = jnp.concatenate([n_a * jax.nn.silu(heads(gdn_gate)), n_b, n_c,
                         n_d * jax.nn.sigmoid(heads(hg_g))], axis=2).reshape(b, t, D_MODEL)
    mix = jnp.matmul(o.astype(dtype), w_out)
    x = _layernorm(DN_ALPHA * x.astype(f32) + mix.astype(f32), ln1_g, ln1_b).astype(dtype)
    hmid = jnp.square(jax.nn.relu(jnp.matmul(x, w1) + b1))
    ff = jnp.matmul(hmid, w2) + b2
    x = _layernorm(DN_ALPHA * x.astype(f32) + ff.astype(f32), ln2_g, ln2_b).astype(dtype)
    return x, (kb, vb, conv_new, gdn_s_new, ml_c_new, ml_n_new, ml_m_new, hg_s_new)


def setup_inputs(seed: int = 0) -> dict:
    key = jax.random.key(seed)
    f32 = jnp.float32
    ks = jax.random.split(key, 32)

    def nrm(i, shape, scale=1.0):
        return jax.random.normal(ks[i], shape, f32) * scale

    hd = (H_GROUP, HEAD_DIM)
    x_prompt = nrm(0, (BATCH, SEQ, D_MODEL))
    x_sample = nrm(1, (DEC_BATCH, DEC_SEQ, D_MODEL))
    cache_sb_k = nrm(2, (DEPTH, DEC_BATCH, PAST_LEN) + hd)
    cache_sb_v = nrm(3, (DEPTH, DEC_BATCH, PAST_LEN) + hd)
    state_gdn_conv = nrm(4, (DEPTH, DEC_BATCH, CONV_W - 1, 3 * G_WIDTH))
    state_gdn_S = nrm(5, (DEPTH, DEC_BATCH) + hd + (HEAD_DIM,), 0.1)
    state_mlstm_C = nrm(6, (DEPTH, DEC_BATCH) + hd + (HEAD_DIM,), 0.1)
    state_mlstm_n = nrm(7, (DEPTH, DEC_BATCH) + hd, 0.1)
    state_mlstm_m = nrm(8, (DEPTH, DEC_BATCH, H_GROUP))
    state_hgrn_S = nrm(9, (DEPTH, DEC_BATCH) + hd + (HEAD_DIM,), 0.1)
    w_in = nrm(10, (DEPTH, D_MODEL, D_IN), D_MODEL ** -0.5)
    gdn_conv_w = nrm(11, (DEPTH, CONV_W, 3 * G_WIDTH), CONV_W ** -0.5)
    gdn_A_log = jnp.log(jax.random.uniform(ks[12], (DEPTH, H_GROUP), f32, 1.0, 16.0))
    dt = jnp.exp(jax.random.uniform(ks[13], (DEPTH, H_GROUP), f32, math.log(1e-3), math.log(1e-1)))
    gdn_dt_bias = dt + jnp.log(-jnp.expm1(-dt))
    mlstm_b_i = nrm(14, (DEPTH, H_GROUP), 0.1)
    mlstm_b_f = jnp.linspace(3.0, 6.0, H_GROUP, dtype=f32)[None, :] + nrm(15, (DEPTH, H_GROUP), 0.1)
    hgrn_lb_logits = nrm(16, (DEPTH, G_WIDTH), 0.5)
    out_norm_g = 1.0 + nrm(17, (DEPTH, D_MODEL), 0.02)
    w_out = nrm(18, (DEPTH, D_MODEL, D_MODEL), DN_BETA * D_MODEL ** -0.5)
    ln1_g = 1.0 + nrm(19, (DEPTH, D_MODEL), 0.02)
    ln1_b = nrm(20, (DEPTH, D_MODEL), 0.02)
    w1 = nrm(21, (DEPTH, D_MODEL, D_FF), D_MODEL ** -0.5)
    b1 = nrm(22, (DEPTH, D_FF), 0.02)
    w2 = nrm(23, (DEPTH, D_FF, D_MODEL), DN_BETA * D_FF ** -0.5)
    b2 = nrm(24, (DEPTH, D_MODEL), 0.02)
    ln2_g = 1.0 + nrm(25, (DEPTH, D_MODEL), 0.02)
    ln2_b = nrm(26, (DEPTH, D_MODEL), 0.02)
    return {"x_prompt": x_prompt, "x_sample": x_sample,
            "cache_sb_k": cache_sb_k, "cache_sb_v": cache_sb_v,
            "state_gdn_conv": state_gdn_conv, "state_gdn_S": state_gdn_S,
            "state_mlstm_C": state_mlstm_C, "state_mlstm_n": state_mlstm_n, "state_mlstm_m": state_mlstm_m,
            "state_hgrn_S": state_hgrn_S,
            "w_in": w_in, "gdn_conv_w": gdn_conv_w, "gdn_A_log": gdn_A_log, "gdn_dt_bias": gdn_dt_bias,
            "mlstm_b_i": mlstm_b_i, "mlstm_b_f": mlstm_b_f, "hgrn_lb_logits": hgrn_lb_logits,
            "out_norm_g": out_norm_g, "w_out": w_out, "ln1_g": ln1_g, "ln1_b": ln1_b,
            "w1": w1, "b1": b1, "w2": w2, "b2": b2, "ln2_g": ln2_g, "ln2_b": ln2_b}


def reference(x_prompt, x_sample, cache_sb_k, cache_sb_v, state_gdn_conv, state_gdn_S,
              state_mlstm_C, state_mlstm_n, state_mlstm_m, state_hgrn_S,
              w_in, gdn_conv_w, gdn_A_log, gdn_dt_bias, mlstm_b_i, mlstm_b_f, hgrn_lb_logits,
              out_norm_g, w_out, ln1_g, ln1_b, w1, b1, w2, b2, ln2_g, ln2_b):
    f32 = jnp.float32
    p = jax.nn.softmax(hgrn_lb_logits.astype(f32), axis=0)
    lower_bounds = jnp.cumsum(p, axis=0) - p[0]
    bp = x_prompt.shape[0]
    hd = (H_GROUP, HEAD_DIM)
    zero_state = (jnp.zeros((bp, 0) + hd, f32), jnp.zeros((bp, 0) + hd, f32),
                  jnp.zeros((bp, CONV_W - 1, 3 * G_WIDTH), f32), jnp.zeros((bp,) + hd + (HEAD_DIM,), f32),
                  jnp.zeros((bp,) + hd + (HEAD_DIM,), f32), jnp.zeros((bp,) + hd, f32),
                  jnp.zeros((bp, H_GROUP), f32), jnp.zeros((bp,) + hd + (HEAD_DIM,), f32))
    y_p, y_s = x_prompt, x_sample
    out_p, out_s = [], []
    for l in range(DEPTH):
        weights = (w_in[l], gdn_conv_w[l], gdn_A_log[l], gdn_dt_bias[l], mlstm_b_i[l], mlstm_b_f[l],
                   lower_bounds[l], out_norm_g[l], w_out[l], ln1_g[l], ln1_b[l], w1[l], b1[l], w2[l], b2[l],
                   ln2_g[l], ln2_b[l])
        past = (cache_sb_k[l], cache_sb_v[l], state_gdn_conv[l], state_gdn_S[l], state_mlstm_C[l],
                state_mlstm_n[l], state_mlstm_m[l], state_hgrn_S[l])
        y_p, st_p = _layer(y_p, zero_state, weights)
        y_s, st_s = _layer(y_s, past, weights)
        out_p.append(st_p)
        out_s.append(st_s)
    (p_k, p_v, p_conv, p_gdn, p_c, p_n, p_m, p_hg) = [jnp.stack(a) for a in zip(*out_p)]
    (s_k, s_v, s_conv, s_gdn, s_c, s_n, s_m, s_hg) = [jnp.stack(a) for a in zip(*out_s)]
    return (y_p, y_s, p_k, p_v, p_conv, p_gdn, p_c, p_n, p_m, p_hg,
            s_k, s_v, s_conv, s_gdn, s_c, s_n, s_m, s_hg)
```

```python
import contextlib
import math
import numpy as np
import concourse.bass as bass
import concourse.mybir as mybir
from concourse.bass_utils import run_bass_kernel_spmd

F32 = mybir.dt.float32
BF16 = mybir.dt.bfloat16
ALU = mybir.AluOpType
AF = mybir.ActivationFunctionType
AX = mybir.AxisListType

D_MODEL = 1024
DEPTH = 4
PAST = 2048
DN_ALPHA = (2 * DEPTH) ** 0.25
OFF = dict(gq=0, gk=256, gv=512, ggate=768, gbeta=1024, ga=1028, sq=1032, sk=1288, sv=1544,
           mq=1800, mk=2056, mv=2312, mo=2568, mi=2824, mf=2828, hq=2832, hf=3088, hi=3344, hg=3600)


def _cols(*specs):
    out = []
    for name, n in specs:
        out.extend(range(OFF[name], OFF[name] + n))
    return np.array(out)


FA_COLS = _cols(("gq", 256), ("gk", 256), ("gv", 256), ("sq", 256), ("sk", 256))
TA_COLS = _cols(("sk", 256), ("sv", 256), ("ggate", 256), ("gbeta", 4), ("ga", 4))
FB_COLS = _cols(("mq", 256), ("mk", 256), ("hq", 256), ("hf", 256))
RB_COLS = _cols(("mi", 4), ("mf", 4))
TB_COLS = _cols(("mk", 256), ("mv", 256), ("hi", 256), ("mo", 256), ("hg", 256))

ENGS = ("pe", "act", "dve", "pool", "sp")
EPOCH = 30000


class Op:
    __slots__ = ("eng", "fn", "deps", "marked", "chan", "val", "n")

    def __init__(self, eng, fn, chan=None):
        self.eng = eng
        self.fn = fn
        self.deps = ()
        self.marked = False
        self.chan = chan
        self.val = 0
        self.n = 0


class Sched:
    def __init__(self, nc, same_sync=True):
        self.nc = nc
        self.same_sync = same_sync
        self.ops = {e: [] for e in ENGS}
        self.last_w = {}
        self.readers = {}
        self.chan_cnt = {}
        self.chan_last = {}
        self.final = []
        self.fence = None
        self.fenced = set()

    def add(self, eng, fn, r=(), w=(), chan=None):
        r2 = []
        w2 = []
        for k in r:
            if k.startswith("pb"):
                if k[:3] not in w2:
                    w2.append(k[:3])
            else:
                r2.append(k)
        for k in w:
            k = k[:3] if k.startswith("pb") else k
            if k not in w2:
                w2.append(k)
        r, w = r2, w2
        op = Op(eng, fn, chan)
        deps = set()
        for k in r:
            lw = self.last_w.get(k)
            if lw is not None:
                deps.add(lw)
        for k in w:
            lw = self.last_w.get(k)
            if lw is not None:
                deps.add(lw)
            for rd in self.readers.get(k, ()):
                deps.add(rd)
        if self.fence is not None and eng not in self.fenced:
            deps.add(self.fence)
            self.fenced.add(eng)
        out = []
        for d in deps:
            if d.eng == eng and d.chan is None and chan is None:
                if eng == "pe" or not self.same_sync:
                    continue
            out.append(d)
            d.marked = True
        op.deps = out
        for k in r:
            self.readers.setdefault(k, []).append(op)
        for k in w:
            self.last_w[k] = op
            self.readers[k] = []
        if chan is not None:
            c = self.chan_cnt.get(chan, 0) + 1
            self.chan_cnt[chan] = c
            op.val = 16 * c
            op.marked = True
            self.chan_last[chan] = op
        self.ops[eng].append(op)
        return op

    def dma(self, eng, out, in_, r=(), w=(), chan=None, final=False, nc_ok=False):
        assert chan is not None
        if nc_ok:
            fn = lambda e: e.dma_start(out=out, in_=in_, allow_slow_non_contiguous=True)
        else:
            fn = lambda e: e.dma_start(out=out, in_=in_)
        op = self.add(eng, fn, r=r, w=w, chan=chan)
        if final:
            self.final.append(op)
        return op

    def barrier(self, dummy):
        lasts = [ops[-1] for ops in self.ops.values() if ops]
        lasts += list(self.chan_last.values())
        op = Op("dve", lambda e: e.memset(dummy, 0.0))
        deps = []
        for d in set(lasts):
            if d.eng == "dve" and d.chan is None:
                continue
            deps.append(d)
            d.marked = True
        op.deps = deps
        op.marked = True
        self.ops["dve"].append(op)
        self.fence = op
        self.fenced = {"dve"}
        self.last_w = {}
        self.readers = {}

    def emit(self):
        nc = self.nc
        with contextlib.ExitStack() as st:
            esems = {}
            for e in ENGS:
                n = 0
                for op in self.ops[e]:
                    if op.marked and op.chan is None:
                        n += 1
                        op.n = n
                nep = (n + EPOCH - 1) // EPOCH
                esems[e] = [st.enter_context(nc.semaphore(f"s_{e}{i}")) for i in range(nep)]
            csems = {c: st.enter_context(nc.semaphore(f"c_{c}")) for c in self.chan_cnt}

            def semval(d):
                if d.chan is not None:
                    return csems[d.chan], d.val
                ep = (d.n - 1) // EPOCH
                return esems[d.eng][ep], d.n - ep * EPOCH

            block = st.enter_context(nc.Block())
            finals = [semval(o) for o in self.final]

            def run(e, eng):
                waited = {}
                for op in self.ops[e]:
                    for d in op.deps:
                        s, v = semval(d)
                        key = id(s)
                        if waited.get(key, 0) >= v:
                            continue
                        waited[key] = v
                        eng.wait_ge(s, v)
                    ins = op.fn(eng)
                    if op.marked:
                        if op.chan is not None:
                            ins.then_inc(csems[op.chan], 16)
                        else:
                            s, _ = semval(op)
                            ins.then_inc(s, 1)
                if e == "sp":
                    best = {}
                    for s, v in finals:
                        if best.get(id(s), (None, 0))[1] < v:
                            best[id(s)] = (s, v)
                    for s, v in best.values():
                        eng.wait_ge(s, v)

            @block.tensor
            def _(eng):
                run("pe", eng)

            @block.scalar
            def _(eng):
                run("act", eng)

            @block.vector
            def _(eng):
                run("dve", eng)

            @block.gpsimd
            def _(eng):
                run("pool", eng)

            @block.sync
            def _(eng):
                run("sp", eng)


class StopBuild(Exception):
    pass


def stop_at(tag):
    import os
    if os.environ.get("KSTOP", "") == tag:
        raise StopBuild(tag)


class Arena:
    def __init__(self, ap, ncols):
        self.ap = ap
        self.n = ncols
        self.off = 0

    def reset(self):
        self.off = 0

    def alloc(self, shape, dt=F32):
        P = shape[0]
        free = int(np.prod(shape[1:]))
        nb = free * (4 if dt == F32 else 2)
        ncol = (nb + 3) // 4
        a = self.ap[0:P, self.off:self.off + ncol]
        self.off += (ncol + 15) // 16 * 16
        assert self.off <= self.n, f"arena overflow {self.off} > {self.n}"
        if dt != F32:
            a = a.bitcast(dt)
            if a.shape[1] != free:
                a = a[:, 0:free]
        if len(shape) == 3:
            a = a.rearrange("p (a b) -> p a b", a=shape[1])
        elif len(shape) == 4:
            a = a.rearrange("p (a b c) -> p a b c", a=shape[1], b=shape[2])
        return a


def make_consts():
    p = np.arange(64)[:, None]
    f = np.arange(64)[None, :]
    c64 = np.zeros((64, 8, 64), np.float32)
    c64[:, 0] = (p <= f)
    c64[:, 1] = -(p <= f).astype(np.float32)
    c64[:, 2] = -(p < f).astype(np.float32)
    c64[:, 3] = -(f < p).astype(np.float32)
    c64[:, 4] = np.eye(64)
    c64[:, 5] = 1.0
    P = np.arange(128)[:, None]
    Fq = np.arange(128)[None, :]
    c128 = np.zeros((128, 4, 128), np.float32)
    c128[:, 0] = np.eye(128)
    c128[:, 1] = (P < Fq)
    c128[:, 2] = (P >= Fq)
    c128[:, 3] = (P // 64 == Fq // 64)
    rmask = np.ones((128, 512), np.float32)
    rmask[:, ::64] = 0.0
    i4 = np.zeros((4, 4), np.float32)
    i4[:] = np.eye(4)
    return dict(c64=c64, c128=c128, rmask=rmask, i4=i4)


def build(NPS, TP, L, SAMPLE, TS_=64):
    nc = bass.Bass("TRN2", target_bir_lowering=False)
    dins = {}
    douts = {}

    def din(name, shape):
        dins[name] = nc.dram_tensor(name, list(shape), F32, kind="ExternalInput").ap()
        return dins[name]

    def dout(name, shape):
        douts[name] = nc.dram_tensor(name, list(shape), F32, kind="ExternalOutput").ap()
        return douts[name]

    xp = din("xp", (NPS, TP, 1024))
    wfa = din("wfa", (L, 128, 8, 1280))
    wta = din("wta", (L, 128, 8, 776))
    wfb = din("wfb", (L, 128, 8, 1024))
    wrb = din("wrb", (L, 128, 8, 8))
    wtb = din("wtb", (L, 128, 8, 1280))
    wdn = din("wdn", (L, 18, 128, 4096))
    convw = din("convw", (L, 128, 6, 4))
    b1c = din("b1c", (L, 128, 32))
    vecs = din("vecs", (L, 128, 5, 8))
    outg = din("outg", (L, 1024))
    hvec = din("hvec", (L, 4, 4))
    lbl = din("lbl", (L, 256))
    c64d = din("c64", (64, 8, 64))
    c128d = din("c128", (128, 4, 128))
    rmaskd = din("rmask", (128, 512))
    i4d = din("i4", (4, 4))
    if SAMPLE:
        xs_in = din("xs", (TS_, 1024))
        ck = din("ck", (L, PAST, 256))
        cv = din("cv", (L, PAST, 256))
        sconv = din("sconv", (L, 3, 768))
        gS_in = din("gS", (L, 4, 64, 64))
        mC_in = din("mC", (L, 4, 64, 64))
        mn_in = din("mn", (L, 4, 64))
        mm_in = din("mm", (L, 4))
        hS_in = din("hS", (L, 4, 64, 64))
    y_p = dout("y_p", (NPS, TP, 1024))
    p_k = dout("p_k", (L, NPS, TP, 256))
    p_v = dout("p_v", (L, NPS, TP, 256))
    p_conv = dout("p_conv", (L, NPS, 3, 768))
    p_gdn = dout("p_gdn", (L, NPS, 4, 64, 64))
    p_c = dout("p_c", (L, NPS, 4, 64, 64))
    p_n = dout("p_n", (L, NPS, 4, 64))
    p_m = dout("p_m", (L, NPS, 4))
    p_hg = dout("p_hg", (L, NPS, 4, 64, 64))
    if SAMPLE:
        y_s = dout("y_s", (TS_, 1024))
        s_k = dout("s_k", (L, 1, TS_, 256))
        s_v = dout("s_v", (L, 1, TS_, 256))
        s_conv = dout("s_conv", (L, 1, 3, 768))
        s_gdn = dout("s_gdn", (L, 1, 4, 64, 64))
        s_c = dout("s_c", (L, 1, 4, 64, 64))
        s_n = dout("s_n", (L, 1, 4, 64))
        s_m = dout("s_m", (L, 1, 4))
        s_hg = dout("s_hg", (L, 1, 4, 64, 64))
    TMAX = max(TP, TS_)
    xscr = nc.dram_tensor("xscr", [2, 128, 8, TMAX], F32, kind="Internal").ap()
    import os
    KDEBUG = os.environ.get("KDEBUG", "") == "1"
    if KDEBUG:
        dbg_oT = nc.dram_tensor("dbg_oT", [128, 8, TMAX], BF16, kind="ExternalOutput").ap()

    seqs = []
    for i in range(NPS):
        seqs.append(dict(T=TP, past=0, x=xp[i], y=y_p[i], k=lambda l, i=i: p_k[l, i], v=lambda l, i=i: p_v[l, i],
                         conv=lambda l, i=i: p_conv[l, i], gdn=lambda l, i=i: p_gdn[l, i], c=lambda l, i=i: p_c[l, i],
                         n=lambda l, i=i: p_n[l, i], m=lambda l, i=i: p_m[l, i], hg=lambda l, i=i: p_hg[l, i], sample=False))
    if SAMPLE:
        seqs.append(dict(T=TS_, past=PAST, x=xs_in, y=y_s, k=lambda l: s_k[l, 0], v=lambda l: s_v[l, 0],
                         conv=lambda l: s_conv[l, 0], gdn=lambda l: s_gdn[l, 0], c=lambda l: s_c[l, 0],
                         n=lambda l: s_n[l, 0], m=lambda l: s_m[l, 0], hg=lambda l: s_hg[l, 0], sample=True))

    AW = 36400
    NKMAX = PAST + TS_ if SAMPLE else TP
    NKMAX = max(NKMAX, TP)
    NKT = (NKMAX + 127) // 128

    with contextlib.ExitStack() as st:
        def sb(name, shape, dt=F32):
            return st.enter_context(nc.sbuf_tensor(name, list(shape), dt))

        S = Sched(nc)
        arena_t = sb("arena", (128, AW))
        AR = Arena(arena_t, AW)
        dummy = sb("fence_dummy", (1, 8))
        oT = sb("oT", (128, 8, TMAX), BF16)
        c64 = sb("c64t", (64, 8, 64))
        c64b = sb("c64b", (64, 8, 64), BF16)
        c128 = sb("c128t", (128, 4, 128))
        c128b = sb("c128b", (128, 4, 128), BF16)
        rmask = sb("rmaskt", (128, 512))
        i4 = sb("i4t", (4, 4))
        cst = sb("cst", (128, 8))
        odiv = sb("odiv", (128, 128), BF16)
        convw_t = sb("convw_t", (128, L, 24))
        b1_t = sb("b1_t", (128, L, 32))
        vec_t = sb("vec_t", (128, L, 40))
        outg_t = sb("outg_t", (64, L, 1024))
        hrow_t = sb("hrow_t", (4, L, 4))
        hbc_t = sb("hbc_t", (64, L, 16))
        negA_t = sb("negA_t", (64, L, 4))
        nbf_t = sb("nbf_t", (4, L, 1))
        lb_t = sb("lb_t", (64, 4, L))
        oml_t = sb("oml_t", (64, 4, L))
        noml_t = sb("noml_t", (64, 4, L))
        lbtmp = sb("lbtmp", (64, 4, L))
        lbsum = sb("lbsum", (64, 4))
        pbank = [st.enter_context(nc.psum_tensor(f"pb{i}", [128, 512], F32)) for i in range(8)]

        def PB(i, p0, p1, c0, c1):
            return pbank[i][p0:p1, c0:c1]

        def PB16(i, p0, p1, c0, c1):
            return pbank[i][p0:p1, c0:c1].bitcast(BF16)

        def ACT(out, in_, func, r, w, bias=None, scale=None):
            kw = {}
            if bias is not None:
                kw["bias"] = bias
            if scale is not None:
                kw["scale"] = scale
            S.add("act", lambda e: e.activation(out=out, in_=in_, func=func, **kw), r=r, w=w)

        def TT(eng, out, in0, in1, op, r, w):
            S.add(eng, lambda e: e.tensor_tensor(out=out, in0=in0, in1=in1, op=op), r=r, w=w)

        def TS(eng, out, in0, s1, op0, r, w, s2=None, op1=None):
            if op1 is None:
                S.add(eng, lambda e: e.tensor_scalar(out=out, in0=in0, scalar1=s1, scalar2=None, op0=op0), r=r, w=w)
            else:
                S.add(eng, lambda e: e.tensor_scalar(out=out, in0=in0, scalar1=s1, scalar2=s2, op0=op0, op1=op1), r=r, w=w)

        def STT(out, in0, scalar, in1, op0, op1, r, w):
            S.add("dve", lambda e: e.scalar_tensor_tensor(out=out, in0=in0, scalar=scalar, in1=in1, op0=op0, op1=op1), r=r, w=w)

        def CP(eng, out, in_, r, w):
            if eng == "act":
                S.add("act", lambda e: e.activation(out=out, in_=in_, func=AF.Copy), r=r, w=w)
            else:
                S.add(eng, lambda e: e.tensor_copy(out=out, in_=in_), r=r, w=w)

        def MM(out, lhsT, rhs, r, w, start=True, stop=True):
            S.add("pe", lambda e: e.matmul(out, lhsT=lhsT, rhs=rhs, start=start, stop=stop), r=r, w=w)

        def TR(out, in_, ident, r, w):
            S.add("pe", lambda e: e.transpose(out, in_, ident), r=r, w=w)

        def MSET(eng, ap, val, w):
            S.add(eng, lambda e: e.memset(ap, val), w=w)

        def bc_mid(ap2, n):
            return ap2.unsqueeze(1).to_broadcast([ap2.shape[0], n, ap2.shape[1]])

        def bc_last(ap2, n):
            return ap2.unsqueeze(2).to_broadcast([ap2.shape[0], ap2.shape[1], n])

        ONE = cst[:, 0:1]
        EPS6 = cst[:, 1:2]
        EPS5 = cst[:, 2:3]

        S.dma("sp", c64[:], c64d, w=["c64"], chan="ld0")
        S.dma("sp", c128[:], c128d, w=["c128"], chan="ld1")
        S.dma("sp", rmask[:], rmaskd, w=["rmask"], chan="ld2")
        S.dma("sp", i4[:], i4d, w=["i4"], chan="ld3")
        S.dma("sp", convw_t[:].rearrange("p l (b j) -> p l b j", b=6), convw.rearrange("l p b j -> p l b j"), w=["convw"], chan="ld4")
        S.dma("sp", b1_t[:], b1c.rearrange("l p m -> p l m"), w=["b1"], chan="ld5")
        S.dma("sp", vec_t[:].rearrange("p l (a b) -> p l a b", a=5), vecs.rearrange("l p a b -> p l a b"), w=["vec"], chan="ld6")
        for l in range(L):
            S.dma("sp", outg_t[:, l, :], outg[l:l + 1, :].to_broadcast([64, 1024]), w=["outg"], chan="ld7")
            S.dma("sp", hbc_t[:, l, :], hvec[l:l + 1].rearrange("o a b -> o (a b)").to_broadcast([64, 16]), w=["hbc"], chan="ld8")
        S.dma("sp", hrow_t[:], hvec.rearrange("l a h -> h l a"), w=["hrow"], chan="ld9", nc_ok=True)
        for l in range(L):
            S.dma("sp", lb_t[:, :, l], lbl[l].rearrange("(h d) -> d h", h=4), w=["lb"], chan="ld10", nc_ok=True)
        MSET("pool", cst[:, 0:1], 1.0, ["cst"])
        MSET("pool", cst[:, 1:2], 1e-6, ["cst"])
        MSET("pool", cst[:, 2:3], 1e-5, ["cst"])
        MSET("pool", odiv[:], 1.0 / 1024.0, ["odiv"])
        CP("dve", c64b[:], c64[:], ["c64"], ["c64b"])
        CP("dve", c128b[:], c128[:], ["c128"], ["c128b"])
        ACT(negA_t[:], hbc_t[:, :, 0:4], AF.Exp, ["hbc"], ["negA"])
        TS("dve", negA_t[:], negA_t[:], -1.0, ALU.mult, ["negA"], ["negA"])
        TS("dve", nbf_t[:], hrow_t[:, :, 3:4], -1.0, ALU.mult, ["hrow"], ["nbf"])
        ACT(lbtmp[:], lb_t[:], AF.Exp, ["lb"], ["lbtmp"])
        S.add("dve", lambda e: e.tensor_reduce(out=lbsum[:], in_=lbtmp[:], axis=AX.X, op=ALU.add), r=["lbtmp"], w=["lbsum"])
        S.add("dve", lambda e: e.reciprocal(out=lbsum[:], in_=lbsum[:]), r=["lbsum"], w=["lbsum"])
        TT("dve", lbtmp[:], lbtmp[:], bc_last(lbsum[:], L), ALU.mult, ["lbtmp", "lbsum"], ["lbtmp"])
        MSET("pool", lb_t[:, :, 0:1], 0.0, ["lb"])
        for l in range(1, L):
            if l == 1:
                CP("dve", lb_t[:, :, 1:2], lbtmp[:, :, 1:2], ["lbtmp", "lb"], ["lb"])
            else:
                TT("dve", lb_t[:, :, l:l + 1], lb_t[:, :, l - 1:l], lbtmp[:, :, l:l + 1], ALU.add, ["lbtmp", "lb"], ["lb"])
        TS("dve", oml_t[:], lb_t[:], -1.0, ALU.mult, ["lb"], ["oml"], s2=1.0, op1=ALU.add)
        TS("dve", noml_t[:], oml_t[:], -1.0, ALU.mult, ["oml"], ["noml"])

        C_TRIU = c64[:, 0, :]
        C_NTRIU = c64[:, 1, :]
        C_ONES64 = c64[:, 5, :]
        C_ID64 = c64[:, 4, :]
        ID128 = c128[:, 0, :]
        ID128B = c128b[:, 0, :]
        AMASK = c128[:, 1, :]
        TRIINC_B = c128b[:, 2, :]
        BLKONES_B = c128b[:, 3, :]
        ones128b = sb("ones128b", (128, 128), BF16)
        MSET("pool", ones128b[:], 1.0, ["ones128b"])

        def phase0(sq):
            T = sq["T"]
            AR.reset()
            S.barrier(dummy[:])
            xtm = [AR.alloc((128, 1024)) for _ in range(2)]
            xfm = [AR.alloc((128, 8, 128)) for _ in range(2)]
            nt = (T + 127) // 128
            for ti in range(nt):
                n = min(128, T - ti * 128)
                a = ti % 2
                S.dma("sp", xtm[a][0:n, :], sq["x"][ti * 128:ti * 128 + n, :], w=[f"xtm{a}"], chan=f"p0l{a}")
                for b in range(8):
                    bank = (2 * a) + b // 4
                    TR(PB(bank, 0, 128, (b % 4) * 128, (b % 4) * 128 + n), xtm[a][0:n, b * 128:(b + 1) * 128], ID128[0:n, 0:n],
                       [f"xtm{a}", "c128"], [f"pb{bank}"])
                for hb in range(2):
                    bank = 2 * a + hb
                    src = pbank[bank][:].rearrange("p (b t) -> p b t", b=4)[:, :, 0:n]
                    CP("act" if hb == 0 else "dve", xfm[a][:, 4 * hb:4 * hb + 4, 0:n], src, [f"pb{bank}"], [f"xfm{a}"])
                S.dma("sp", xscr[0, :, :, ti * 128:ti * 128 + n], xfm[a][:, :, 0:n], r=[f"xfm{a}"], w=["xscr0"], chan=f"p0s{a}")

        def load_w(dst3, src3, key, chan, nk=8):
            for k in range(nk):
                S.dma("pool", dst3[:, k, :], src3[:, k, :], w=[key], chan=chan)

        def passA(sq, l):
            T = sq["T"]
            past = sq["past"]
            ntok = min(128, T)
            nch = ntok // 64
            ntile = T // ntok
            AR.reset()
            S.barrier(dummy[:])
            Wf = AR.alloc((128, 8, 1280), BF16)
            Wt = AR.alloc((128, 8, 776), BF16)
            NK = past + T
            kTs = AR.alloc((64, 4, NK), BF16)
            nkt = (NK + 127) // 128
            Vs = AR.alloc((128, nkt, 256), BF16)
            xb = AR.alloc((128, 8, ntok), BF16)
            ext = AR.alloc((128, 6, ntok + 3))
            yv = AR.alloc((128, 6, ntok))
            ys = AR.alloc((128, 6, ntok))
            vb16 = AR.alloc((128, 2, ntok), BF16)
            sqb = AR.alloc((128, 4, ntok), BF16)
            rn = AR.alloc((128, 4, ntok))
            qT = AR.alloc((64, 4, ntok), BF16)
            kT = AR.alloc((64, 4, ntok), BF16)
            qsb = AR.alloc((64, 4, ntok), BF16)
            kvout = AR.alloc((128, 512))
            gg = AR.alloc((64, nch, 256))
            sc8 = AR.alloc((64, nch, 8))
            bet = AR.alloc((64, nch, 4))
            tg = AR.alloc((64, nch, 4))
            gvec = AR.alloc((64, nch, 4))
            kvtm = AR.alloc((64, nch, 512), BF16)
            arg = AR.alloc((64, 12))
            ex = AR.alloc((64, 12))
            bg = AR.alloc((64, 4))
            vbt = AR.alloc((64, 4, 64), BF16)
            kbg = AR.alloc((64, 4, 64), BF16)
            kdec = AR.alloc((64, 4, 64), BF16)
            kb = AR.alloc((64, 4, 64), BF16)
            kbT = AR.alloc((64, 4, 64), BF16)
            absD = AR.alloc((64, 256))
            G = AR.alloc((64, 4, 64))
            eB = AR.alloc((64, 4, 64))
            qg = AR.alloc((64, 4, 64), BF16)
            Gu = AR.alloc((64, 4, 64))
            nGs = AR.alloc((64, 4, 64))
            nGl = AR.alloc((64, 4, 64))
            PP = AR.alloc((64, 2, 4, 64))
            ATb = AR.alloc((64, 4, 64), BF16)
            TTm = AR.alloc((64, 4, 64))
            TTb = AR.alloc((64, 4, 64), BF16)
            Uf = AR.alloc((64, 4, 64))
            WTb = AR.alloc((64, 4, 64), BF16)
            vnew = AR.alloc((64, 4, 64), BF16)
            Sg = AR.alloc((64, 4, 64))
            Sgb = AR.alloc((64, 4, 64), BF16)
            O = AR.alloc((64, nch, 8, 64))
            Osq = AR.alloc((64, nch, 8, 64))
            ssq = AR.alloc((64, nch * 8))
            sgate = AR.alloc((64, nch, 256))
            Ob = AR.alloc((64, nch, 512), BF16)
            zs = AR.alloc((128, 4, ntok))
            et = AR.alloc((128, 4, ntok))
            spf = AR.alloc((128, 4, ntok))
            spb = AR.alloc((128, 4, ntok), BF16)
            Rf = AR.alloc((128, 4, ntok))
            Rb = AR.alloc((128, 4, ntok), BF16)
            tf = AR.alloc((128, 4, ntok))
            af = AR.alloc((128, 4, ntok))
            ab = AR.alloc((128, 4, ntok), BF16)
            if sq["sample"]:
                ktm = [AR.alloc((128, 256)) for _ in range(2)]

            load_w(Wf, wfa[l], "Wf", "wA0")
            load_w(Wt, wta[l], "Wt", "wA1")
            cw = convw_t[:, l, :].rearrange("p (b j) -> p b j", b=6)
            if sq["sample"]:
                S.dma("sp", Sg[:], gS_in[l].rearrange("h k v -> k h v"), w=["Sg"], chan="stA")
                for b in range(6):
                    S.dma("sp", ext[:, b, 0:3], sconv[l, :, b * 128:(b + 1) * 128].rearrange("r p -> p r"), w=["ext"], chan="stB", nc_ok=True)
                S.dma("pool", Vs[:, 0:PAST // 128, :], cv[l].rearrange("(t p) c -> p t c", p=128), w=["Vs"], chan="stC")
                for kt in range(PAST // 128):
                    a = kt % 2
                    S.dma("sp", ktm[a][:], ck[l, kt * 128:(kt + 1) * 128, :], w=[f"ktm{a}"], chan=f"stK{a}")
                    for h in range(4):
                        TR(PB(2 + a, 0, 64, h * 128, (h + 1) * 128), ktm[a][:, h * 64:(h + 1) * 64], ID128, [f"ktm{a}", "c128"], [f"pb{2 + a}"])
                    CP("act" if a == 0 else "dve", kTs[:, :, kt * 128:(kt + 1) * 128],
                       pbank[2 + a][0:64, :].rearrange("p (h t) -> p h t", h=4), [f"pb{2 + a}"], ["kTs"])
            else:
                MSET("pool", Sg[:], 0.0, ["Sg"])
                MSET("pool", ext[:, :, 0:3], 0.0, ["ext"])
            CP("dve", Sgb[:], Sg[:], ["Sg"], ["Sgb"])

            stop_at("A0a")
            for ti in range(ntile):
                pos0 = ti * ntok
                S.dma("pool", xb[:], xscr[l % 2, :, :, pos0:pos0 + ntok], r=[f"xscr{l % 2}"], w=["xb"], chan="xbA")
                stop_at("A0b")
                nblk_bank = 512 // ntok
                groups = [(0, 4), (4, 8), (8, 10)]
                for gi, (b0, b1) in enumerate(groups):
                    bank = gi % 2
                    for b in range(b0, b1):
                        for k in range(8):
                            MM(PB(bank, 0, 128, (b - b0) * ntok, (b - b0 + 1) * ntok), Wf[:, k, b * 128:(b + 1) * 128], xb[:, k, :],
                               ["Wf", "xb"], [f"pb{bank}"], start=(k == 0), stop=(k == 7))
                    pv = pbank[bank][:, 0:(b1 - b0) * ntok].rearrange("p (b t) -> p b t", t=ntok)
                    import os
                    if os.environ.get("KDBG", "") == "noevac":
                        continue
                    if os.environ.get("KDBG", "") == f"only{gi}":
                        stop_at("A0c")
                    if gi == 0:
                        CP("act", ext[:, 0:4, 3:3 + ntok], pv, [f"pb{bank}"], ["ext"])
                    elif gi == 1:
                        dbg = os.environ.get("KDBG2", "abc")
                        if "a" in dbg:
                            CP("act", ext[:, 4:6, 3:3 + ntok], pv[:, 0:2, :], [f"pb{bank}"], ["ext"])
                        q4 = qsb[:].rearrange("p (b h) t -> p b h t", b=2)
                        if "b" in dbg:
                            TS("dve", q4[:, :, 0, :], pv[0:64, 2:4, :], 0.125, ALU.mult, [f"pb{bank}"], ["qsb"])
                        if "c" in dbg:
                            TS("dve", q4[:, :, 1, :], pv[64:128, 2:4, :], 0.125, ALU.mult, [f"pb{bank}"], ["qsb"])
                    else:
                        k4 = kTs[:, :, past + pos0:past + pos0 + ntok].rearrange("p (b h) t -> p b h t", b=2)
                        CP("act", k4[:, :, 0, :], pv[0:64, 0:2, :], [f"pb{bank}"], ["kTs"])
                        CP("dve", k4[:, :, 1, :], pv[64:128, 0:2, :], [f"pb{bank}"], ["kTs"])
                stop_at("A0c")
                for k in range(8):
                    MM(PB(1, 0, ntok, 0, 512), xb[:, k, :], Wt[:, k, 0:512], ["Wt", "xb"], ["pb1"], start=(k == 0), stop=(k == 7))
                CP("act", kvout[0:ntok, :], PB(1, 0, ntok, 0, 512), ["pb1"], ["kvout"])
                kt_new = (past + pos0) // 128
                pr0 = (past + pos0) % 128
                CP("dve", Vs[pr0:pr0 + ntok, kt_new, :], PB(1, 0, ntok, 256, 512), ["pb1"], ["Vs"])
                S.dma("sp", sq["k"](l)[pos0:pos0 + ntok, :], kvout[0:ntok, 0:256], r=["kvout"], chan="okA", final=True)
                S.dma("sp", sq["v"](l)[pos0:pos0 + ntok, :], kvout[0:ntok, 256:512], r=["kvout"], chan="ovA", final=True)
                for k in range(8):
                    MM(PB(0, 0, ntok, 0, 264), xb[:, k, :], Wt[:, k, 512:776], ["Wt", "xb"], ["pb0"], start=(k == 0), stop=(k == 7))
                for c in range(nch):
                    CP("act", gg[:, c, :], PB(0, 64 * c, 64 * c + 64, 0, 256), ["pb0"], ["gg"])
                    CP("dve", sc8[:, c, :], PB(0, 64 * c, 64 * c + 64, 256, 264), ["pb0"], ["sc8"])
                stop_at("A1")
                for b in range(6):
                    TS("dve", yv[:, b, :], ext[:, b, 0:ntok], cw[:, b, 0:1], ALU.mult, ["ext", "convw"], [f"yv{b}"])
                    for j in range(1, 4):
                        STT(yv[:, b, :], ext[:, b, j:j + ntok], cw[:, b, j:j + 1], yv[:, b, :], ALU.mult, ALU.add,
                            ["ext", "convw", f"yv{b}"], [f"yv{b}"])
                if ti == ntile - 1:
                    for b in range(6):
                        S.dma("sp", sq["conv"](l)[:, b * 128:(b + 1) * 128].rearrange("r p -> p r"), ext[:, b, ntok:ntok + 3], r=["ext"],
                              chan="ocA", final=True, nc_ok=True)
                else:
                    CP("pool", ext[:, :, 0:3], ext[:, :, ntok:ntok + 3], ["ext"] + [f"yv{b}" for b in range(6)], ["ext"])
                ACT(ys[:], yv[:], AF.Silu, [f"yv{b}" for b in range(6)], ["ys"])
                CP("pool", vb16[:], ys[:, 4:6, :], ["ys"], ["vb16"])
                ACT(sqb[:], ys[:, 0:4, :], AF.Square, ["ys"], ["sqb"])
                for b in range(4):
                    MM(PB(2, 0, 128, b * ntok, (b + 1) * ntok), BLKONES_B, sqb[:, b, :], ["sqb", "c128b"], ["pb2"])
                ACT(rn[:], pbank[2][:, 0:4 * ntok].rearrange("p (b t) -> p b t", b=4), AF.Sqrt, ["pb2"], ["rn"], bias=EPS6, scale=1.0)
                S.add("dve", lambda e: e.reciprocal(out=rn[:], in_=rn[:]), r=["rn"], w=["rn"])
                q4 = qT[:].rearrange("p (b h) t -> p b h t", b=2)
                k4 = kT[:].rearrange("p (b h) t -> p b h t", b=2)
                for hf in range(2):
                    STT(q4[:, :, hf, :], ys[64 * hf:64 * hf + 64, 0:2, :], 0.125, rn[64 * hf:64 * hf + 64, 0:2, :], ALU.mult, ALU.mult,
                        ["ys", "rn"], ["qT"])
                    TT("dve", k4[:, :, hf, :], ys[64 * hf:64 * hf + 64, 2:4, :], rn[64 * hf:64 * hf + 64, 2:4, :], ALU.mult, ["ys", "rn"], ["kT"])
                ACT(bet[:], sc8[:, :, 0:4], AF.Sigmoid, ["sc8"], ["bet"])
                TT("dve", tg[:], sc8[:, :, 4:8], bc_mid(hbc_t[:, l, 4:8], nch), ALU.add, ["sc8", "hbc"], ["tg"])
                ACT(tg[:], tg[:], AF.Exp, ["tg"], ["tg"])
                ACT(tg[:], tg[:], AF.Ln, ["tg", "cst"], ["tg"], bias=ONE[0:64], scale=1.0)
                TT("dve", gvec[:], tg[:], bc_mid(negA_t[:, l, :], nch), ALU.mult, ["tg", "negA"], ["gvec"])
                for c in range(nch):
                    cs = slice(64 * c, 64 * c + 64)
                    pT = PB16(3, 0, 64, 0, 256)
                    for h in range(4):
                        TR(pT[:, h * 64:(h + 1) * 64], kT[:, h, cs], c64b[:, 4, :], ["kT", "c64b"], ["pb3a"])
                    for bv in range(2):
                        TR(pT[:, 256 + bv * 128:256 + (bv + 1) * 128], vb16[:, bv, cs], ID128B, ["vb16", "c128b"], ["pb3a"])
                    CP("act", kvtm[:, c, :], pT, ["pb3a"], [f"kvtm{c}"])
                stop_at("A2")
                for c in range(nch):
                    cs = slice(64 * c, 64 * c + 64)
                    ktm3 = kvtm[:, c, 0:256].rearrange("p (h d) -> p h d", h=4)
                    vtm3 = kvtm[:, c, 256:512].rearrange("p (h d) -> p h d", h=4)
                    MM(PB(2, 0, 64, 0, 4), C_TRIU, gvec[:, c, :], ["c64", "gvec"], ["pb2"])
                    MM(PB(2, 0, 64, 4, 8), C_ONES64, gvec[:, c, :], ["c64", "gvec"], ["pb2"])
                    CP("act", arg[:, 0:8], PB(2, 0, 64, 0, 8), ["pb2"], ["arg"])
                    TT("dve", arg[:, 8:12], arg[:, 4:8], arg[:, 0:4], ALU.subtract, ["arg"], ["arg"])
                    ACT(ex[:], arg[:], AF.Exp, ["arg"], ["ex"])
                    TT("dve", bg[:], bet[:, c, :], ex[:, 0:4], ALU.mult, ["bet", "ex"], ["bg"])
                    TT("pool", vbt[:], vtm3, bc_last(bet[:, c, :], 64), ALU.mult, [f"kvtm{c}", "bet"], ["vbt"])
                    TT("pool", kbg[:], ktm3, bc_last(bg[:], 64), ALU.mult, [f"kvtm{c}", "bg"], ["kbg"])
                    TT("pool", kdec[:], ktm3, bc_last(ex[:, 8:12], 64), ALU.mult, [f"kvtm{c}", "ex"], ["kdec"])
                    TT("pool", kb[:], ktm3, bc_last(bet[:, c, :], 64), ALU.mult, [f"kvtm{c}", "bet"], ["kb"])
                    pT2 = PB16(3, 0, 64, 256, 384)
                    for h in range(4):
                        TR(pT2[:, h * 64:(h + 1) * 64], kb[:, h, :], c64b[:, 4, :], ["kb", "c64b"], ["pb3b"])
                    CP("dve", kbT[:].rearrange("p h t -> p (h t)"), pT2, ["pb3b"], ["kbT"])
                    for h in range(4):
                        gbc = gvec[:, c, h:h + 1].to_broadcast([64, 64])
                        MM(PB(4, 0, 64, h * 64, (h + 1) * 64), gbc, C_TRIU, ["gvec", "c64"], ["pb4a"])
                        MM(PB(4, 0, 64, 256 + h * 64, 256 + (h + 1) * 64), gbc, C_TRIU, ["gvec", "c64"], ["pb4b"], start=True, stop=False)
                        MM(PB(4, 0, 64, 256 + h * 64, 256 + (h + 1) * 64), C_NTRIU, gbc, ["gvec", "c64"], ["pb4b"], start=False, stop=True)
                    ACT(absD[:], PB(4, 0, 64, 256, 512), AF.Abs, ["pb4b"], ["absD"])
                    ACT(G[:].rearrange("p h t -> p (h t)"), absD[:], AF.Exp, ["absD"], ["G"], scale=-1.0)
                    ACT(eB[:].rearrange("p h t -> p (h t)"), PB(4, 0, 64, 0, 256), AF.Exp, ["pb4a"], ["eB"])
                    TT("dve", qg[:], qT[:, :, cs], eB[:], ALU.mult, ["qT", "eB"], ["qg"])
                    TT("pool", Gu[:], G[:], bc_mid(C_TRIU, 4), ALU.mult, ["G", "c64"], ["Gu"])
                    TT("pool", nGs[:], G[:], bc_mid(c64[:, 2, :], 4), ALU.mult, ["G", "c64"], ["nGs"])
                    TT("pool", nGl[:], G[:], bc_mid(c64[:, 3, :], 4), ALU.mult, ["G", "c64"], ["nGl"])
                    for h in range(4):
                        MM(PB(5, 0, 64, h * 64, (h + 1) * 64), kT[:, h, cs], kbT[:, h, :], ["kT", "kbT"], ["pb5a"])
                    for h in range(4):
                        MM(PB(5, 0, 64, 256 + h * 64, 256 + (h + 1) * 64), kbT[:, h, :], kT[:, h, cs], ["kT", "kbT"], ["pb5b"])
                    for h in range(4):
                        MM(PB(6, 0, 64, h * 64, (h + 1) * 64), kT[:, h, cs], qT[:, h, cs], ["kT", "qT"], ["pb6a"])
                    TT("dve", PP[:, 1].rearrange("p h t -> p (h t)"), PB(5, 0, 64, 0, 256), nGs[:].rearrange("p h t -> p (h t)"), ALU.mult,
                       ["pb5a", "nGs"], ["PP1"])
                    TT("dve", PP[:, 0].rearrange("p h t -> p (h t)"), PB(5, 0, 64, 256, 512), nGl[:].rearrange("p h t -> p (h t)"), ALU.mult,
                       ["pb5b", "nGl"], ["PP0"])
                    TT("dve", ATb[:].rearrange("p h t -> p (h t)"), PB(6, 0, 64, 0, 256), Gu[:].rearrange("p h t -> p (h t)"), ALU.mult,
                       ["pb6a", "Gu"], ["ATb"])
                    TT("pool", TTm[:], PP[:, 1], bc_mid(C_ID64, 4), ALU.add, ["PP1", "c64"], ["TTm"])
                    for lev in range(1, 6):
                        for h in range(4):
                            MM(PB(7, 0, 64, h * 64, (h + 1) * 64), PP[:, 1, h, :], PP[:, 0, h, :], ["PP0", "PP1"], ["pb7"])
                        if lev < 5:
                            for h in range(4):
                                MM(PB(7, 0, 64, 256 + h * 64, 256 + (h + 1) * 64), PP[:, 0, h, :], PP[:, 1, h, :], ["PP0", "PP1"], ["pb7"])
                            CP("act", PP[:].rearrange("p a h t -> p (a h t)"), PB(7, 0, 64, 0, 512), ["pb7"], ["PP0", "PP1"])
                        else:
                            CP("act", PP[:, 0].rearrange("p h t -> p (h t)"), PB(7, 0, 64, 0, 256), ["pb7"], ["PP0"])
                        for h in range(4):
                            MM(PB(6, 0, 64, 256 + h * 64, 256 + (h + 1) * 64), PP[:, 0, h, :], TTm[:, h, :], ["PP0", "TTm"], ["pb6b"])
                        TT("dve", TTm[:].rearrange("p h t -> p (h t)"), TTm[:].rearrange("p h t -> p (h t)"), PB(6, 0, 64, 256, 512), ALU.add,
                           ["TTm", "pb6b"], ["TTm"])
                    CP("pool", TTb[:], TTm[:], ["TTm"], ["TTb"])
                    for h in range(4):
                        MM(PB(4, 0, 64, h * 64, (h + 1) * 64), TTb[:, h, :], vbt[:, h, :], ["TTb", "vbt"], ["pb4a"])
                    for h in range(4):
                        MM(PB(4, 0, 64, 256 + h * 64, 256 + (h + 1) * 64), kbg[:, h, :], TTb[:, h, :], ["TTb", "kbg"], ["pb4b"])
                    CP("act", Uf[:].rearrange("p h t -> p (h t)"), PB(4, 0, 64, 0, 256), ["pb4a"], ["Uf"])
                    CP("dve", WTb[:].rearrange("p h t -> p (h t)"), PB(4, 0, 64, 256, 512), ["pb4b"], ["WTb"])
                    for h in range(4):
                        MM(PB(5, 0, 64, h * 64, (h + 1) * 64), WTb[:, h, :], Sgb[:, h, :], ["WTb", "Sgb"], ["pb5a"])
                    TT("dve", vnew[:].rearrange("p h t -> p (h t)"), Uf[:].rearrange("p h t -> p (h t)"), PB(5, 0, 64, 0, 256), ALU.subtract,
                       ["Uf", "pb5a"], ["vnew"])
                    for h in range(4):
                        MM(PB(5, 0, 64, 256 + h * 64, 256 + (h + 1) * 64), ATb[:, h, :], vnew[:, h, :], ["ATb", "vnew"], ["pb5b"], start=True, stop=False)
                        MM(PB(5, 0, 64, 256 + h * 64, 256 + (h + 1) * 64), qg[:, h, :], Sgb[:, h, :], ["qg", "Sgb"], ["pb5b"], start=False, stop=True)
                    CP("act", O[:, c, 0:4, :].rearrange("p h t -> p (h t)"), PB(5, 0, 64, 256, 512), ["pb5b"], ["O"])
                    for h in range(4):
                        MM(PB(6, 0, 64, h * 64, (h + 1) * 64), kdec[:, h, :], vnew[:, h, :], ["kdec", "vnew"], ["pb6a"])
                    TT("pool", Sg[:], Sg[:], bc_last(ex[:, 4:8], 64), ALU.mult, ["Sg", "ex"], ["Sg"])
                    TT("dve", Sg[:].rearrange("p h t -> p (h t)"), Sg[:].rearrange("p h t -> p (h t)"), PB(6, 0, 64, 0, 256), ALU.add,
                       ["Sg", "pb6a"], ["Sg"])
                    CP("pool", Sgb[:], Sg[:], ["Sg"], ["Sgb"])
                stop_at("A3")
                kts = [(kt_new, ntok, True)] + [(kt, 128, False) for kt in range(kt_new - 1, -1, -1)]
                first = True
                W4 = 4 * ntok
                for idx, (kt, nk, diag) in enumerate(kts):
                    last = idx == len(kts) - 1
                    for h in range(4):
                        MM(PB(0, 0, nk, h * ntok, (h + 1) * ntok), kTs[:, h, kt * 128:kt * 128 + nk], qsb[:, h, :], ["kTs", "qsb"], ["pb0"])
                    pz = pbank[0][0:nk, 0:W4].rearrange("p (h t) -> p h t", h=4)
                    CP("dve", zs[0:nk], pz, ["pb0"], ["zs"])
                    ACT(et[0:nk], pz, AF.Exp, ["pb0"], ["et"])
                    ACT(spf[0:nk], et[0:nk], AF.Ln, ["et", "cst"], ["spf"], bias=ONE[0:nk], scale=1.0)
                    if diag:
                        TT("pool", spf[0:nk], spf[0:nk], bc_mid(AMASK[0:nk, 0:ntok], 4), ALU.mult, ["spf", "c128"], ["spf"])
                    CP("pool", spb[0:nk], spf[0:nk], ["spf"], ["spb"])
                    pa = pbank[1][0:nk, 0:W4]
                    MM(pa, TRIINC_B[0:nk, 0:nk], spb[0:nk].rearrange("p h t -> p (h t)"), ["spb", "c128b"], ["pb1"], start=True, stop=first)
                    if not first:
                        MM(pa, ones128b[:, 0:nk], Rb[:].rearrange("p h t -> p (h t)"), ["Rb", "ones128b"], ["pb1"], start=False, stop=True)
                    TT("dve", tf[0:nk].rearrange("p h t -> p (h t)"), zs[0:nk].rearrange("p h t -> p (h t)"), pa, ALU.subtract, ["zs", "pb1"], ["tf"])
                    if diag:
                        ACT(af[0:nk], tf[0:nk], AF.Exp, ["tf"], ["af"])
                        TT("pool", ab[0:nk], af[0:nk], bc_mid(AMASK[0:nk, 0:ntok], 4), ALU.mult, ["af", "c128"], ["ab"])
                    else:
                        ACT(ab[0:nk], tf[0:nk], AF.Exp, ["tf"], ["ab"])
                    if not last:
                        if first:
                            if nk < 128:
                                MSET("pool", Rf[:], 0.0, ["Rf"])
                            CP("pool", Rf[0:nk], spf[0:nk], ["spf", "Rf"], ["Rf"])
                        else:
                            TT("pool", Rf[0:nk], Rf[0:nk], spf[0:nk], ALU.add, ["spf", "Rf"], ["Rf"])
                        CP("pool", Rb[:], Rf[:], ["Rf"], ["Rb"])
                    for h in range(4):
                        S.add("pe", lambda e, h=h, kt=kt, nk=nk, st_=(first and h == 0), sp_=last: e.matmul(
                            PB(2, 0, ntok, h * 64, (h + 1) * 64), lhsT=ab[0:nk, h, :], rhs=Vs[0:nk, kt, h * 64:(h + 1) * 64],
                            start=st_, stop=sp_, skip_group_check=True), r=["ab", "Vs"], w=["pb2"])
                    first = False
                for c in range(nch):
                    CP("act", O[:, c, 4:8, :].rearrange("p h t -> p (h t)"), PB(2, 64 * c, 64 * c + 64, 0, 256), ["pb2"], ["O"])
                stop_at("A4")
                merge(O, Osq, ssq, Ob, nch, l, 0, pos0,
                      gate0=("silu", gg, sgate, "gg", 0), gate1=None)
            S.dma("sp", sq["gdn"](l).rearrange("h k v -> k h v"), Sg[:], r=["Sg"], chan="ogA", final=True)

        def merge(O, Osq, ssq, Ob, nch, l, blk0, pos0, gate0, gate1):
            Of = O[:].rearrange("p c h d -> p (c h d)")
            ACT(Osq[:].rearrange("p c h d -> p (c h d)"), Of, AF.Square, ["O"], ["Osq"])
            S.add("dve", lambda e: e.tensor_reduce(out=ssq[:], in_=Osq[:].rearrange("p c h d -> p (c h) d"), axis=AX.X, op=ALU.add),
                  r=["Osq"], w=["ssq"])
            TS("dve", ssq[:], ssq[:], 1.0 / 64.0, ALU.mult, ["ssq"], ["ssq"], s2=1e-5, op1=ALU.add)
            ACT(ssq[:], ssq[:], AF.Sqrt, ["ssq"], ["ssq"])
            S.add("dve", lambda e: e.reciprocal(out=ssq[:], in_=ssq[:]), r=["ssq"], w=["ssq"])
            O3 = O[:].rearrange("p c h d -> p (c h) d")
            TT("dve", O3, O3, bc_last(ssq[:], 64), ALU.mult, ["O", "ssq"], ["O"])
            O2 = O[:].rearrange("p c h d -> p c (h d)")
            TT("pool", O2, O2, bc_mid(outg_t[:, l, blk0 * 128:blk0 * 128 + 512], nch), ALU.mult, ["O", "outg"], ["O"])
            for gi, gate in enumerate((gate0, gate1)):
                sl = slice(256 * gi, 256 * gi + 256)
                if gate is None:
                    CP("pool", Ob[:, :, sl], O2[:, :, sl], ["O"], [f"Ob{gi}"])
                else:
                    kind, gsrc, gdst, gkey, goff = gate
                    if kind == "silu":
                        ACT(gdst[:], gsrc[:, :, goff:goff + 256], AF.Silu, [gkey], ["sgate"])
                    else:
                        ACT(gdst[:], gsrc[:, :, goff:goff + 256], AF.Sigmoid, [gkey], ["sgate"])
                    TT("dve", Ob[:, :, sl], O2[:, :, sl], gdst[:], ALU.mult, ["O", "sgate"], [f"Ob{gi}"])
            for c in range(nch):
                pT = PB16(3, 0, 128, 0, 128)
                for b in range(4):
                    TR(pT[:, b * 64:(b + 1) * 64], Ob[:, c, b * 128:(b + 1) * 128], c64b[:, 4, :], ["Ob0", "Ob1", "c64b"], ["pb3a"])
                CP("act", oT[:, blk0:blk0 + 4, pos0 + 64 * c:pos0 + 64 * c + 64], pT.rearrange("p (b t) -> p b t", b=4), ["pb3a"], ["oT"])

        def passB(sq, l):
            T = sq["T"]
            ntok = min(128, T)
            nch = ntok // 64
            ntile = T // ntok
            AR.reset()
            S.barrier(dummy[:])
            Wf = AR.alloc((128, 8, 1024), BF16)
            Wr = AR.alloc((128, 8, 8), BF16)
            Wt = AR.alloc((128, 8, 1280), BF16)
            xb = AR.alloc((128, 8, ntok), BF16)
            mqT = AR.alloc((64, 4, ntok), BF16)
            mkT = AR.alloc((64, 4, ntok), BF16)
            hqf = AR.alloc((64, 2, 4, ntok))
            ri = AR.alloc((4, ntok))
            rf = AR.alloc((4, ntok))
            cl = AR.alloc((4, ntok))
            wv = AR.alloc((4, ntok))
            wsr = AR.alloc((4, ntok))
            gfr = AR.alloc((4, ntok))
            wmax = AR.alloc((4, nch))
            blast = AR.alloc((4, nch))
            mcur = AR.alloc((4, nch))
            Mc = AR.alloc((4, nch))
            mprev = AR.alloc((4, nch))
            decr = AR.alloc((4, nch))
            decd = AR.alloc((4, nch, 4))
            mstate = AR.alloc((4, 1))
            wsg = AR.alloc((64, nch, 8))
            decbc = AR.alloc((64, nch, 4))
            mktm = AR.alloc((64, nch, 256), BF16)
            V1 = AR.alloc((64, nch, 4, 65), BF16)
            V2 = AR.alloc((64, 4, 65), BF16)
            hgv = AR.alloc((64, nch, 256), BF16)
            gmo = AR.alloc((64, nch, 256))
            ghg = AR.alloc((64, nch, 256))
            sgo = AR.alloc((64, nch, 256))
            STb = AR.alloc((64, 4, 64), BF16)
            Cm = AR.alloc((64, 4, 65))
            Cmb = AR.alloc((64, 4, 65), BF16)
            aden = AR.alloc((64, 4))
            rec = AR.alloc((64, 4))
            hC = AR.alloc((64, 4, 64))
            qsil = AR.alloc((64, 4, ntok))
            sig = AR.alloc((64, 4, ntok))
            logf = AR.alloc((64, 4, ntok))
            kk = AR.alloc((64, 4, ntok))
            lcum = AR.alloc((64, 4, ntok))
            t1 = AR.alloc((64, 4, ntok))
            t2 = AR.alloc((64, 4, ntok))
            qs_ = AR.alloc((64, 4, ntok), BF16)
            qe_ = AR.alloc((64, 4, ntok), BF16)
            ks_ = AR.alloc((64, 4, ntok), BF16)
            kdT = AR.alloc((64, 4, ntok), BF16)
            kdtm = AR.alloc((64, nch, 4, 64), BF16)
            el = AR.alloc((64, 4, nch))
            aTb = AR.alloc((64, 4, 64), BF16)
            Sh = AR.alloc((64, 4, 64))
            Shb = AR.alloc((64, 4, 64), BF16)
            O = AR.alloc((64, nch, 8, 64))
            Osq = AR.alloc((64, nch, 8, 64))
            ssq = AR.alloc((64, nch * 8))
            sgate = AR.alloc((64, nch, 256))
            Ob = AR.alloc((64, nch, 512), BF16)

            load_w(Wf, wfb[l], "Wf", "wB0")
            load_w(Wr, wrb[l], "Wr", "wB2")
            load_w(Wt, wtb[l], "Wt", "wB1")
            if sq["sample"]:
                S.dma("sp", Cm[:, :, 0:64], mC_in[l].rearrange("h k v -> k h v"), w=["Cm"], chan="stA")
                S.dma("sp", Cm[:, :, 64:65], mn_in[l].rearrange("h (k o) -> k h o", o=1), w=["Cm"], chan="stB", nc_ok=True)
                S.dma("sp", mstate[:], mm_in[l].rearrange("(h o) -> h o", o=1), w=["mstate"], chan="stC", nc_ok=True)
                S.dma("sp", Sh[:], hS_in[l].rearrange("h k v -> k h v"), w=["Sh"], chan="stD")
            else:
                MSET("pool", Cm[:], 0.0, ["Cm"])
                MSET("pool", mstate[:], 0.0, ["mstate"])
                MSET("pool", Sh[:], 0.0, ["Sh"])
            CP("dve", Shb[:], Sh[:], ["Sh"], ["Shb"])
            MSET("pool", V1[:], 1.0, ["V1"])
            bi_col = hrow_t[:, l, 2:3]
            nbf_col = nbf_t[:, l, :]

            for ti in range(ntile):
                pos0 = ti * ntok
                S.dma("pool", xb[:], xscr[l % 2, :, :, pos0:pos0 + ntok], r=[f"xscr{l % 2}"], w=["xb"], chan="xbA")
                for gi in range(2):
                    bank = gi
                    for bb in range(4):
                        b = gi * 4 + bb
                        for k in range(8):
                            MM(PB(bank, 0, 128, bb * ntok, (bb + 1) * ntok), Wf[:, k, b * 128:(b + 1) * 128], xb[:, k, :],
                               ["Wf", "xb"], [f"pb{bank}"], start=(k == 0), stop=(k == 7))
                    pv = pbank[bank][:, 0:4 * ntok].rearrange("p (b t) -> p b t", t=ntok)
                    if gi == 0:
                        q4 = mqT[:].rearrange("p (b h) t -> p b h t", b=2)
                        k4 = mkT[:].rearrange("p (b h) t -> p b h t", b=2)
                        for hf in range(2):
                            CP("act", q4[:, :, hf, :], pv[64 * hf:64 * hf + 64, 0:2, :], ["pb0"], ["mqT"])
                            TS("dve", k4[:, :, hf, :], pv[64 * hf:64 * hf + 64, 2:4, :], 0.125, ALU.mult, ["pb0"], ["mkT"])
                    else:
                        for a in range(2):
                            d4 = hqf[:, a].rearrange("p (b h) t -> p b h t", b=2)
                            for hf in range(2):
                                CP("act" if hf == 0 else "dve", d4[:, :, hf, :], pv[64 * hf:64 * hf + 64, 2 * a:2 * a + 2, :], ["pb1"], ["hqf"])
                for a in range(2):
                    for k in range(8):
                        MM(PB(2, 0, 4, a * ntok, (a + 1) * ntok), Wr[:, k, 4 * a:4 * a + 4], xb[:, k, :], ["Wr", "xb"], ["pb2"],
                           start=(k == 0), stop=(k == 7))
                CP("act", ri[:], PB(2, 0, 4, 0, ntok), ["pb2"], ["ri"])
                ACT(rf[:], PB(2, 0, 4, ntok, 2 * ntok), AF.Exp, ["pb2", "nbf"], ["rf"], bias=nbf_col, scale=-1.0)
                ACT(rf[:], rf[:], AF.Ln, ["rf", "cst"], ["rf"], bias=ONE[0:4], scale=1.0)
                for k in range(8):
                    MM(PB(0, 0, ntok, 0, 512), xb[:, k, :], Wt[:, k, 0:512], ["Wt", "xb"], ["pb0"], start=(k == 0), stop=(k == 7))
                for c in range(nch):
                    TS("dve", mktm[:, c, :], PB(0, 64 * c, 64 * c + 64, 0, 256), 0.125, ALU.mult, ["pb0"], ["mktm"])
                    CP("act", V1[:, c, :, 0:64], PB(0, 64 * c, 64 * c + 64, 256, 512).rearrange("p (h d) -> p h d", h=4), ["pb0"], ["V1"])
                for k in range(8):
                    MM(PB(1, 0, ntok, 0, 512), xb[:, k, :], Wt[:, k, 512:1024], ["Wt", "xb"], ["pb1"], start=(k == 0), stop=(k == 7))
                for c in range(nch):
                    CP("dve", hgv[:, c, :], PB(1, 64 * c, 64 * c + 64, 0, 256), ["pb1"], ["hgv"])
                    CP("act", gmo[:, c, :], PB(1, 64 * c, 64 * c + 64, 256, 512), ["pb1"], ["gmo"])
                for k in range(8):
                    MM(PB(0, 0, ntok, 0, 256), xb[:, k, :], Wt[:, k, 1024:1280], ["Wt", "xb"], ["pb0"], start=(k == 0), stop=(k == 7))
                for c in range(nch):
                    CP("act", ghg[:, c, :], PB(0, 64 * c, 64 * c + 64, 0, 256), ["pb0"], ["ghg"])
                ACT(sgo[:], gmo[:], AF.Sigmoid, ["gmo"], ["sgo"])
                S.add("dve", lambda e: e.tensor_tensor_scan(out=cl[:], data0=rmask[0:4, 0:ntok], data1=rf[:], initial=0.0, op0=ALU.mult, op1=ALU.add),
                      r=["rf", "rmask"], w=["cl"])
                STT(wv[:], ri[:], bi_col, cl[:], ALU.add, ALU.add, ["ri", "cl", "hrow"], ["wv"])
                S.add("dve", lambda e: e.tensor_reduce(out=wmax[:], in_=wv[:].rearrange("p (c t) -> p c t", t=64), axis=AX.X, op=ALU.max),
                      r=["wv"], w=["wmax"])
                TS("dve", blast[:], cl[:].rearrange("p (c t) -> p c t", t=64)[:, :, 63], -1.0, ALU.mult, ["cl"], ["blast"])
                S.add("dve", lambda e: e.tensor_tensor_scan(out=mcur[:], data0=wmax[:], data1=blast[:], initial=mstate[:], op0=ALU.max, op1=ALU.add),
                      r=["wmax", "blast", "mstate"], w=["mcur"])
                TT("dve", Mc[:], mcur[:], blast[:], ALU.subtract, ["mcur", "blast"], ["Mc"])
                CP("dve", mprev[:, 0:1], mstate[:], ["mstate"], ["mprev"])
                if nch > 1:
                    CP("dve", mprev[:, 1:nch], mcur[:, 0:nch - 1], ["mcur", "mprev"], ["mprev"])
                CP("dve", mstate[:], mcur[:, nch - 1:nch], ["mcur", "mprev"], ["mstate"])
                TT("dve", decr[:], mprev[:], Mc[:], ALU.subtract, ["mprev", "Mc"], ["decr"])
                ACT(decr[:], decr[:], AF.Exp, ["decr"], ["decr"])
                TT("dve", wsr[:].rearrange("p (c t) -> p c t", t=64), wv[:].rearrange("p (c t) -> p c t", t=64), bc_last(Mc[:], 64), ALU.subtract,
                   ["wv", "Mc"], ["wsr"])
                ACT(wsr[:], wsr[:], AF.Exp, ["wsr"], ["wsr"])
                TT("dve", gfr[:].rearrange("p (c t) -> p c t", t=64), cl[:].rearrange("p (c t) -> p c t", t=64), bc_last(Mc[:], 64), ALU.subtract,
                   ["cl", "Mc"], ["gfr"])
                ACT(gfr[:], gfr[:], AF.Exp, ["gfr"], ["gfr"])
                for c in range(nch):
                    TR(PB(2, 0, 64, 8 * c, 8 * c + 4), wsr[:, 64 * c:64 * c + 64], i4[:], ["wsr", "i4"], ["pb2"])
                    TR(PB(2, 0, 64, 8 * c + 4, 8 * c + 8), gfr[:, 64 * c:64 * c + 64], i4[:], ["gfr", "i4"], ["pb2"])
                CP("act", wsg[:].rearrange("p c e -> p (c e)"), PB(2, 0, 64, 0, 8 * nch), ["pb2"], ["wsg"])
                TT("dve", decd[:], bc_last(decr[:], 4), bc_mid(i4[:], nch), ALU.mult, ["decr", "i4"], ["decd"])
                MM(PB(2, 0, 64, 64, 64 + 4 * nch), c64[0:4, 5, :], decd[:].rearrange("p c h -> p (c h)"), ["decd", "c64"], ["pb2b"])
                CP("act", decbc[:].rearrange("p c h -> p (c h)"), PB(2, 0, 64, 64, 64 + 4 * nch), ["pb2b"], ["decbc"])
                ACT(qsil[:], hqf[:, 0], AF.Silu, ["hqf"], ["qsil"])
                ACT(sig[:], hqf[:, 1], AF.Sigmoid, ["hqf"], ["sig"])
                for h in range(4):
                    ACT(logf[:, h, :], sig[:, h, :], AF.Ln, ["sig", "oml", "lb"], ["logf"], bias=lb_t[:, h, l:l + 1], scale=oml_t[:, h, l:l + 1])
                    TS("pool", kk[:, h, :], sig[:, h, :], noml_t[:, h, l:l + 1], ALU.mult, ["sig", "noml", "oml"], ["kk"],
                       s2=oml_t[:, h, l:l + 1], op1=ALU.add)
                for h in range(4):
                    S.add("dve", lambda e, h=h: e.tensor_tensor_scan(out=lcum[:, h, :], data0=rmask[0:64, 0:ntok], data1=logf[:, h, :], initial=0.0,
                                                                     op0=ALU.mult, op1=ALU.add), r=["logf", "rmask"], w=["lcum"])
                lc4 = lcum[:].rearrange("p h (c t) -> p (h c) t", t=64)
                nhc = 4 * nch
                TT("dve", t1[:].rearrange("p h (c t) -> p (h c) t", t=64), lc4, bc_last(lc4[:, :, 31], 64), ALU.subtract, ["lcum"], ["t1"])
                ACT(t2[:], t1[:], AF.Exp, ["t1"], ["t2"])
                TT("dve", qs_[:], qsil[:], t2[:], ALU.mult, ["qsil", "t2"], ["qs_"])
                ACT(t2[:], t1[:], AF.Exp, ["t1", "qs_"], ["t2"], scale=-1.0)
                TT("dve", ks_[:], kk[:], t2[:], ALU.mult, ["kk", "t2"], ["ks_"])
                ACT(t2[:], lcum[:], AF.Exp, ["lcum", "ks_"], ["t2"])
                TT("dve", qe_[:], qsil[:], t2[:], ALU.mult, ["qsil", "t2"], ["qe_"])
                TT("dve", t1[:].rearrange("p h (c t) -> p (h c) t", t=64), bc_last(lc4[:, :, 63], 64), lc4, ALU.subtract, ["lcum", "t2"], ["t1"])
                ACT(t2[:], t1[:], AF.Exp, ["t1", "qe_"], ["t2"])
                TT("dve", kdT[:], kk[:], t2[:], ALU.mult, ["kk", "t2"], ["kdT"])
                ACT(el[:].rearrange("p h c -> p (h c)"), lc4[:, :, 63], AF.Exp, ["lcum"], ["el"])
                for c in range(nch):
                    cs = slice(64 * c, 64 * c + 64)
                    pT = PB16(3, 0, 64, 0, 128)
                    for h in range(4):
                        TR(pT[:, h * 64:(h + 1) * 64], kdT[:, h, cs], c64b[:, 4, :], ["kdT", "c64b"], ["pb3a"])
                    CP("act", kdtm[:, c].rearrange("p h t -> p (h t)"), pT, ["pb3a"], ["kdtm"])
                for c in range(nch):
                    cs = slice(64 * c, 64 * c + 64)
                    TT("pool", V2[:], V1[:, c], bc_last(wsg[:, c, 0:4], 65), ALU.mult, ["V1", "wsg"], ["V2"])
                    for h in range(4):
                        MM(PB(4, 0, 64, h * 64, (h + 1) * 64), mkT[:, h, cs], mqT[:, h, cs], ["mkT", "mqT"], ["pb4a"])
                    TT("dve", STb[:], pbank[4][0:64, 0:256].rearrange("p (h t) -> p h t", h=4), bc_mid(C_TRIU, 4), ALU.mult, ["pb4a", "c64"], ["STb"])
                    TT("pool", Cm[:], Cm[:], bc_last(decbc[:, c, :], 65), ALU.mult, ["Cm", "decbc"], ["Cm"])
                    CP("pool", Cmb[:], Cm[:], ["Cm"], ["Cmb"])
                    for h in range(4):
                        MM(PB(5, 0, 64, h * 65, (h + 1) * 65), STb[:, h, :], V2[:, h, :], ["STb", "V2"], ["pb5"], start=True, stop=False)
                        MM(PB(5, 0, 64, h * 65, (h + 1) * 65), mqT[:, h, cs], Cmb[:, h, :], ["mqT", "Cmb"], ["pb5"], start=False, stop=True)
                    pn = pbank[5][0:64, 0:260].rearrange("p (h e) -> p h e", h=4)
                    CP("act", hC[:], pn[:, :, 0:64], ["pb5"], ["hC"])
                    CP("act", aden[:], pn[:, :, 64], ["pb5"], ["aden"])
                    STT(rec[:], aden[:], -1.0, aden[:], ALU.mult, ALU.max, ["aden"], ["rec"])
                    TT("dve", rec[:], rec[:], wsg[:, c, 4:8], ALU.max, ["rec", "wsg"], ["rec"])
                    S.add("dve", lambda e: e.reciprocal(out=rec[:], in_=rec[:]), r=["rec"], w=["rec"])
                    TT("dve", hC[:], hC[:], bc_last(rec[:], 64), ALU.mult, ["hC", "rec"], ["hC"])
                    TT("pool", O[:, c, 0:4, :], hC[:], sgo[:, c, :].rearrange("p (h d) -> p h d", h=4), ALU.mult, ["hC", "sgo"], ["O"])
                    for h in range(4):
                        MM(PB(6, 0, 64, h * 65, (h + 1) * 65), mktm[:, c, h * 64:(h + 1) * 64], V2[:, h, :], ["mktm", "V2"], ["pb6"])
                    TT("dve", Cm[:], Cm[:], pbank[6][0:64, 0:260].rearrange("p (h e) -> p h e", h=4), ALU.add, ["Cm", "pb6", "Cmb"], ["Cm"])
                    for h in range(4):
                        MM(PB(4, 0, 64, 256 + h * 64, 256 + (h + 1) * 64), ks_[:, h, cs], qs_[:, h, cs], ["ks_", "qs_"], ["pb4b"])
                    TT("dve", aTb[:], pbank[4][0:64, 256:512].rearrange("p (h t) -> p h t", h=4), bc_mid(C_TRIU, 4), ALU.mult, ["pb4b", "c64"], ["aTb"])
                    for h in range(4):
                        MM(PB(7, 0, 64, h * 64, (h + 1) * 64), aTb[:, h, :], hgv[:, c, h * 64:(h + 1) * 64], ["aTb", "hgv"], ["pb7a"], start=True, stop=False)
                        MM(PB(7, 0, 64, h * 64, (h + 1) * 64), qe_[:, h, cs], Shb[:, h, :], ["qe_", "Shb"], ["pb7a"], start=False, stop=True)
                    CP("act", O[:, c, 4:8, :].rearrange("p h t -> p (h t)"), PB(7, 0, 64, 0, 256), ["pb7a"], ["O"])
                    for h in range(4):
                        MM(PB(7, 0, 64, 256 + h * 64, 256 + (h + 1) * 64), kdtm[:, c, h, :], hgv[:, c, h * 64:(h + 1) * 64], ["kdtm", "hgv"], ["pb7b"])
                    TT("pool", Sh[:], Sh[:], bc_last(el[:, :, c], 64), ALU.mult, ["Sh", "el", "Shb"], ["Sh"])
                    TT("dve", Sh[:].rearrange("p h t -> p (h t)"), Sh[:].rearrange("p h t -> p (h t)"), PB(7, 0, 64, 256, 512), ALU.add, ["Sh", "pb7b"], ["Sh"])
                    CP("pool", Shb[:], Sh[:], ["Sh"], ["Shb"])
                merge(O, Osq, ssq, Ob, nch, l, 4, pos0, gate0=None, gate1=("sigmoid", ghg, sgate, "ghg", 0))
            S.dma("sp", sq["c"](l).rearrange("h k v -> k h v"), Cm[:, :, 0:64], r=["Cm"], chan="osA", final=True)
            S.dma("sp", sq["n"](l).rearrange("h (k o) -> k h o", o=1), Cm[:, :, 64:65], r=["Cm"], chan="osB", final=True, nc_ok=True)
            S.dma("sp", sq["m"](l).rearrange("(h o) -> h o", o=1), mstate[:], r=["mstate"], chan="osC", final=True, nc_ok=True)
            S.dma("sp", sq["hg"](l).rearrange("h k v -> k h v"), Sh[:], r=["Sh"], chan="osD", final=True)

        NW = 8

        def phaseD(sq, l, last_layer):
            T = sq["T"]
            nd = min(512, T)
            ntile = T // nd
            AR.reset()
            S.barrier(dummy[:])
            ring = [AR.alloc((128, 4096), BF16) for _ in range(NW)]
            rx = AR.alloc((128, 8, nd))
            x1b = AR.alloc((128, 8, nd), BF16)
            sqb = AR.alloc((128, 8, nd), BF16)
            hT = AR.alloc((128, 32, nd), BF16)
            tmp = [AR.alloc((128, nd)) for _ in range(2)]
            mean_sb = AR.alloc((128, nd))
            rstd = AR.alloc((128, nd))
            ytm = AR.alloc((128, 1024)) if last_layer else None
            vec = vec_t[:, l, :].rearrange("p (a b) -> p a b", a=5)
            total = ntile * 18
            state = dict(next=0)

            def fetch():
                i = state["next"]
                if i >= total:
                    return
                slot = i % NW
                ci = i % 18
                for part in range(4):
                    S.dma("pool", ring[slot][:, part * 1024:(part + 1) * 1024], wdn[l, ci, :, part * 1024:(part + 1) * 1024],
                          w=[f"ring{slot}"], chan=f"rg{slot}")
                state["next"] = i + 1

            for _ in range(NW):
                fetch()
            used = dict(n=0)

            def chunk():
                i = used["n"]
                used["n"] = i + 1
                return i % NW

            def layernorm(gi, bi_):
                CP("act", x1b[:], rx[:], ["rx"], ["x1b"])
                ACT(sqb[:], rx[:], AF.Square, ["rx"], ["sqb"])
                for m in range(8):
                    MM(PB(2, 0, 128, 0, nd), odiv[:], x1b[:, m, :], ["odiv", "x1b"], ["pb2"], start=(m == 0), stop=(m == 7))
                for m in range(8):
                    MM(PB(3, 0, 128, 0, nd), odiv[:], sqb[:, m, :], ["odiv", "sqb"], ["pb3"], start=(m == 0), stop=(m == 7))
                CP("act", mean_sb[:], PB(2, 0, 128, 0, nd), ["pb2"], ["mean"])
                ACT(rstd[:], PB(2, 0, 128, 0, nd), AF.Square, ["pb2"], ["rstd"])
                TT("dve", rstd[:], PB(3, 0, 128, 0, nd), rstd[:], ALU.subtract, ["pb3", "rstd"], ["rstd"])
                ACT(rstd[:], rstd[:], AF.Sqrt, ["rstd", "cst"], ["rstd"], bias=EPS5, scale=1.0)
                S.add("dve", lambda e: e.reciprocal(out=rstd[:], in_=rstd[:]), r=["rstd"], w=["rstd"])
                for m in range(8):
                    t = tmp[m % 2]
                    tk = f"tmp{m % 2}"
                    TT("dve", t[:], rx[:, m, :], mean_sb[:], ALU.subtract, ["rx", "mean"], [tk])
                    TT("pool", t[:], t[:], rstd[:], ALU.mult, [tk, "rstd"], [tk])
                    ACT(rx[:, m, :], t[:], AF.Identity, [tk, "vec"], ["rx"], bias=vec[:, bi_, m:m + 1], scale=vec[:, gi, m:m + 1])
                    CP("pool", x1b[:, m, :], rx[:, m, :], ["rx"], ["x1b"])

            for ti in range(ntile):
                pos0 = ti * nd
                S.dma("sp", rx[:], xscr[l % 2, :, :, pos0:pos0 + nd], r=[f"xscr{l % 2}"], w=["rx"], chan="rxD")
                for j in range(2):
                    slot = chunk()
                    wv_ = ring[slot][:].rearrange("p (k c) -> p k c", k=8)
                    for mm_ in range(4):
                        m = 4 * j + mm_
                        bank = m % 2
                        for k in range(8):
                            MM(PB(bank, 0, 128, 0, nd), wv_[:, k, mm_ * 128:(mm_ + 1) * 128], oT[:, k, pos0:pos0 + nd], [f"ring{slot}", "oT"], [f"pb{bank}"],
                               start=(k == 0), stop=(k == 7))
                        STT(rx[:, m, :], rx[:, m, :], DN_ALPHA, PB(bank, 0, 128, 0, nd), ALU.mult, ALU.add, ["rx", f"pb{bank}"], ["rx"])
                    fetch()
                layernorm(1, 2)
                for j in range(8):
                    slot = chunk()
                    wv_ = ring[slot][:].rearrange("p (k c) -> p k c", k=8)
                    for mm_ in range(4):
                        m = 4 * j + mm_
                        bank = m % 2
                        for k in range(8):
                            MM(PB(bank, 0, 128, 0, nd), wv_[:, k, mm_ * 128:(mm_ + 1) * 128], x1b[:, k, :], [f"ring{slot}", "x1b"], [f"pb{bank}"],
                               start=(k == 0), stop=(k == 7))
                        t = tmp[m % 2]
                        tk = f"tmp{m % 2}"
                        ACT(t[:], PB(bank, 0, 128, 0, nd), AF.Relu, [f"pb{bank}", "b1"], [tk], bias=b1_t[:, l, m:m + 1], scale=1.0)
                        TT("pool", hT[:, m, :], t[:], t[:], ALU.mult, [tk], ["hT"])
                    fetch()
                for m in range(8):
                    slot = chunk()
                    wv_ = ring[slot][:].rearrange("p (k c) -> p k c", k=32)
                    bank = m % 2
                    for k in range(32):
                        MM(PB(bank, 0, 128, 0, nd), wv_[:, k, :], hT[:, k, :], [f"ring{slot}", "hT"], [f"pb{bank}"], start=(k == 0), stop=(k == 31))
                    t = tmp[m % 2]
                    tk = f"tmp{m % 2}"
                    ACT(t[:], PB(bank, 0, 128, 0, nd), AF.Identity, [f"pb{bank}", "vec"], [tk], bias=vec[:, 0, m:m + 1], scale=1.0)
                    STT(rx[:, m, :], rx[:, m, :], DN_ALPHA, t[:], ALU.mult, ALU.add, ["rx", tk], ["rx"])
                    fetch()
                layernorm(3, 4)
                if not last_layer:
                    S.dma("sp", xscr[(l + 1) % 2, :, :, pos0:pos0 + nd], rx[:], r=["rx"], w=[f"xscr{(l + 1) % 2}"], chan="xoD")
                else:
                    nsub = (nd + 127) // 128
                    for su in range(nsub):
                        ns = min(128, nd - su * 128)
                        for m in range(8):
                            bank = 4 + m // 4
                            TR(PB(bank, 0, ns, (m % 4) * 128, (m % 4 + 1) * 128), rx[:, m, su * 128:su * 128 + ns], ID128, ["rx", "c128"], [f"pb{bank}"])
                        CP("act", ytm[0:ns, 0:512], PB(4, 0, ns, 0, 512), ["pb4"], ["ytm"])
                        CP("dve", ytm[0:ns, 512:1024], PB(5, 0, ns, 0, 512), ["pb5"], ["ytm"])
                        S.dma("sp", sq["y"][pos0 + su * 128:pos0 + su * 128 + ns, :], ytm[0:ns, :], r=["ytm"], chan="yD", final=True)

        import os
        kstop = os.environ.get("KSTOP", "")
        try:
            for sq in seqs:
                phase0(sq)
                stop_at("p0")
                for l in range(L):
                    passA(sq, l)
                    stop_at("A")
                    passB(sq, l)
                    if KDEBUG and l == 0 and sq is seqs[0]:
                        S.dma("sp", dbg_oT, oT[:], r=["oT"], chan="dbgo", final=True)
                    stop_at("B")
                    phaseD(sq, l, l == L - 1)
                    stop_at("D")
        except StopBuild as e:
            print("build stopped at", e)
        S.emit()
    return nc


def host_layout_weights(inp, L):
    w_in = np.asarray(inp["w_in"], np.float32)[:L]

    def fm(cols):
        w = w_in[:, :, cols]
        return np.ascontiguousarray(w.reshape(L, 8, 128, len(cols)).transpose(0, 2, 1, 3))

    out = dict(wfa=fm(FA_COLS), wta=fm(TA_COLS), wfb=fm(FB_COLS), wrb=fm(RB_COLS), wtb=fm(TB_COLS))
    w_out = np.asarray(inp["w_out"], np.float32)[:L]
    w1 = np.asarray(inp["w1"], np.float32)[:L]
    w2 = np.asarray(inp["w2"], np.float32)[:L]
    wdn = np.empty((L, 18, 128, 4096), np.float32)
    wo = w_out.reshape(L, 8, 128, 2, 512).transpose(0, 3, 2, 1, 4)
    wdn[:, 0:2] = wo.reshape(L, 2, 128, 4096)
    w1r = w1.reshape(L, 8, 128, 8, 512).transpose(0, 3, 2, 1, 4)
    wdn[:, 2:10] = w1r.reshape(L, 8, 128, 4096)
    w2r = w2.reshape(L, 32, 128, 8, 128).transpose(0, 3, 2, 1, 4)
    wdn[:, 10:18] = w2r.reshape(L, 8, 128, 4096)
    out["wdn"] = wdn
    cw = np.asarray(inp["gdn_conv_w"], np.float32)[:L]
    out["convw"] = np.ascontiguousarray(cw.reshape(L, 4, 6, 128).transpose(0, 3, 2, 1))
    out["b1c"] = np.ascontiguousarray(np.asarray(inp["b1"], np.float32)[:L].reshape(L, 32, 128).transpose(0, 2, 1))
    vs = np.stack([np.asarray(inp[k], np.float32)[:L].reshape(L, 8, 128).transpose(0, 2, 1)
                   for k in ("b2", "ln1_g", "ln1_b", "ln2_g", "ln2_b")], axis=2)
    out["vecs"] = np.ascontiguousarray(vs)
    out["outg"] = np.ascontiguousarray(np.asarray(inp["out_norm_g"], np.float32)[:L])
    out["hvec"] = np.ascontiguousarray(np.stack([np.asarray(inp[k], np.float32)[:L] for k in
                                                 ("gdn_A_log", "gdn_dt_bias", "mlstm_b_i", "mlstm_b_f")], axis=1))
    out["lbl"] = np.ascontiguousarray(np.asarray(inp["hgrn_lb_logits"], np.float32)[:L])
    out.update(make_consts())
    return out


_PROG_CACHE = {}


def run(inp, n_cores, NPS, TP, L, SAMPLE):
    key = (NPS, TP, L, SAMPLE)
    if key not in _PROG_CACHE:
        _PROG_CACHE[key] = build(NPS, TP, L, SAMPLE)
    nc = _PROG_CACHE[key]
    shared = host_layout_weights(inp, L)
    xpr = np.asarray(inp["x_prompt"], np.float32)
    in_maps = []
    for c in range(n_cores):
        m = dict(shared)
        m["xp"] = np.ascontiguousarray(xpr[c * NPS:(c + 1) * NPS])
        if SAMPLE:
            m["xs"] = np.ascontiguousarray(np.asarray(inp["x_sample"], np.float32)[c])
            m["ck"] = np.ascontiguousarray(np.asarray(inp["cache_sb_k"], np.float32)[:L, c].reshape(L, PAST, 256))
            m["cv"] = np.ascontiguousarray(np.asarray(inp["cache_sb_v"], np.float32)[:L, c].reshape(L, PAST, 256))
            m["sconv"] = np.ascontiguousarray(np.asarray(inp["state_gdn_conv"], np.float32)[:L, c])
            m["gS"] = np.ascontiguousarray(np.asarray(inp["state_gdn_S"], np.float32)[:L, c])
            m["mC"] = np.ascontiguousarray(np.asarray(inp["state_mlstm_C"], np.float32)[:L, c])
            m["mn"] = np.ascontiguousarray(np.asarray(inp["state_mlstm_n"], np.float32)[:L, c])
            m["mm"] = np.ascontiguousarray(np.asarray(inp["state_mlstm_m"], np.float32)[:L, c])
            m["hS"] = np.ascontiguousarray(np.asarray(inp["state_hgrn_S"], np.float32)[:L, c])
        in_maps.append(m)
    res = run_bass_kernel_spmd(nc, in_maps, core_ids=list(range(n_cores)))
    R = res.results
    global LAST_RESULTS
    LAST_RESULTS = R

    def cat(name, axis):
        return np.concatenate([np.asarray(r[name], np.float32) for r in R], axis=axis)

    y_p = cat("y_p", 0)
    outs = [y_p]
    if SAMPLE:
        outs.append(np.stack([np.asarray(r["y_s"], np.float32) for r in R], 0))
    else:
        outs.append(None)
    B = n_cores * NPS
    outs += [cat("p_k", 1).reshape(L, B, TP, 4, 64), cat("p_v", 1).reshape(L, B, TP, 4, 64), cat("p_conv", 1), cat("p_gdn", 1),
             cat("p_c", 1), cat("p_n", 1), cat("p_m", 1), cat("p_hg", 1)]
    if SAMPLE:
        outs += [cat("s_k", 1).reshape(L, n_cores, 64, 4, 64), cat("s_v", 1).reshape(L, n_cores, 64, 4, 64), cat("s_conv", 1),
                 cat("s_gdn", 1), cat("s_c", 1), cat("s_n", 1), cat("s_m", 1), cat("s_hg", 1)]
    return tuple(outs)


def kernel(**inputs):
    return run(inputs, 8, 2, 2048, DEPTH, True)
```

```python
import contextlib
import os
import math
import numpy as np
import concourse.bass as bass
import concourse.mybir as mybir
from concourse.bass_utils import run_bass_kernel_spmd

F32 = mybir.dt.float32
BF16 = mybir.dt.bfloat16
ALU = mybir.AluOpType
AF = mybir.ActivationFunctionType
AX = mybir.AxisListType

D_MODEL = 1024
DEPTH = 4
PAST = 2048
DN_ALPHA = (2 * DEPTH) ** 0.25
OFF = dict(gq=0, gk=256, gv=512, ggate=768, gbeta=1024, ga=1028, sq=1032, sk=1288, sv=1544,
           mq=1800, mk=2056, mv=2312, mo=2568, mi=2824, mf=2828, hq=2832, hf=3088, hi=3344, hg=3600)


def _cols(*specs):
    out = []
    for name, n in specs:
        out.extend(range(OFF[name], OFF[name] + n))
    return np.array(out)


FA_COLS = _cols(("gq", 256), ("gk", 256), ("gv", 256), ("sq", 256), ("sk", 256))
TA_COLS = _cols(("sk", 256), ("sv", 256), ("ggate", 256), ("gbeta", 4), ("ga", 4))
FB_COLS = _cols(("mq", 256), ("mk", 256), ("hq", 256), ("hf", 256))
RB_COLS = _cols(("mi", 4), ("mf", 4))
TB_COLS = _cols(("mk", 256), ("mv", 256), ("hi", 256), ("mo", 256), ("hg", 256))

ENGS = ("pe", "act", "dve", "pool", "sp")
EPOCH = 30000


class Op:
    __slots__ = ("eng", "fn", "deps", "marked", "chan", "val", "n")

    def __init__(self, eng, fn, chan=None):
        self.eng = eng
        self.fn = fn
        self.deps = ()
        self.marked = False
        self.chan = chan
        self.val = 0
        self.n = 0


class Sched:
    def __init__(self, nc, same_sync=True):
        self.nc = nc
        self.same_sync = same_sync
        self.ops = {e: [] for e in ENGS}
        self.last_w = {}
        self.readers = {}
        self.chan_cnt = {}
        self.chan_last = {}
        self.final = []
        self.fence = None
        self.fenced = set()

    def add(self, eng, fn, r=(), w=(), chan=None):
        r2 = []
        w2 = []
        for k in r:
            if k.startswith("pb"):
                if k[:3] not in w2:
                    w2.append(k[:3])
            else:
                r2.append(k)
        for k in w:
            k = k[:3] if k.startswith("pb") else k
            if k not in w2:
                w2.append(k)
        r, w = r2, w2
        op = Op(eng, fn, chan)
        deps = set()
        for k in r:
            lw = self.last_w.get(k)
            if lw is not None:
                deps.add(lw)
        for k in w:
            lw = self.last_w.get(k)
            if lw is not None:
                deps.add(lw)
            for rd in self.readers.get(k, ()):
                deps.add(rd)
        if self.fence is not None and eng not in self.fenced:
            deps.add(self.fence)
            self.fenced.add(eng)
        out = []
        for d in deps:
            if d.eng == eng and d.chan is None and chan is None:
                if eng == "pe" or not self.same_sync:
                    continue
            out.append(d)
            d.marked = True
        op.deps = out
        for k in r:
            self.readers.setdefault(k, []).append(op)
        for k in w:
            self.last_w[k] = op
            self.readers[k] = []
        if chan is not None:
            c = self.chan_cnt.get(chan, 0) + 1
            self.chan_cnt[chan] = c
            op.val = 16 * c
            op.marked = True
            self.chan_last[chan] = op
        self.ops[eng].append(op)
        return op

    def dma(self, eng, out, in_, r=(), w=(), chan=None, final=False, nc_ok=False):
        assert chan is not None
        if nc_ok:
            fn = lambda e: e.dma_start(out=out, in_=in_, allow_slow_non_contiguous=True)
        else:
            fn = lambda e: e.dma_start(out=out, in_=in_)
        op = self.add(eng, fn, r=r, w=w, chan=chan)
        if final:
            self.final.append(op)
        return op

    def barrier(self, dummy):
        lasts = [ops[-1] for ops in self.ops.values() if ops and ops[-1].chan is None]
        lasts += [o for c, o in self.chan_last.items() if not c.startswith("pc")]
        op = Op("dve", lambda e: e.memset(dummy, 0.0))
        deps = []
        for d in set(lasts):
            if d.eng == "dve" and d.chan is None:
                continue
            deps.append(d)
            d.marked = True
        op.deps = deps
        op.marked = True
        self.ops["dve"].append(op)
        self.fence = op
        self.fenced = {"dve"}
        self.last_w = {k: v for k, v in self.last_w.items() if k.startswith("wbf")}
        self.readers = {}

    def emit(self):
        nc = self.nc
        with contextlib.ExitStack() as st:
            esems = {}
            for e in ENGS:
                n = 0
                for op in self.ops[e]:
                    if op.marked and op.chan is None:
                        n += 1
                        op.n = n
                nep = (n + EPOCH - 1) // EPOCH
                esems[e] = [st.enter_context(nc.semaphore(f"s_{e}{i}")) for i in range(nep)]
            csems = {c: st.enter_context(nc.semaphore(f"c_{c}")) for c in self.chan_cnt}

            def semval(d):
                if d.chan is not None:
                    return csems[d.chan], d.val
                ep = (d.n - 1) // EPOCH
                return esems[d.eng][ep], d.n - ep * EPOCH

            block = st.enter_context(nc.Block())
            finals = [semval(o) for o in self.final]

            def run(e, eng):
                waited = {}
                for op in self.ops[e]:
                    for d in op.deps:
                        s, v = semval(d)
                        key = id(s)
                        if waited.get(key, 0) >= v:
                            continue
                        waited[key] = v
                        eng.wait_ge(s, v)
                    ins = op.fn(eng)
                    if op.marked:
                        if op.chan is not None:
                            ins.then_inc(csems[op.chan], 16)
                        else:
                            s, _ = semval(op)
                            ins.then_inc(s, 1)
                if e == "sp":
                    best = {}
                    for s, v in finals:
                        if best.get(id(s), (None, 0))[1] < v:
                            best[id(s)] = (s, v)
                    for s, v in best.values():
                        eng.wait_ge(s, v)

            @block.tensor
            def _(eng):
                run("pe", eng)

            @block.scalar
            def _(eng):
                run("act", eng)

            @block.vector
            def _(eng):
                run("dve", eng)

            @block.gpsimd
            def _(eng):
                run("pool", eng)

            @block.sync
            def _(eng):
                run("sp", eng)


class StopBuild(Exception):
    pass


def interleave(*gens):
    gens = list(gens)
    while gens:
        for g in list(gens):
            try:
                next(g)
            except StopIteration:
                gens.remove(g)


def stop_at(tag):
    import os
    if os.environ.get("KSTOP", "") == tag:
        raise StopBuild(tag)


class Arena:
    def __init__(self, ap, ncols):
        self.ap = ap
        self.n = ncols
        self.off = 0

    def reset(self):
        self.off = 0

    def alloc(self, shape, dt=F32):
        P = shape[0]
        free = int(np.prod(shape[1:]))
        nb = free * (4 if dt == F32 else 2)
        ncol = (nb + 3) // 4
        a = self.ap[0:P, self.off:self.off + ncol]
        self.off += (ncol + 15) // 16 * 16
        self.hi = max(getattr(self, "hi", 0), self.off)
        assert self.off <= self.n, f"arena overflow {self.off} > {self.n}"
        if dt != F32:
            a = a.bitcast(dt)
            if a.shape[1] != free:
                a = a[:, 0:free]
        if len(shape) == 3:
            a = a.rearrange("p (a b) -> p a b", a=shape[1])
        elif len(shape) == 4:
            a = a.rearrange("p (a b c) -> p a b c", a=shape[1], b=shape[2])
        return a


def make_consts():
    p = np.arange(64)[:, None]
    f = np.arange(64)[None, :]
    c64 = np.zeros((64, 8, 64), np.float32)
    c64[:, 0] = (p <= f)
    c64[:, 1] = -(p <= f).astype(np.float32)
    c64[:, 2] = -(p < f).astype(np.float32)
    c64[:, 3] = -(f < p).astype(np.float32)
    c64[:, 4] = np.eye(64)
    c64[:, 5] = 1.0
    P = np.arange(128)[:, None]
    Fq = np.arange(128)[None, :]
    c128 = np.zeros((128, 4, 128), np.float32)
    c128[:, 0] = np.eye(128)
    c128[:, 1] = (P < Fq)
    c128[:, 2] = (P >= Fq)
    c128[:, 3] = (P // 64 == Fq // 64)
    rmask = np.ones((64, 128), np.float32)
    rmask[:, ::64] = 0.0
    i4 = np.zeros((4, 4), np.float32)
    i4[:] = np.eye(4)
    return dict(c64=c64, c128=c128, rmask=rmask, i4=i4)


def build(NPS, TP, L, SAMPLE, TS_=64):
    nc = bass.Bass("TRN2", target_bir_lowering=False)
    dins = {}
    douts = {}

    def din(name, shape):
        dins[name] = nc.dram_tensor(name, list(shape), F32, kind="ExternalInput").ap()
        return dins[name]

    def dout(name, shape):
        douts[name] = nc.dram_tensor(name, list(shape), F32, kind="ExternalOutput").ap()
        return douts[name]

    xp = din("xp", (NPS, TP, 1024))
    wfa = din("wfa", (L, 128, 8, 1280))
    wta = din("wta", (L, 128, 8, 776))
    wfb = din("wfb", (L, 128, 8, 1024))
    wrb = din("wrb", (L, 128, 8, 8))
    wtb = din("wtb", (L, 128, 8, 1280))
    wdn = din("wdn", (L, 18, 128, 4096))
    convw = din("convw", (L, 128, 6, 4))
    b1c = din("b1c", (L, 128, 32))
    vecs = din("vecs", (L, 128, 5, 8))
    outg = din("outg", (L, 1024))
    hvec = din("hvec", (L, 4, 4))
    lbl = din("lbl", (L, 256))
    c64d = din("c64", (64, 8, 64))
    c128d = din("c128", (128, 4, 128))
    rmaskd = din("rmask", (64, 128))
    i4d = din("i4", (4, 4))
    if SAMPLE:
        xs_in = din("xs", (TS_, 1024))
        ck = din("ck", (L, PAST, 256))
        cv = din("cv", (L, PAST, 256))
        sconv = din("sconv", (L, 3, 768))
        gS_in = din("gS", (L, 4, 64, 64))
        mC_in = din("mC", (L, 4, 64, 64))
        mn_in = din("mn", (L, 4, 64))
        mm_in = din("mm", (L, 4))
        hS_in = din("hS", (L, 4, 64, 64))
    y_p = dout("y_p", (NPS, TP, 1024))
    p_k = dout("p_k", (L, NPS, TP, 256))
    p_v = dout("p_v", (L, NPS, TP, 256))
    p_conv = dout("p_conv", (L, NPS, 3, 768))
    p_gdn = dout("p_gdn", (L, NPS, 4, 64, 64))
    p_c = dout("p_c", (L, NPS, 4, 64, 64))
    p_n = dout("p_n", (L, NPS, 4, 64))
    p_m = dout("p_m", (L, NPS, 4))
    p_hg = dout("p_hg", (L, NPS, 4, 64, 64))
    if SAMPLE:
        y_s = dout("y_s", (TS_, 1024))
        s_k = dout("s_k", (L, 1, TS_, 256))
        s_v = dout("s_v", (L, 1, TS_, 256))
        s_conv = dout("s_conv", (L, 1, 3, 768))
        s_gdn = dout("s_gdn", (L, 1, 4, 64, 64))
        s_c = dout("s_c", (L, 1, 4, 64, 64))
        s_n = dout("s_n", (L, 1, 4, 64))
        s_m = dout("s_m", (L, 1, 4))
        s_hg = dout("s_hg", (L, 1, 4, 64, 64))
    TMAX = max(TP, TS_)
    xscr = nc.dram_tensor("xscr", [2, 128, 8, TMAX], F32, kind="Internal").ap()
    wdn_bf = nc.dram_tensor("wdn_bf", [L, 18, 128, 4096], BF16, kind="Internal").ap()
    wfa_bf = nc.dram_tensor("wfa_bf", [L, 128, 8 * 1280], BF16, kind="Internal").ap()
    wta_bf = nc.dram_tensor("wta_bf", [L, 128, 8 * 776], BF16, kind="Internal").ap()
    wfb_bf = nc.dram_tensor("wfb_bf", [L, 128, 8 * 1024], BF16, kind="Internal").ap()
    wrb_bf = nc.dram_tensor("wrb_bf", [L, 128, 8 * 8], BF16, kind="Internal").ap()
    wtb_bf = nc.dram_tensor("wtb_bf", [L, 128, 8 * 1280], BF16, kind="Internal").ap()
    import os
    KDEBUG = os.environ.get("KDEBUG", "") == "1"
    if KDEBUG:
        dbg_oT = nc.dram_tensor("dbg_oT", [128, 8, TMAX], BF16, kind="ExternalOutput").ap()

    seqs = []
    for i in range(NPS):
        seqs.append(dict(T=TP, past=0, x=xp[i], y=y_p[i], k=lambda l, i=i: p_k[l, i], v=lambda l, i=i: p_v[l, i],
                         conv=lambda l, i=i: p_conv[l, i], gdn=lambda l, i=i: p_gdn[l, i], c=lambda l, i=i: p_c[l, i],
                         n=lambda l, i=i: p_n[l, i], m=lambda l, i=i: p_m[l, i], hg=lambda l, i=i: p_hg[l, i], sample=False))
    if SAMPLE:
        seqs.append(dict(T=TS_, past=PAST, x=xs_in, y=y_s, k=lambda l: s_k[l, 0], v=lambda l: s_v[l, 0],
                         conv=lambda l: s_conv[l, 0], gdn=lambda l: s_gdn[l, 0], c=lambda l: s_c[l, 0],
                         n=lambda l: s_n[l, 0], m=lambda l: s_m[l, 0], hg=lambda l: s_hg[l, 0], sample=True))

    AW = 38100
    NKMAX = PAST + TS_ if SAMPLE else TP
    NKMAX = max(NKMAX, TP)
    NKT = (NKMAX + 127) // 128

    with contextlib.ExitStack() as st:
        def sb(name, shape, dt=F32):
            return st.enter_context(nc.sbuf_tensor(name, list(shape), dt))

        S = Sched(nc)
        arena_t = sb("arena", (128, AW))
        AR = Arena(arena_t, AW)
        dummy = sb("fence_dummy", (1, 8))
        oT = sb("oT", (128, 8, TMAX), BF16)
        c64 = sb("c64t", (64, 8, 64))
        c64b = sb("c64b", (64, 8, 64), BF16)
        c128 = sb("c128t", (128, 4, 128))
        c128b = sb("c128b", (128, 4, 128), BF16)
        rmask = sb("rmaskt", (64, 128))
        i4 = sb("i4t", (4, 4))
        cst = sb("cst", (128, 8))
        odiv = sb("odiv", (128, 128), BF16)
        convw_t = sb("convw_t", (128, L, 24))
        b1_t = sb("b1_t", (128, L, 32))
        vec_t = sb("vec_t", (128, L, 40))
        outg_t = sb("outg_t", (64, L, 1024))
        hrow_t = sb("hrow_t", (4, L, 4))
        hbc_t = sb("hbc_t", (64, L, 16))
        negA_t = sb("negA_t", (64, L, 4))
        nbf_t = sb("nbf_t", (4, L, 1))
        lb_t = sb("lb_t", (64, 4, L))
        oml_t = sb("oml_t", (64, 4, L))
        noml_t = sb("noml_t", (64, 4, L))
        lbtmp = sb("lbtmp", (64, 4, L))
        lbsum = sb("lbsum", (64, 4))
        pbank = [st.enter_context(nc.psum_tensor(f"pb{i}", [128, 512], F32)) for i in range(8)]

        def PB(i, p0, p1, c0, c1):
            return pbank[i][p0:p1, c0:c1]

        def PB16(i, p0, p1, c0, c1):
            return pbank[i][p0:p1, c0:c1].bitcast(BF16)

        def ACT(out, in_, func, r, w, bias=None, scale=None):
            kw = {}
            if bias is not None:
                kw["bias"] = bias
            if scale is not None:
                kw["scale"] = scale
            S.add("act", lambda e: e.activation(out=out, in_=in_, func=func, **kw), r=r, w=w)

        def TT(eng, out, in0, in1, op, r, w):
            S.add(eng, lambda e: e.tensor_tensor(out=out, in0=in0, in1=in1, op=op), r=r, w=w)

        def TS(eng, out, in0, s1, op0, r, w, s2=None, op1=None):
            if op1 is None:
                S.add(eng, lambda e: e.tensor_scalar(out=out, in0=in0, scalar1=s1, scalar2=None, op0=op0), r=r, w=w)
            else:
                S.add(eng, lambda e: e.tensor_scalar(out=out, in0=in0, scalar1=s1, scalar2=s2, op0=op0, op1=op1), r=r, w=w)

        def STT(out, in0, scalar, in1, op0, op1, r, w):
            S.add("dve", lambda e: e.scalar_tensor_tensor(out=out, in0=in0, scalar=scalar, in1=in1, op0=op0, op1=op1), r=r, w=w)

        def CP(eng, out, in_, r, w):
            if eng == "act":
                S.add("act", lambda e: e.activation(out=out, in_=in_, func=AF.Copy), r=r, w=w)
            else:
                S.add(eng, lambda e: e.tensor_copy(out=out, in_=in_), r=r, w=w)

        def MM(out, lhsT, rhs, r, w, start=True, stop=True):
            S.add("pe", lambda e: e.matmul(out, lhsT=lhsT, rhs=rhs, start=start, stop=stop), r=r, w=w)

        def TR(out, in_, ident, r, w):
            S.add("pe", lambda e: e.transpose(out, in_, ident), r=r, w=w)

        def MSET(eng, ap, val, w):
            S.add(eng, lambda e: e.memset(ap, val), w=w)

        def bc_mid(ap2, n):
            return ap2.unsqueeze(1).to_broadcast([ap2.shape[0], n, ap2.shape[1]])

        def bc_last(ap2, n):
            return ap2.unsqueeze(2).to_broadcast([ap2.shape[0], ap2.shape[1], n])

        ONE = cst[:, 0:1]
        EPS6 = cst[:, 1:2]
        EPS5 = cst[:, 2:3]

        S.dma("sp", c64[:], c64d, w=["c64"], chan="ld0")
        S.dma("sp", c128[:], c128d, w=["c128"], chan="ld1")
        S.dma("sp", rmask[:], rmaskd, w=["rmask"], chan="ld2")
        S.dma("sp", i4[:], i4d, w=["i4"], chan="ld3")
        S.dma("sp", convw_t[:].rearrange("p l (b j) -> p l b j", b=6), convw.rearrange("l p b j -> p l b j"), w=["convw"], chan="ld4")
        S.dma("sp", b1_t[:], b1c.rearrange("l p m -> p l m"), w=["b1"], chan="ld5")
        S.dma("sp", vec_t[:].rearrange("p l (a b) -> p l a b", a=5), vecs.rearrange("l p a b -> p l a b"), w=["vec"], chan="ld6")
        for l in range(L):
            S.dma("sp", outg_t[:, l, :], outg[l:l + 1, :].to_broadcast([64, 1024]), w=["outg"], chan="ld7")
            S.dma("sp", hbc_t[:, l, :], hvec[l:l + 1].rearrange("o a b -> o (a b)").to_broadcast([64, 16]), w=["hbc"], chan="ld8")
        S.dma("sp", hrow_t[:], hvec.rearrange("l a h -> h l a"), w=["hrow"], chan="ld9", nc_ok=True)
        for l in range(L):
            S.dma("sp", lb_t[:, :, l], lbl[l].rearrange("(h d) -> d h", h=4), w=["lb"], chan="ld10", nc_ok=True)
        MSET("pool", cst[:, 0:1], 1.0, ["cst"])
        MSET("pool", cst[:, 1:2], 1e-6, ["cst"])
        MSET("pool", cst[:, 2:3], 1e-5, ["cst"])
        MSET("pool", odiv[:], 1.0 / 1024.0, ["odiv"])
        CP("dve", c64b[:], c64[:], ["c64"], ["c64b"])
        CP("dve", c128b[:], c128[:], ["c128"], ["c128b"])
        ACT(negA_t[:], hbc_t[:, :, 0:4], AF.Exp, ["hbc"], ["negA"])
        TS("dve", negA_t[:], negA_t[:], -1.0, ALU.mult, ["negA"], ["negA"])
        TS("dve", nbf_t[:], hrow_t[:, :, 3:4], -1.0, ALU.mult, ["hrow"], ["nbf"])
        ACT(lbtmp[:], lb_t[:], AF.Exp, ["lb"], ["lbtmp"])
        S.add("dve", lambda e: e.tensor_reduce(out=lbsum[:], in_=lbtmp[:], axis=AX.X, op=ALU.add), r=["lbtmp"], w=["lbsum"])
        S.add("dve", lambda e: e.reciprocal(out=lbsum[:], in_=lbsum[:]), r=["lbsum"], w=["lbsum"])
        TT("dve", lbtmp[:], lbtmp[:], bc_last(lbsum[:], L), ALU.mult, ["lbtmp", "lbsum"], ["lbtmp"])
        MSET("pool", lb_t[:, :, 0:1], 0.0, ["lb"])
        for l in range(1, L):
            if l == 1:
                CP("dve", lb_t[:, :, 1:2], lbtmp[:, :, 1:2], ["lbtmp", "lb"], ["lb"])
            else:
                TT("dve", lb_t[:, :, l:l + 1], lb_t[:, :, l - 1:l], lbtmp[:, :, l:l + 1], ALU.add, ["lbtmp", "lb"], ["lb"])
        TS("dve", oml_t[:], lb_t[:], -1.0, ALU.mult, ["lb"], ["oml"], s2=1.0, op1=ALU.add)
        TS("dve", noml_t[:], oml_t[:], -1.0, ALU.mult, ["oml"], ["noml"])

        for l in range(L):
            ch = f"pc{l}"
            key = f"wbf{l}"
            for src, dst in ((wfa, wfa_bf), (wta, wta_bf), (wfb, wfb_bf), (wrb, wrb_bf), (wtb, wtb_bf)):
                S.dma("pool", dst[l], src[l].rearrange("p k c -> p (k c)"), w=[key], chan=ch)
            for ci in range(18):
                S.dma("pool", wdn_bf[l, ci], wdn[l, ci], w=[key], chan=ch)
        C_TRIU = c64[:, 0, :]
        C_NTRIU = c64[:, 1, :]
        C_ONES64 = c64[:, 5, :]
        C_ID64 = c64[:, 4, :]
        ID128 = c128[:, 0, :]
        ID128B = c128b[:, 0, :]
        AMASK = c128[:, 1, :]
        TRIINC_B = c128b[:, 2, :]
        BLKONES_B = c128b[:, 3, :]
        ones128b = sb("ones128b", (128, 128), BF16)
        MSET("pool", ones128b[:], 1.0, ["ones128b"])

        def phase0(sq):
            T = sq["T"]
            AR.reset()
            S.barrier(dummy[:])
            xtm = [AR.alloc((128, 1024)) for _ in range(2)]
            xfm = [AR.alloc((128, 8, 128)) for _ in range(2)]
            nt = (T + 127) // 128
            for ti in range(nt):
                n = min(128, T - ti * 128)
                a = ti % 2
                S.dma("sp", xtm[a][0:n, :], sq["x"][ti * 128:ti * 128 + n, :], w=[f"xtm{a}"], chan=f"p0l{a}")
                for b in range(8):
                    bank = (2 * a) + b // 4
                    TR(PB(bank, 0, 128, (b % 4) * 128, (b % 4) * 128 + n), xtm[a][0:n, b * 128:(b + 1) * 128], ID128[0:n, 0:n],
                       [f"xtm{a}", "c128"], [f"pb{bank}"])
                for hb in range(2):
                    bank = 2 * a + hb
                    src = pbank[bank][:].rearrange("p (b t) -> p b t", b=4)[:, :, 0:n]
                    CP("act" if hb == 0 else "dve", xfm[a][:, 4 * hb:4 * hb + 4, 0:n], src, [f"pb{bank}"], [f"xfm{a}"])
                S.dma("sp", xscr[0, :, :, ti * 128:ti * 128 + n], xfm[a][:, :, 0:n], r=[f"xfm{a}"], w=["xscr0"], chan=f"p0s{a}")

        def load_w(dst3, src2, l, key, chan):
            S.dma("sp", dst3.rearrange("p k c -> p (k c)"), src2, r=[f"wbf{l}"], w=[key], chan=chan)

        def passA(sq, l):
            T = sq["T"]
            past = sq["past"]
            ntok = min(128, T)
            nch = ntok // 64
            ntile = T // ntok
            AR.reset()
            S.barrier(dummy[:])
            Wf = AR.alloc((128, 8, 1280), BF16)
            Wt = AR.alloc((128, 8, 776), BF16)
            NK = past + T
            kTs = AR.alloc((64, 4, NK), BF16)
            nkt = (NK + 127) // 128
            Vs = AR.alloc((128, nkt, 256), BF16)
            xb = AR.alloc((128, 8, ntok), BF16)
            ext = AR.alloc((128, 6, ntok + 3))
            yv = AR.alloc((128, 6, ntok))
            ys = AR.alloc((128, 6, ntok))
            vb16 = AR.alloc((128, 2, ntok), BF16)
            sqb = AR.alloc((128, 4, ntok), BF16)
            rn = AR.alloc((128, 4, ntok))
            qT2 = [AR.alloc((64, 4, ntok), BF16) for _ in range(2)]
            kT2 = [AR.alloc((64, 4, ntok), BF16) for _ in range(2)]
            qsb2 = [AR.alloc((64, 4, ntok), BF16) for _ in range(2)]
            kvout2 = [AR.alloc((128, 512)) for _ in range(2)]
            gg2 = [AR.alloc((64, nch, 256)) for _ in range(2)]
            sc8 = AR.alloc((64, nch, 8))
            bet2 = [AR.alloc((64, nch, 4)) for _ in range(2)]
            tg = AR.alloc((64, nch, 4))
            gvec2 = [AR.alloc((64, nch, 4)) for _ in range(2)]
            kvtm2 = [AR.alloc((64, nch, 512), BF16) for _ in range(2)]
            arg = AR.alloc((64, nch, 12))
            ex = AR.alloc((64, nch, 12))
            bg = AR.alloc((64, nch, 4))
            vbt = AR.alloc((64, nch, 4, 64), BF16)
            kbg = AR.alloc((64, nch, 4, 64), BF16)
            kdec = AR.alloc((64, nch, 4, 64), BF16)
            kb = AR.alloc((64, nch, 4, 64), BF16)
            kbT = AR.alloc((64, nch, 4, 64), BF16)
            absD = AR.alloc((64, nch * 256))
            G = AR.alloc((64, nch, 4, 64))
            eB = AR.alloc((64, nch, 4, 64))
            qg = AR.alloc((64, nch, 4, 64), BF16)
            Gu = AR.alloc((64, nch, 4, 64))
            nGs = AR.alloc((64, nch, 4, 64))
            nGl = AR.alloc((64, nch, 4, 64))
            PP = AR.alloc((64, 2 * nch, 4, 64)).rearrange("p (a c) h t -> p a c h t", a=2)
            ATb = AR.alloc((64, nch, 4, 64), BF16)
            TTm = AR.alloc((64, nch, 4, 64))
            TTb = AR.alloc((64, nch, 4, 64), BF16)
            Uf = AR.alloc((64, nch, 4, 64))
            WTb = AR.alloc((64, nch, 4, 64), BF16)
            vnew = AR.alloc((64, 4, 64), BF16)
            Sg = AR.alloc((64, 4, 64))
            Sgb = AR.alloc((64, 4, 64), BF16)
            O = AR.alloc((64, nch, 8, 64))
            Osq = AR.alloc((64, nch, 8, 64))
            ssq = AR.alloc((64, nch * 8))
            sgate = AR.alloc((64, nch, 256))
            Ob = AR.alloc((64, nch, 512), BF16)
            zs2 = [AR.alloc((128, 4, ntok)) for _ in range(2)]
            et = AR.alloc((128, 4, ntok))
            spb2 = [AR.alloc((128, 4, ntok), BF16) for _ in range(2)]
            Rb2 = [AR.alloc((128, 4, ntok), BF16) for _ in range(2)]
            tf = AR.alloc((128, 4, ntok))
            af = AR.alloc((128, 4, ntok))
            ab2 = [AR.alloc((128, 4, ntok), BF16) for _ in range(2)]
            if sq["sample"]:
                ktm = [AR.alloc((128, 256)) for _ in range(2)]

            load_w(Wf, wfa_bf[l], l, "Wf", "wA0")
            load_w(Wt, wta_bf[l], l, "Wt", "wA1")
            cw = convw_t[:, l, :].rearrange("p (b j) -> p b j", b=6)
            if sq["sample"]:
                S.dma("sp", Sg[:], gS_in[l].rearrange("h k v -> k h v"), w=["Sg"], chan="stA")
                for b in range(6):
                    S.dma("sp", ext[:, b, 0:3], sconv[l, :, b * 128:(b + 1) * 128].rearrange("r p -> p r"), w=["ext"], chan="stB", nc_ok=True)
                S.dma("pool", Vs[:, 0:PAST // 128, :], cv[l].rearrange("(t p) c -> p t c", p=128), w=[f"Vs{kt_}" for kt_ in range(PAST // 128)], chan="stC")
                for kt in range(PAST // 128):
                    a = kt % 2
                    S.dma("sp", ktm[a][:], ck[l, kt * 128:(kt + 1) * 128, :], w=[f"ktm{a}"], chan=f"stK{a}")
                    for h in range(4):
                        TR(PB(2 + a, 0, 64, h * 128, (h + 1) * 128), ktm[a][:, h * 64:(h + 1) * 64], ID128, [f"ktm{a}", "c128"], [f"pb{2 + a}"])
                    CP("act" if a == 0 else "dve", kTs[:, :, kt * 128:(kt + 1) * 128],
                       pbank[2 + a][0:64, :].rearrange("p (h t) -> p h t", h=4), [f"pb{2 + a}"], [f"kTs{kt}"])
            else:
                MSET("pool", Sg[:], 0.0, ["Sg"])
                MSET("pool", ext[:, :, 0:3], 0.0, ["ext"])
            CP("dve", Sgb[:], Sg[:], ["Sg"], ["Sgb"])

            stop_at("A0a")
            def pre_stream(ti):
                p = ti % 2
                pos0 = ti * ntok
                kt_new = (past + pos0) // 128
                pr0 = (past + pos0) % 128
                pos0 = ti * ntok
                S.dma("pool", xb[:], xscr[l % 2, :, :, pos0:pos0 + ntok], r=[f"xscr{l % 2}"], w=["xb"], chan="xbA")
                stop_at("A0b")
                nblk_bank = 512 // ntok
                groups = [(0, 4), (4, 8), (8, 10)]
                for gi, (b0, b1) in enumerate(groups):
                    bank = gi % 2
                    for b in range(b0, b1):
                        for k in range(8):
                            MM(PB(bank, 0, 128, (b - b0) * ntok, (b - b0 + 1) * ntok), Wf[:, k, b * 128:(b + 1) * 128], xb[:, k, :],
                               ["Wf", "xb"], [f"pb{bank}"], start=(k == 0), stop=(k == 7))
                    pv = pbank[bank][:, 0:(b1 - b0) * ntok].rearrange("p (b t) -> p b t", t=ntok)
                    import os
                    if os.environ.get("KDBG", "") == "noevac":
                        continue
                    if os.environ.get("KDBG", "") == f"only{gi}":
                        stop_at("A0c")
                    if gi == 0:
                        CP("act", ext[:, 0:4, 3:3 + ntok], pv, [f"pb{bank}"], ["ext"])
                    elif gi == 1:
                        dbg = os.environ.get("KDBG2", "abc")
                        if "a" in dbg:
                            CP("act", ext[:, 4:6, 3:3 + ntok], pv[:, 0:2, :], [f"pb{bank}"], ["ext"])
                        q4 = qsb2[p][:].rearrange("p (b h) t -> p b h t", b=2)
                        if "b" in dbg:
                            TS("dve", q4[:, :, 0, :], pv[0:64, 2:4, :], 0.125, ALU.mult, [f"pb{bank}"], [f"qsb{p}"])
                        if "c" in dbg:
                            TS("dve", q4[:, :, 1, :], pv[64:128, 2:4, :], 0.125, ALU.mult, [f"pb{bank}"], [f"qsb{p}"])
                    else:
                        k4 = kTs[:, :, past + pos0:past + pos0 + ntok].rearrange("p (b h) t -> p b h t", b=2)
                        CP("act", k4[:, :, 0, :], pv[0:64, 0:2, :], [f"pb{bank}"], [f"kTs{kt_new}a"])
                        CP("dve", k4[:, :, 1, :], pv[64:128, 0:2, :], [f"pb{bank}"], [f"kTs{kt_new}b"])
                stop_at("A0c")
                for k in range(8):
                    MM(PB(1, 0, ntok, 0, 512), xb[:, k, :], Wt[:, k, 0:512], ["Wt", "xb"], ["pb1"], start=(k == 0), stop=(k == 7))
                CP("act", kvout2[p][0:ntok, :], PB(1, 0, ntok, 0, 512), ["pb1"], [f"kvout{p}"])
                kt_new = (past + pos0) // 128
                pr0 = (past + pos0) % 128
                CP("dve", Vs[pr0:pr0 + ntok, kt_new, :], PB(1, 0, ntok, 256, 512), ["pb1"], [f"Vs{kt_new}"])
                S.dma("sp", sq["k"](l)[pos0:pos0 + ntok, :], kvout2[p][0:ntok, 0:256], r=[f"kvout{p}"], chan="okA", final=True)
                S.dma("sp", sq["v"](l)[pos0:pos0 + ntok, :], kvout2[p][0:ntok, 256:512], r=[f"kvout{p}"], chan="ovA", final=True)
                for k in range(8):
                    MM(PB(0, 0, ntok, 0, 264), xb[:, k, :], Wt[:, k, 512:776], ["Wt", "xb"], ["pb0"], start=(k == 0), stop=(k == 7))
                for c in range(nch):
                    CP("act", gg2[p][:, c, :], PB(0, 64 * c, 64 * c + 64, 0, 256), ["pb0"], [f"gg{p}"])
                    CP("dve", sc8[:, c, :], PB(0, 64 * c, 64 * c + 64, 256, 264), ["pb0"], ["sc8"])
                    yield
                stop_at("A1")
                for b in range(6):
                    TS("dve", yv[:, b, :], ext[:, b, 0:ntok], cw[:, b, 0:1], ALU.mult, ["ext", "convw"], [f"yv{b}"])
                    for j in range(1, 4):
                        STT(yv[:, b, :], ext[:, b, j:j + ntok], cw[:, b, j:j + 1], yv[:, b, :], ALU.mult, ALU.add,
                            ["ext", "convw", f"yv{b}"], [f"yv{b}"])
                if ti == ntile - 1:
                    for b in range(6):
                        S.dma("sp", sq["conv"](l)[:, b * 128:(b + 1) * 128].rearrange("r p -> p r"), ext[:, b, ntok:ntok + 3], r=["ext"],
                              chan="ocA", final=True, nc_ok=True)
                else:
                    CP("pool", ext[:, :, 0:3], ext[:, :, ntok:ntok + 3], ["ext"] + [f"yv{b}" for b in range(6)], ["ext"])
                    yield
                ACT(ys[:], yv[:], AF.Silu, [f"yv{b}" for b in range(6)], ["ys"])
                CP("pool", vb16[:], ys[:, 4:6, :], ["ys"], ["vb16"])
                ACT(sqb[:], ys[:, 0:4, :], AF.Square, ["ys"], ["sqb"])
                yield
                for b in range(4):
                    MM(PB(1, 0, 128, b * ntok, (b + 1) * ntok), BLKONES_B, sqb[:, b, :], ["sqb", "c128b"], ["pb1"])
                ACT(rn[:], pbank[1][:, 0:4 * ntok].rearrange("p (b t) -> p b t", b=4), AF.Sqrt, ["pb1"], ["rn"], bias=EPS6, scale=1.0)
                S.add("dve", lambda e: e.reciprocal(out=rn[:], in_=rn[:]), r=["rn"], w=["rn"])
                yield
                q4 = qT2[p][:].rearrange("p (b h) t -> p b h t", b=2)
                k4 = kT2[p][:].rearrange("p (b h) t -> p b h t", b=2)
                for hf in range(2):
                    STT(q4[:, :, hf, :], ys[64 * hf:64 * hf + 64, 0:2, :], 0.125, rn[64 * hf:64 * hf + 64, 0:2, :], ALU.mult, ALU.mult,
                        ["ys", "rn"], [f"qT{p}"])
                    TT("dve", k4[:, :, hf, :], ys[64 * hf:64 * hf + 64, 2:4, :], rn[64 * hf:64 * hf + 64, 2:4, :], ALU.mult, ["ys", "rn"], [f"kT{p}"])
                ACT(bet2[p][:], sc8[:, :, 0:4], AF.Sigmoid, ["sc8"], [f"bet{p}"])
                TT("dve", tg[:], sc8[:, :, 4:8], bc_mid(hbc_t[:, l, 4:8], nch), ALU.add, ["sc8", "hbc"], ["tg"])
                ACT(tg[:], tg[:], AF.Exp, ["tg"], ["tg"])
                ACT(tg[:], tg[:], AF.Ln, ["tg", "cst"], ["tg"], bias=ONE[0:64], scale=1.0)
                TT("dve", gvec2[p][:], tg[:], bc_mid(negA_t[:, l, :], nch), ALU.mult, ["tg", "negA"], [f"gvec{p}"])
                yield
                for c in range(nch):
                    cs = slice(64 * c, 64 * c + 64)
                    pT = PB16(3, 0, 64, 0, 256)
                    for h in range(4):
                        TR(pT[:, h * 64:(h + 1) * 64], kT2[p][:, h, cs], c64b[:, 4, :], [f"kT{p}", "c64b"], ["pb3a"])
                    for bv in range(2):
                        TR(pT[:, 256 + bv * 128:256 + (bv + 1) * 128], vb16[:, bv, cs], ID128B, ["vb16", "c128b"], ["pb3a"])
                    CP("act", kvtm2[p][:, c, :], pT, ["pb3a"], [f"kvtm{p}_{c}"])
                    yield
                yield
            def gdn_stream(ti):
                p = ti % 2
                NH_ = nch * 4
                W_ = NH_ * 64
                fl = lambda ap: ap.rearrange("p c h t -> p (c h t)")
                kv = kvtm2[p]
                ktm4 = kv[:, :, 0:256].rearrange("p c (h d) -> p c h d", h=4)
                vtm4 = kv[:, :, 256:512].rearrange("p c (h d) -> p c h d", h=4)
                gv = gvec2[p]
                for c in range(nch):
                    MM(PB(4, 0, 64, 8 * c, 8 * c + 4), C_TRIU, gv[:, c, :], ["c64", f"gvec{p}"], ["pb4"])
                    MM(PB(4, 0, 64, 8 * c + 4, 8 * c + 8), C_ONES64, gv[:, c, :], ["c64", f"gvec{p}"], ["pb4"])
                CP("act", arg[:, :, 0:8], pbank[4][0:64, 0:8 * nch].rearrange("p (c e) -> p c e", e=8), ["pb4"], ["arg"])
                TT("dve", arg[:, :, 8:12], arg[:, :, 4:8], arg[:, :, 0:4], ALU.subtract, ["arg"], ["arg"])
                ACT(ex[:], arg[:], AF.Exp, ["arg"], ["ex"])
                yield
                TT("dve", bg[:], bet2[p][:], ex[:, :, 0:4], ALU.mult, [f"bet{p}", "ex"], ["bg"])
                b3 = lambda a3: a3.unsqueeze(3).to_broadcast([64, nch, 4, 64])
                TT("pool", vbt[:], vtm4, b3(bet2[p][:]), ALU.mult, [f"kvtm{p}_{c_}" for c_ in range(nch)] + [f"bet{p}"], ["vbt"])
                TT("pool", kb[:], ktm4, b3(bet2[p][:]), ALU.mult, [f"kvtm{p}_{c_}" for c_ in range(nch)] + [f"bet{p}"], ["kb"])
                TT("pool", kbg[:], ktm4, b3(bg[:]), ALU.mult, [f"kvtm{p}_{c_}" for c_ in range(nch)] + ["bg"], ["kbg"])
                TT("pool", kdec[:], ktm4, b3(ex[:, :, 8:12]), ALU.mult, [f"kvtm{p}_{c_}" for c_ in range(nch)] + ["ex"], ["kdec"])
                yield
                pT2 = PB16(3, 0, 64, 256, 256 + W_ // 2)
                for c in range(nch):
                    for h in range(4):
                        j = c * 4 + h
                        TR(pT2[:, j * 64:(j + 1) * 64], kb[:, c, h, :], c64b[:, 4, :], ["kb", "c64b"], ["pb3b"])
                CP("dve", fl(kbT[:]), pT2, ["pb3b"], ["kbT"])
                yield
                for c in range(nch):
                    for h in range(4):
                        j = c * 4 + h
                        gbc = gv[:, c, h:h + 1].to_broadcast([64, 64])
                        MM(PB(4, 0, 64, j * 64, (j + 1) * 64), gbc, C_TRIU, [f"gvec{p}", "c64"], ["pb4"])
                        MM(PB(5, 0, 64, j * 64, (j + 1) * 64), gbc, C_TRIU, [f"gvec{p}", "c64"], ["pb5"], start=True, stop=False)
                        MM(PB(5, 0, 64, j * 64, (j + 1) * 64), C_NTRIU, gbc, [f"gvec{p}", "c64"], ["pb5"], start=False, stop=True)
                ACT(absD[:], PB(5, 0, 64, 0, W_), AF.Abs, ["pb5"], ["absD"])
                yield
                ACT(fl(G[:]), absD[:], AF.Exp, ["absD"], ["G"], scale=-1.0)
                ACT(fl(eB[:]), PB(4, 0, 64, 0, W_), AF.Exp, ["pb4"], ["eB"])
                qTv = qT2[p][:].rearrange("p h (c t) -> p c h t", t=64)
                TT("dve", qg[:], qTv, eB[:], ALU.mult, [f"qT{p}", "eB"], ["qg"])
                yield
                G3 = G[:].rearrange("p c h t -> p (c h) t")
                TT("dve", Gu[:].rearrange("p c h t -> p (c h) t"), G3, bc_mid(C_TRIU, NH_), ALU.mult, ["G", "c64"], ["Gu"])
                TT("dve", nGs[:].rearrange("p c h t -> p (c h) t"), G3, bc_mid(c64[:, 2, :], NH_), ALU.mult, ["G", "c64"], ["nGs"])
                TT("dve", nGl[:].rearrange("p c h t -> p (c h) t"), G3, bc_mid(c64[:, 3, :], NH_), ALU.mult, ["G", "c64"], ["nGl"])
                yield
                for c in range(nch):
                    cs = slice(64 * c, 64 * c + 64)
                    for h in range(4):
                        j = c * 4 + h
                        MM(PB(6, 0, 64, j * 64, (j + 1) * 64), kT2[p][:, h, cs], kbT[:, c, h, :], [f"kT{p}", "kbT"], ["pb6"])
                        MM(PB(7, 0, 64, j * 64, (j + 1) * 64), kbT[:, c, h, :], kT2[p][:, h, cs], [f"kT{p}", "kbT"], ["pb7"])
                        MM(PB(4, 0, 64, j * 64, (j + 1) * 64), kT2[p][:, h, cs], qT2[p][:, h, cs], [f"kT{p}", f"qT{p}"], ["pb4"])
                TT("dve", fl(PP[:, 1]), PB(6, 0, 64, 0, W_), fl(nGs[:]), ALU.mult, ["pb6", "nGs"], ["PP1"])
                TT("dve", fl(PP[:, 0]), PB(7, 0, 64, 0, W_), fl(nGl[:]), ALU.mult, ["pb7", "nGl"], ["PP0"])
                TT("dve", fl(ATb[:]), PB(4, 0, 64, 0, W_), fl(Gu[:]), ALU.mult, ["pb4", "Gu"], ["ATb"])
                yield
                TT("dve", TTm[:].rearrange("p c h t -> p (c h) t"), PP[:, 1].rearrange("p c h t -> p (c h) t"), bc_mid(C_ID64, NH_), ALU.add,
                   ["PP1", "c64"], ["TTm"])
                yield
                for lev in range(1, 6):
                    for c in range(nch):
                        for h in range(4):
                            j = c * 4 + h
                            MM(PB(5, 0, 64, j * 64, (j + 1) * 64), PP[:, 1, c, h, :], PP[:, 0, c, h, :], ["PP0", "PP1"], ["pb5"])
                            if lev < 5:
                                MM(PB(6, 0, 64, j * 64, (j + 1) * 64), PP[:, 0, c, h, :], PP[:, 1, c, h, :], ["PP0", "PP1"], ["pb6"])
                    CP("act", fl(PP[:, 0]), PB(5, 0, 64, 0, W_), ["pb5"], ["PP0"])
                    if lev < 5:
                        CP("dve", fl(PP[:, 1]), PB(6, 0, 64, 0, W_), ["pb6"], ["PP1"])
                    yield
                    for c in range(nch):
                        for h in range(4):
                            j = c * 4 + h
                            MM(PB(7, 0, 64, j * 64, (j + 1) * 64), PP[:, 0, c, h, :], TTm[:, c, h, :], ["PP0", "TTm"], ["pb7"])
                    TT("dve", fl(TTm[:]), fl(TTm[:]), PB(7, 0, 64, 0, W_), ALU.add, ["TTm", "pb7"], ["TTm"])
                    yield
                CP("act", TTb[:], TTm[:], ["TTm"], ["TTb"])
                yield
                for c in range(nch):
                    for h in range(4):
                        j = c * 4 + h
                        MM(PB(4, 0, 64, j * 64, (j + 1) * 64), TTb[:, c, h, :], vbt[:, c, h, :], ["TTb", "vbt"], ["pb4"])
                        MM(PB(5, 0, 64, j * 64, (j + 1) * 64), kbg[:, c, h, :], TTb[:, c, h, :], ["TTb", "kbg"], ["pb5"])
                CP("act", fl(Uf[:]), PB(4, 0, 64, 0, W_), ["pb4"], ["Uf"])
                CP("dve", fl(WTb[:]), PB(5, 0, 64, 0, W_), ["pb5"], ["WTb"])
                yield
                f3 = lambda ap: ap.rearrange("p h t -> p (h t)")
                for c in range(nch):
                    for h in range(4):
                        MM(PB(6, 0, 64, h * 64, (h + 1) * 64), WTb[:, c, h, :], Sgb[:, h, :], ["WTb", "Sgb"], ["pb6"])
                    TT("dve", f3(vnew[:]), f3(Uf[:, c]), PB(6, 0, 64, 0, 256), ALU.subtract, ["Uf", "pb6"], ["vnew"])
                    yield
                    for h in range(4):
                        MM(PB(7, 0, 64, h * 64, (h + 1) * 64), ATb[:, c, h, :], vnew[:, h, :], ["ATb", "vnew"], ["pb7"], start=True, stop=False)
                        MM(PB(7, 0, 64, h * 64, (h + 1) * 64), qg[:, c, h, :], Sgb[:, h, :], ["qg", "Sgb"], ["pb7"], start=False, stop=True)
                    for h in range(4):
                        MM(PB(6, 0, 64, 256 + h * 64, 256 + (h + 1) * 64), kdec[:, c, h, :], vnew[:, h, :], ["kdec", "vnew"], ["pb6"])
                    CP("act", f3(O[:, c, 0:4, :]), PB(7, 0, 64, 0, 256), ["pb7"], ["O"])
                    TT("dve", Sg[:], Sg[:], bc_last(ex[:, c, 4:8], 64), ALU.mult, ["Sg", "ex"], ["Sg"])
                    TT("dve", f3(Sg[:]), f3(Sg[:]), PB(6, 0, 64, 256, 512), ALU.add, ["Sg", "pb6"], ["Sg"])
                    CP("act", Sgb[:], Sg[:], ["Sg"], ["Sgb"])
                    yield
            def attn_stream(ti):
                p = ti % 2
                pos0 = ti * ntok
                kt_new = (past + pos0) // 128
                pr0 = (past + pos0) % 128
                kts = [(kt_new, ntok, True)] + [(kt, 128, False) for kt in range(kt_new - 1, -1, -1)]
                nkts = len(kts)
                W4 = 4 * ntok

                def stage1(i):
                    kt, nk, diag = kts[i]
                    a = i % 2
                    for h in range(4):
                        MM(PB(0, 0, nk, h * ntok, (h + 1) * ntok), kTs[:, h, kt * 128:kt * 128 + nk], qsb2[p][:, h, :], [f"kTs{kt}", f"kTs{kt}a", f"kTs{kt}b", f"qsb{p}"], ["pb0"])
                    pz = pbank[0][0:nk, 0:W4].rearrange("p (h t) -> p h t", h=4)
                    CP("dve", zs2[a][0:nk], pz, ["pb0"], [f"zs{a}"])
                    ACT(et[0:nk], pz, AF.Exp, ["pb0"], ["et"])
                    ACT(spb2[a][0:nk], et[0:nk], AF.Ln, ["et", "cst"], [f"spb{a}"], bias=ONE[0:nk], scale=1.0)
                    if diag:
                        TT("pool", spb2[a][0:nk], spb2[a][0:nk], bc_mid(AMASK[0:nk, 0:ntok], 4), ALU.mult, [f"spb{a}", "c128"], [f"spb{a}"])

                def stage2(i):
                    kt, nk, diag = kts[i]
                    a = i % 2
                    first = i == 0
                    last = i == nkts - 1
                    pa = pbank[1][0:nk, 0:W4]
                    MM(pa, TRIINC_B[0:nk, 0:nk], spb2[a][0:nk].rearrange("p h t -> p (h t)"), [f"spb{a}", "c128b"], ["pb1"], start=True, stop=first)
                    if not first:
                        MM(pa, ones128b[:, 0:nk], Rb2[a][:].rearrange("p h t -> p (h t)"), [f"Rb{a}", "ones128b"], ["pb1"], start=False, stop=True)
                    TT("dve", tf[0:nk].rearrange("p h t -> p (h t)"), zs2[a][0:nk].rearrange("p h t -> p (h t)"), pa, ALU.subtract,
                       [f"zs{a}", "pb1"], ["tf"])
                    if diag:
                        ACT(af[0:nk], tf[0:nk], AF.Exp, ["tf"], ["af"])
                        TT("pool", ab2[a][0:nk], af[0:nk], bc_mid(AMASK[0:nk, 0:ntok], 4), ALU.mult, ["af", "c128"], [f"ab{a}"])
                    else:
                        ACT(ab2[a][0:nk], tf[0:nk], AF.Exp, ["tf"], [f"ab{a}"])
                    if not last:
                        nb_ = (i + 1) % 2
                        if first:
                            if nk < 128:
                                MSET("pool", Rb2[nb_][:], 0.0, [f"Rb{nb_}"])
                            CP("dve", Rb2[nb_][0:nk], spb2[a][0:nk], [f"spb{a}", f"Rb{nb_}"], [f"Rb{nb_}"])
                        else:
                            TT("dve", Rb2[nb_][:], Rb2[a][:], spb2[a][:], ALU.add, [f"spb{a}", f"Rb{a}"], [f"Rb{nb_}"])
                    for h in range(4):
                        S.add("pe", lambda e, h=h, kt=kt, nk=nk, a=a, st_=(first and h == 0), sp_=last: e.matmul(
                            PB(2, 0, ntok, h * 64, (h + 1) * 64), lhsT=ab2[a][0:nk, h, :], rhs=Vs[0:nk, kt, h * 64:(h + 1) * 64],
                            start=st_, stop=sp_, skip_group_check=True), r=[f"ab{a}", f"Vs{kt}"], w=["pb2"])

                stage1(0)
                yield
                for i in range(nkts):
                    if i + 1 < nkts:
                        stage1(i + 1)
                        yield
                    stage2(i)
                    yield
                for c in range(nch):
                    CP("act", O[:, c, 4:8, :].rearrange("p h t -> p (h t)"), PB(2, 64 * c, 64 * c + 64, 0, 256), ["pb2"], ["O"])
            def merge_tile(ti):
                p = ti % 2
                pos0 = ti * ntok
                kt_new = (past + pos0) // 128
                pr0 = (past + pos0) % 128
                merge(O, Osq, ssq, Ob, nch, l, 0, pos0,
                      gate0=("silu", gg2[p], sgate, f"gg{p}", 0), gate1=None)
            for _ in pre_stream(0):
                pass
            for ti in range(ntile):
                interleave(gdn_stream(ti), attn_stream(ti))
                merge_tile(ti)
                if ti + 1 < ntile:
                    for _ in pre_stream(ti + 1):
                        pass
            S.dma("sp", sq["gdn"](l).rearrange("h k v -> k h v"), Sg[:], r=["Sg"], chan="ogA", final=True)

        def merge(O, Osq, ssq, Ob, nch, l, blk0, pos0, gate0, gate1):
            Of = O[:].rearrange("p c h d -> p (c h d)")
            ACT(Osq[:].rearrange("p c h d -> p (c h d)"), Of, AF.Square, ["O"], ["Osq"])
            S.add("dve", lambda e: e.tensor_reduce(out=ssq[:], in_=Osq[:].rearrange("p c h d -> p (c h) d"), axis=AX.X, op=ALU.add),
                  r=["Osq"], w=["ssq"])
            TS("dve", ssq[:], ssq[:], 1.0 / 64.0, ALU.mult, ["ssq"], ["ssq"], s2=1e-5, op1=ALU.add)
            ACT(ssq[:], ssq[:], AF.Sqrt, ["ssq"], ["ssq"])
            S.add("dve", lambda e: e.reciprocal(out=ssq[:], in_=ssq[:]), r=["ssq"], w=["ssq"])
            O3 = O[:].rearrange("p c h d -> p (c h) d")
            TT("dve", O3, O3, bc_last(ssq[:], 64), ALU.mult, ["O", "ssq"], ["O"])
            O2 = O[:].rearrange("p c h d -> p c (h d)")
            TT("pool", O2, O2, bc_mid(outg_t[:, l, blk0 * 128:blk0 * 128 + 512], nch), ALU.mult, ["O", "outg"], ["O"])
            for gi, gate in enumerate((gate0, gate1)):
                sl = slice(256 * gi, 256 * gi + 256)
                if gate is None:
                    CP("pool", Ob[:, :, sl], O2[:, :, sl], ["O"], [f"Ob{gi}"])
                else:
                    kind, gsrc, gdst, gkey, goff = gate
                    if kind == "silu":
                        ACT(gdst[:], gsrc[:, :, goff:goff + 256], AF.Silu, [gkey], ["sgate"])
                    else:
                        ACT(gdst[:], gsrc[:, :, goff:goff + 256], AF.Sigmoid, [gkey], ["sgate"])
                    TT("dve", Ob[:, :, sl], O2[:, :, sl], gdst[:], ALU.mult, ["O", "sgate"], [f"Ob{gi}"])
            for c in range(nch):
                pT = PB16(3, 0, 128, 0, 128)
                for b in range(4):
                    TR(pT[:, b * 64:(b + 1) * 64], Ob[:, c, b * 128:(b + 1) * 128], c64b[:, 4, :], ["Ob0", "Ob1", "c64b"], ["pb3a"])
                CP("act", oT[:, blk0:blk0 + 4, pos0 + 64 * c:pos0 + 64 * c + 64], pT.rearrange("p (b t) -> p b t", b=4), ["pb3a"], ["oT"])

        def passB(sq, l):
            T = sq["T"]
            ntok = min(128, T)
            nch = ntok // 64
            ntile = T // ntok
            AR.reset()
            S.barrier(dummy[:])
            Wf = AR.alloc((128, 8, 1024), BF16)
            Wr = AR.alloc((128, 8, 8), BF16)
            Wt = AR.alloc((128, 8, 1280), BF16)
            xb = AR.alloc((128, 8, ntok), BF16)
            mqT = AR.alloc((64, 4, ntok), BF16)
            mkT = AR.alloc((64, 4, ntok), BF16)
            hqf = AR.alloc((64, 2, 4, ntok))
            ri = AR.alloc((4, ntok))
            rf = AR.alloc((4, ntok))
            cl = AR.alloc((4, ntok))
            wv = AR.alloc((4, ntok))
            wsr = AR.alloc((4, ntok))
            gfr = AR.alloc((4, ntok))
            wmax = AR.alloc((4, nch))
            blast = AR.alloc((4, nch))
            mcur = AR.alloc((4, nch))
            Mc = AR.alloc((4, nch))
            mprev = AR.alloc((4, nch))
            decr = AR.alloc((4, nch))
            decd = AR.alloc((4, nch, 4))
            mstate = AR.alloc((4, 1))
            wsg = AR.alloc((64, nch, 8))
            decbc = AR.alloc((64, nch, 4))
            mktm = AR.alloc((64, nch, 256), BF16)
            V1 = AR.alloc((64, nch, 4, 65), BF16)
            V2 = AR.alloc((64, 4, 65), BF16)
            hgv = AR.alloc((64, nch, 256), BF16)
            gmo = AR.alloc((64, nch, 256))
            ghg = AR.alloc((64, nch, 256))
            sgo = AR.alloc((64, nch, 256))
            STb = AR.alloc((64, 4, 64), BF16)
            Cm = AR.alloc((64, 4, 65))
            Cmb = AR.alloc((64, 4, 65), BF16)
            aden = AR.alloc((64, 4))
            rec = AR.alloc((64, 4))
            hC = AR.alloc((64, 4, 64))
            qsil = AR.alloc((64, 4, ntok))
            sig = AR.alloc((64, 4, ntok))
            logf = AR.alloc((64, 4, ntok))
            kk = AR.alloc((64, 4, ntok))
            lcum = AR.alloc((64, 4, ntok))
            t1 = AR.alloc((64, 4, ntok))
            t2 = AR.alloc((64, 4, ntok))
            qs_ = AR.alloc((64, 4, ntok), BF16)
            qe_ = AR.alloc((64, 4, ntok), BF16)
            ks_ = AR.alloc((64, 4, ntok), BF16)
            kdT = AR.alloc((64, 4, ntok), BF16)
            kdtm = AR.alloc((64, nch, 4, 64), BF16)
            el = AR.alloc((64, 4, nch))
            aTb = AR.alloc((64, 4, 64), BF16)
            Sh = AR.alloc((64, 4, 64))
            Shb = AR.alloc((64, 4, 64), BF16)
            O = AR.alloc((64, nch, 8, 64))
            Osq = AR.alloc((64, nch, 8, 64))
            ssq = AR.alloc((64, nch * 8))
            sgate = AR.alloc((64, nch, 256))
            Ob = AR.alloc((64, nch, 512), BF16)

            load_w(Wf, wfb_bf[l], l, "Wf", "wB0")
            load_w(Wr, wrb_bf[l], l, "Wr", "wB2")
            load_w(Wt, wtb_bf[l], l, "Wt", "wB1")
            if sq["sample"]:
                S.dma("sp", Cm[:, :, 0:64], mC_in[l].rearrange("h k v -> k h v"), w=["Cm"], chan="stA")
                S.dma("sp", Cm[:, :, 64:65], mn_in[l].rearrange("h (k o) -> k h o", o=1), w=["Cm"], chan="stB", nc_ok=True)
                S.dma("sp", mstate[:], mm_in[l].rearrange("(h o) -> h o", o=1), w=["mstate"], chan="stC", nc_ok=True)
                S.dma("sp", Sh[:], hS_in[l].rearrange("h k v -> k h v"), w=["Sh"], chan="stD")
            else:
                MSET("pool", Cm[:], 0.0, ["Cm"])
                MSET("pool", mstate[:], 0.0, ["mstate"])
                MSET("pool", Sh[:], 0.0, ["Sh"])
            CP("dve", Shb[:], Sh[:], ["Sh"], ["Shb"])
            MSET("pool", V1[:], 1.0, ["V1"])
            bi_col = hrow_t[:, l, 2:3]
            nbf_col = nbf_t[:, l, :]

            for ti in range(ntile):
                pos0 = ti * ntok
                S.dma("pool", xb[:], xscr[l % 2, :, :, pos0:pos0 + ntok], r=[f"xscr{l % 2}"], w=["xb"], chan="xbA")
                for gi in range(2):
                    bank = gi
                    for bb in range(4):
                        b = gi * 4 + bb
                        for k in range(8):
                            MM(PB(bank, 0, 128, bb * ntok, (bb + 1) * ntok), Wf[:, k, b * 128:(b + 1) * 128], xb[:, k, :],
                               ["Wf", "xb"], [f"pb{bank}"], start=(k == 0), stop=(k == 7))
                    pv = pbank[bank][:, 0:4 * ntok].rearrange("p (b t) -> p b t", t=ntok)
                    if gi == 0:
                        q4 = mqT[:].rearrange("p (b h) t -> p b h t", b=2)
                        k4 = mkT[:].rearrange("p (b h) t -> p b h t", b=2)
                        for hf in range(2):
                            CP("act", q4[:, :, hf, :], pv[64 * hf:64 * hf + 64, 0:2, :], ["pb0"], ["mqT"])
                            TS("dve", k4[:, :, hf, :], pv[64 * hf:64 * hf + 64, 2:4, :], 0.125, ALU.mult, ["pb0"], ["mkT"])
                    else:
                        for a in range(2):
                            d4 = hqf[:, a].rearrange("p (b h) t -> p b h t", b=2)
                            for hf in range(2):
                                CP("act" if hf == 0 else "dve", d4[:, :, hf, :], pv[64 * hf:64 * hf + 64, 2 * a:2 * a + 2, :], ["pb1"], ["hqf"])
                for a in range(2):
                    for k in range(8):
                        MM(PB(2, 0, 4, a * ntok, (a + 1) * ntok), Wr[:, k, 4 * a:4 * a + 4], xb[:, k, :], ["Wr", "xb"], ["pb2"],
                           start=(k == 0), stop=(k == 7))
                CP("act", ri[:], PB(2, 0, 4, 0, ntok), ["pb2"], ["ri"])
                ACT(rf[:], PB(2, 0, 4, ntok, 2 * ntok), AF.Exp, ["pb2", "nbf"], ["rf"], bias=nbf_col, scale=-1.0)
                ACT(rf[:], rf[:], AF.Ln, ["rf", "cst"], ["rf"], bias=ONE[0:4], scale=1.0)
                for k in range(8):
                    MM(PB(0, 0, ntok, 0, 512), xb[:, k, :], Wt[:, k, 0:512], ["Wt", "xb"], ["pb0"], start=(k == 0), stop=(k == 7))
                for c in range(nch):
                    TS("dve", mktm[:, c, :], PB(0, 64 * c, 64 * c + 64, 0, 256), 0.125, ALU.mult, ["pb0"], ["mktm"])
                    CP("act", V1[:, c, :, 0:64], PB(0, 64 * c, 64 * c + 64, 256, 512).rearrange("p (h d) -> p h d", h=4), ["pb0"], ["V1"])
                for k in range(8):
                    MM(PB(1, 0, ntok, 0, 512), xb[:, k, :], Wt[:, k, 512:1024], ["Wt", "xb"], ["pb1"], start=(k == 0), stop=(k == 7))
                for c in range(nch):
                    CP("dve", hgv[:, c, :], PB(1, 64 * c, 64 * c + 64, 0, 256), ["pb1"], ["hgv"])
                    CP("act", gmo[:, c, :], PB(1, 64 * c, 64 * c + 64, 256, 512), ["pb1"], ["gmo"])
                for k in range(8):
                    MM(PB(0, 0, ntok, 0, 256), xb[:, k, :], Wt[:, k, 1024:1280], ["Wt", "xb"], ["pb0"], start=(k == 0), stop=(k == 7))
                for c in range(nch):
                    CP("act", ghg[:, c, :], PB(0, 64 * c, 64 * c + 64, 0, 256), ["pb0"], ["ghg"])
                ACT(sgo[:], gmo[:], AF.Sigmoid, ["gmo"], ["sgo"])
                S.add("dve", lambda e: e.tensor_tensor_scan(out=cl[:], data0=rmask[0:4, 0:ntok], data1=rf[:], initial=0.0, op0=ALU.mult, op1=ALU.add),
                      r=["rf", "rmask"], w=["cl"])
                STT(wv[:], ri[:], bi_col, cl[:], ALU.add, ALU.add, ["ri", "cl", "hrow"], ["wv"])
                S.add("dve", lambda e: e.tensor_reduce(out=wmax[:], in_=wv[:].rearrange("p (c t) -> p c t", t=64), axis=AX.X, op=ALU.max),
                      r=["wv"], w=["wmax"])
                TS("dve", blast[:], cl[:].rearrange("p (c t) -> p c t", t=64)[:, :, 63], -1.0, ALU.mult, ["cl"], ["blast"])
                S.add("dve", lambda e: e.tensor_tensor_scan(out=mcur[:], data0=wmax[:], data1=blast[:], initial=mstate[:], op0=ALU.max, op1=ALU.add),
                      r=["wmax", "blast", "mstate"], w=["mcur"])
                TT("dve", Mc[:], mcur[:], blast[:], ALU.subtract, ["mcur", "blast"], ["Mc"])
                CP("dve", mprev[:, 0:1], mstate[:], ["mstate"], ["mprev"])
                if nch > 1:
                    CP("dve", mprev[:, 1:nch], mcur[:, 0:nch - 1], ["mcur", "mprev"], ["mprev"])
                CP("dve", mstate[:], mcur[:, nch - 1:nch], ["mcur", "mprev"], ["mstate"])
                TT("dve", decr[:], mprev[:], Mc[:], ALU.subtract, ["mprev", "Mc"], ["decr"])
                ACT(decr[:], decr[:], AF.Exp, ["decr"], ["decr"])
                TT("dve", wsr[:].rearrange("p (c t) -> p c t", t=64), wv[:].rearrange("p (c t) -> p c t", t=64), bc_last(Mc[:], 64), ALU.subtract,
                   ["wv", "Mc"], ["wsr"])
                ACT(wsr[:], wsr[:], AF.Exp, ["wsr"], ["wsr"])
                TT("dve", gfr[:].rearrange("p (c t) -> p c t", t=64), cl[:].rearrange("p (c t) -> p c t", t=64), bc_last(Mc[:], 64), ALU.subtract,
                   ["cl", "Mc"], ["gfr"])
                ACT(gfr[:], gfr[:], AF.Exp, ["gfr"], ["gfr"])
                for c in range(nch):
                    TR(PB(2, 0, 64, 8 * c, 8 * c + 4), wsr[:, 64 * c:64 * c + 64], i4[:], ["wsr", "i4"], ["pb2"])
                    TR(PB(2, 0, 64, 8 * c + 4, 8 * c + 8), gfr[:, 64 * c:64 * c + 64], i4[:], ["gfr", "i4"], ["pb2"])
                CP("act", wsg[:].rearrange("p c e -> p (c e)"), PB(2, 0, 64, 0, 8 * nch), ["pb2"], ["wsg"])
                TT("dve", decd[:], bc_last(decr[:], 4), bc_mid(i4[:], nch), ALU.mult, ["decr", "i4"], ["decd"])
                MM(PB(2, 0, 64, 64, 64 + 4 * nch), c64[0:4, 5, :], decd[:].rearrange("p c h -> p (c h)"), ["decd", "c64"], ["pb2b"])
                CP("act", decbc[:].rearrange("p c h -> p (c h)"), PB(2, 0, 64, 64, 64 + 4 * nch), ["pb2b"], ["decbc"])
                ACT(qsil[:], hqf[:, 0], AF.Silu, ["hqf"], ["qsil"])
                ACT(sig[:], hqf[:, 1], AF.Sigmoid, ["hqf"], ["sig"])
                for h in range(4):
                    ACT(logf[:, h, :], sig[:, h, :], AF.Ln, ["sig", "oml", "lb"], ["logf"], bias=lb_t[:, h, l:l + 1], scale=oml_t[:, h, l:l + 1])
                    TS("pool", kk[:, h, :], sig[:, h, :], noml_t[:, h, l:l + 1], ALU.mult, ["sig", "noml", "oml"], ["kk"],
                       s2=oml_t[:, h, l:l + 1], op1=ALU.add)
                for h in range(4):
                    S.add("dve", lambda e, h=h: e.tensor_tensor_scan(out=lcum[:, h, :], data0=rmask[0:64, 0:ntok], data1=logf[:, h, :], initial=0.0,
                                                                     op0=ALU.mult, op1=ALU.add), r=["logf", "rmask"], w=["lcum"])
                lc4 = lcum[:].rearrange("p h (c t) -> p (h c) t", t=64)
                nhc = 4 * nch
                TT("dve", t1[:].rearrange("p h (c t) -> p (h c) t", t=64), lc4, bc_last(lc4[:, :, 31], 64), ALU.subtract, ["lcum"], ["t1"])
                ACT(t2[:], t1[:], AF.Exp, ["t1"], ["t2"])
                TT("dve", qs_[:], qsil[:], t2[:], ALU.mult, ["qsil", "t2"], ["qs_"])
                ACT(t2[:], t1[:], AF.Exp, ["t1", "qs_"], ["t2"], scale=-1.0)
                TT("dve", ks_[:], kk[:], t2[:], ALU.mult, ["kk", "t2"], ["ks_"])
                ACT(t2[:], lcum[:], AF.Exp, ["lcum", "ks_"], ["t2"])
                TT("dve", qe_[:], qsil[:], t2[:], ALU.mult, ["qsil", "t2"], ["qe_"])
                TT("dve", t1[:].rearrange("p h (c t) -> p (h c) t", t=64), bc_last(lc4[:, :, 63], 64), lc4, ALU.subtract, ["lcum", "t2"], ["t1"])
                ACT(t2[:], t1[:], AF.Exp, ["t1", "qe_"], ["t2"])
                TT("dve", kdT[:], kk[:], t2[:], ALU.mult, ["kk", "t2"], ["kdT"])
                ACT(el[:].rearrange("p h c -> p (h c)"), lc4[:, :, 63], AF.Exp, ["lcum"], ["el"])
                for c in range(nch):
                    cs = slice(64 * c, 64 * c + 64)
                    pT = PB16(3, 0, 64, 0, 128)
                    for h in range(4):
                        TR(pT[:, h * 64:(h + 1) * 64], kdT[:, h, cs], c64b[:, 4, :], ["kdT", "c64b"], ["pb3a"])
                    CP("act", kdtm[:, c].rearrange("p h t -> p (h t)"), pT, ["pb3a"], ["kdtm"])
                def mlstm_stream():
                    for c in range(nch):
                        cs = slice(64 * c, 64 * c + 64)
                        TT("pool", V2[:], V1[:, c], bc_last(wsg[:, c, 0:4], 65), ALU.mult, ["V1", "wsg"], ["V2"])
                        yield
                        for h in range(4):
                            MM(PB(4, 0, 64, h * 64, (h + 1) * 64), mkT[:, h, cs], mqT[:, h, cs], ["mkT", "mqT"], ["pb4a"])
                        TT("dve", STb[:], pbank[4][0:64, 0:256].rearrange("p (h t) -> p h t", h=4), bc_mid(C_TRIU, 4), ALU.mult, ["pb4a", "c64"], ["STb"])
                        yield
                        TT("dve", Cm[:], Cm[:], bc_last(decbc[:, c, :], 65), ALU.mult, ["Cm", "decbc"], ["Cm"])
                        CP("act", Cmb[:], Cm[:], ["Cm"], ["Cmb"])
                        yield
                        for h in range(4):
                            MM(PB(5, 0, 64, h * 65, (h + 1) * 65), STb[:, h, :], V2[:, h, :], ["STb", "V2"], ["pb5"], start=True, stop=False)
                            MM(PB(5, 0, 64, h * 65, (h + 1) * 65), mqT[:, h, cs], Cmb[:, h, :], ["mqT", "Cmb"], ["pb5"], start=False, stop=True)
                        pn = pbank[5][0:64, 0:260].rearrange("p (h e) -> p h e", h=4)
                        CP("act", hC[:], pn[:, :, 0:64], ["pb5"], ["hC"])
                        CP("act", aden[:], pn[:, :, 64], ["pb5"], ["aden"])
                        yield
                        STT(rec[:], aden[:], -1.0, aden[:], ALU.mult, ALU.max, ["aden"], ["rec"])
                        TT("dve", rec[:], rec[:], wsg[:, c, 4:8], ALU.max, ["rec", "wsg"], ["rec"])
                        S.add("dve", lambda e: e.reciprocal(out=rec[:], in_=rec[:]), r=["rec"], w=["rec"])
                        TT("dve", hC[:], hC[:], bc_last(rec[:], 64), ALU.mult, ["hC", "rec"], ["hC"])
                        TT("pool", O[:, c, 0:4, :], hC[:], sgo[:, c, :].rearrange("p (h d) -> p h d", h=4), ALU.mult, ["hC", "sgo"], ["O"])
                        yield
                        for h in range(4):
                            MM(PB(6, 0, 64, h * 65, (h + 1) * 65), mktm[:, c, h * 64:(h + 1) * 64], V2[:, h, :], ["mktm", "V2"], ["pb6"])
                        TT("dve", Cm[:], Cm[:], pbank[6][0:64, 0:260].rearrange("p (h e) -> p h e", h=4), ALU.add, ["Cm", "pb6", "Cmb"], ["Cm"])
                        yield
                def hgrn_stream():
                    for c in range(nch):
                        cs = slice(64 * c, 64 * c + 64)
                        for h in range(4):
                            MM(PB(2, 0, 64, h * 64, (h + 1) * 64), ks_[:, h, cs], qs_[:, h, cs], ["ks_", "qs_"], ["pb2"])
                        TT("dve", aTb[:], pbank[2][0:64, 0:256].rearrange("p (h t) -> p h t", h=4), bc_mid(C_TRIU, 4), ALU.mult, ["pb2", "c64"], ["aTb"])
                        yield
                        for h in range(4):
                            MM(PB(7, 0, 64, h * 64, (h + 1) * 64), aTb[:, h, :], hgv[:, c, h * 64:(h + 1) * 64], ["aTb", "hgv"], ["pb7a"], start=True, stop=False)
                            MM(PB(7, 0, 64, h * 64, (h + 1) * 64), qe_[:, h, cs], Shb[:, h, :], ["qe_", "Shb"], ["pb7a"], start=False, stop=True)
                        CP("act", O[:, c, 4:8, :].rearrange("p h t -> p (h t)"), PB(7, 0, 64, 0, 256), ["pb7a"], ["O"])
                        yield
                        for h in range(4):
                            MM(PB(7, 0, 64, 256 + h * 64, 256 + (h + 1) * 64), kdtm[:, c, h, :], hgv[:, c, h * 64:(h + 1) * 64], ["kdtm", "hgv"], ["pb7b"])
                        TT("dve", Sh[:], Sh[:], bc_last(el[:, :, c], 64), ALU.mult, ["Sh", "el", "Shb"], ["Sh"])
                        TT("dve", Sh[:].rearrange("p h t -> p (h t)"), Sh[:].rearrange("p h t -> p (h t)"), PB(7, 0, 64, 256, 512), ALU.add, ["Sh", "pb7b"], ["Sh"])
                        CP("act", Shb[:], Sh[:], ["Sh"], ["Shb"])
                        yield
                interleave(mlstm_stream(), hgrn_stream())
                merge(O, Osq, ssq, Ob, nch, l, 4, pos0, gate0=None, gate1=("sigmoid", ghg, sgate, "ghg", 0))
            S.dma("sp", sq["c"](l).rearrange("h k v -> k h v"), Cm[:, :, 0:64], r=["Cm"], chan="osA", final=True)
            S.dma("sp", sq["n"](l).rearrange("h (k o) -> k h o", o=1), Cm[:, :, 64:65], r=["Cm"], chan="osB", final=True, nc_ok=True)
            S.dma("sp", sq["m"](l).rearrange("(h o) -> h o", o=1), mstate[:], r=["mstate"], chan="osC", final=True, nc_ok=True)
            S.dma("sp", sq["hg"](l).rearrange("h k v -> k h v"), Sh[:], r=["Sh"], chan="osD", final=True)

        NW = 8

        def phaseD(sq, l, last_layer):
            T = sq["T"]
            nd = min(512, T)
            ntile = T // nd
            AR.reset()
            S.barrier(dummy[:])
            ring = [AR.alloc((128, 4096), BF16) for _ in range(NW)]
            rx = AR.alloc((128, 8, nd))
            x1b = AR.alloc((128, 8, nd), BF16)
            sqb = AR.alloc((128, 8, nd), BF16)
            hT = AR.alloc((128, 32, nd), BF16)
            tmp = [AR.alloc((128, nd)) for _ in range(2)]
            mean_sb = AR.alloc((128, nd))
            rstd = AR.alloc((128, nd))
            ytm = AR.alloc((128, 1024)) if last_layer else None
            vec = vec_t[:, l, :].rearrange("p (a b) -> p a b", a=5)
            total = ntile * 18
            state = dict(next=0)

            def fetch():
                i = state["next"]
                if i >= total:
                    return
                slot = i % NW
                ci = i % 18
                S.dma("sp", ring[slot][:], wdn_bf[l, ci], r=[f"wbf{l}"], w=[f"ring{slot}"], chan=f"rg{slot}")
                state["next"] = i + 1

            for _ in range(NW):
                fetch()
            used = dict(n=0)

            def chunk():
                i = used["n"]
                used["n"] = i + 1
                return i % NW

            def layernorm(gi, bi_):
                CP("act", x1b[:], rx[:], ["rx"], ["x1b"])
                ACT(sqb[:], rx[:], AF.Square, ["rx"], ["sqb"])
                for m in range(8):
                    MM(PB(2, 0, 128, 0, nd), odiv[:], x1b[:, m, :], ["odiv", "x1b"], ["pb2"], start=(m == 0), stop=(m == 7))
                for m in range(8):
                    MM(PB(3, 0, 128, 0, nd), odiv[:], sqb[:, m, :], ["odiv", "sqb"], ["pb3"], start=(m == 0), stop=(m == 7))
                CP("act", mean_sb[:], PB(2, 0, 128, 0, nd), ["pb2"], ["mean"])
                ACT(rstd[:], PB(2, 0, 128, 0, nd), AF.Square, ["pb2"], ["rstd"])
                TT("dve", rstd[:], PB(3, 0, 128, 0, nd), rstd[:], ALU.subtract, ["pb3", "rstd"], ["rstd"])
                ACT(rstd[:], rstd[:], AF.Sqrt, ["rstd", "cst"], ["rstd"], bias=EPS5, scale=1.0)
                S.add("dve", lambda e: e.reciprocal(out=rstd[:], in_=rstd[:]), r=["rstd"], w=["rstd"])
                for m in range(8):
                    t = tmp[m % 2]
                    tk = f"tmp{m % 2}"
                    TT("dve", t[:], rx[:, m, :], mean_sb[:], ALU.subtract, ["rx", "mean"], [tk])
                    TT("pool", t[:], t[:], rstd[:], ALU.mult, [tk, "rstd"], [tk])
                    ACT(rx[:, m, :], t[:], AF.Identity, [tk, "vec"], ["rx"], bias=vec[:, bi_, m:m + 1], scale=vec[:, gi, m:m + 1])
                    CP("pool", x1b[:, m, :], rx[:, m, :], ["rx"], ["x1b"])

            for ti in range(ntile):
                pos0 = ti * nd
                S.dma("sp", rx[:], xscr[l % 2, :, :, pos0:pos0 + nd], r=[f"xscr{l % 2}"], w=["rx"], chan="rxD")
                for j in range(2):
                    slot = chunk()
                    wv_ = ring[slot][:].rearrange("p (k c) -> p k c", k=8)
                    for mm_ in range(4):
                        m = 4 * j + mm_
                        bank = m % 2
                        for k in range(8):
                            MM(PB(bank, 0, 128, 0, nd), wv_[:, k, mm_ * 128:(mm_ + 1) * 128], oT[:, k, pos0:pos0 + nd], [f"ring{slot}", "oT"], [f"pb{bank}"],
                               start=(k == 0), stop=(k == 7))
                        STT(rx[:, m, :], rx[:, m, :], DN_ALPHA, PB(bank, 0, 128, 0, nd), ALU.mult, ALU.add, ["rx", f"pb{bank}"], ["rx"])
                    fetch()
                layernorm(1, 2)
                for j in range(8):
                    slot = chunk()
                    wv_ = ring[slot][:].rearrange("p (k c) -> p k c", k=8)
                    for mm_ in range(4):
                        m = 4 * j + mm_
                        bank = m % 2
                        for k in range(8):
                            MM(PB(bank, 0, 128, 0, nd), wv_[:, k, mm_ * 128:(mm_ + 1) * 128], x1b[:, k, :], [f"ring{slot}", "x1b"], [f"pb{bank}"],
                               start=(k == 0), stop=(k == 7))
                        t = tmp[m % 2]
                        tk = f"tmp{m % 2}"
                        ACT(t[:], PB(bank, 0, 128, 0, nd), AF.Relu, [f"pb{bank}", "b1"], [tk], bias=b1_t[:, l, m:m + 1], scale=1.0)
                        TT("pool", hT[:, m, :], t[:], t[:], ALU.mult, [tk], ["hT"])
                    fetch()
                for m in range(8):
                    slot = chunk()
                    wv_ = ring[slot][:].rearrange("p (k c) -> p k c", k=32)
                    bank = m % 2
                    for k in range(32):
                        MM(PB(bank, 0, 128, 0, nd), wv_[:, k, :], hT[:, k, :], [f"ring{slot}", "hT"], [f"pb{bank}"], start=(k == 0), stop=(k == 31))
                    t = tmp[m % 2]
                    tk = f"tmp{m % 2}"
                    ACT(t[:], PB(bank, 0, 128, 0, nd), AF.Identity, [f"pb{bank}", "vec"], [tk], bias=vec[:, 0, m:m + 1], scale=1.0)
                    STT(rx[:, m, :], rx[:, m, :], DN_ALPHA, t[:], ALU.mult, ALU.add, ["rx", tk], ["rx"])
                    fetch()
                layernorm(3, 4)
                if not last_layer:
                    S.dma("sp", xscr[(l + 1) % 2, :, :, pos0:pos0 + nd], rx[:], r=["rx"], w=[f"xscr{(l + 1) % 2}"], chan="xoD")
                else:
                    nsub = (nd + 127) // 128
                    for su in range(nsub):
                        ns = min(128, nd - su * 128)
                        for m in range(8):
                            bank = 4 + m // 4
                            TR(PB(bank, 0, ns, (m % 4) * 128, (m % 4 + 1) * 128), rx[:, m, su * 128:su * 128 + ns], ID128, ["rx", "c128"], [f"pb{bank}"])
                        CP("act", ytm[0:ns, 0:512], PB(4, 0, ns, 0, 512), ["pb4"], ["ytm"])
                        CP("dve", ytm[0:ns, 512:1024], PB(5, 0, ns, 0, 512), ["pb5"], ["ytm"])
                        S.dma("sp", sq["y"][pos0 + su * 128:pos0 + su * 128 + ns, :], ytm[0:ns, :], r=["ytm"], chan="yD", final=True)

        import os
        kstop = os.environ.get("KSTOP", "")
        try:
            for sq in seqs:
                phase0(sq)
                stop_at("p0")
                for l in range(L):
                    passA(sq, l)
                    stop_at("A")
                    passB(sq, l)
                    if KDEBUG and l == 0 and sq is seqs[0]:
                        S.dma("sp", dbg_oT, oT[:], r=["oT"], chan="dbgo", final=True)
                    stop_at("B")
                    phaseD(sq, l, l == L - 1)
                    stop_at("D")
        except StopBuild as e:
            print("build stopped at", e)
        print('arena high water', AR.hi, 'of', AW)
        S.emit()
    return nc


def host_layout_weights(inp, L):
    w_in = np.asarray(inp["w_in"], np.float32)[:L]

    def fm(cols):
        w = w_in[:, :, cols]
        return np.ascontiguousarray(w.reshape(L, 8, 128, len(cols)).transpose(0, 2, 1, 3))

    out = dict(wfa=fm(FA_COLS), wta=fm(TA_COLS), wfb=fm(FB_COLS), wrb=fm(RB_COLS), wtb=fm(TB_COLS))
    w_out = np.asarray(inp["w_out"], np.float32)[:L]
    w1 = np.asarray(inp["w1"], np.float32)[:L]
    w2 = np.asarray(inp["w2"], np.float32)[:L]
    wdn = np.empty((L, 18, 128, 4096), np.float32)
    wo = w_out.reshape(L, 8, 128, 2, 512).transpose(0, 3, 2, 1, 4)
    wdn[:, 0:2] = wo.reshape(L, 2, 128, 4096)
    w1r = w1.reshape(L, 8, 128, 8, 512).transpose(0, 3, 2, 1, 4)
    wdn[:, 2:10] = w1r.reshape(L, 8, 128, 4096)
    w2r = w2.reshape(L, 32, 128, 8, 128).transpose(0, 3, 2, 1, 4)
    wdn[:, 10:18] = w2r.reshape(L, 8, 128, 4096)
    out["wdn"] = wdn
    cw = np.asarray(inp["gdn_conv_w"], np.float32)[:L]
    out["convw"] = np.ascontiguousarray(cw.reshape(L, 4, 6, 128).transpose(0, 3, 2, 1))
    out["b1c"] = np.ascontiguousarray(np.asarray(inp["b1"], np.float32)[:L].reshape(L, 32, 128).transpose(0, 2, 1))
    vs = np.stack([np.asarray(inp[k], np.float32)[:L].reshape(L, 8, 128).transpose(0, 2, 1)
                   for k in ("b2", "ln1_g", "ln1_b", "ln2_g", "ln2_b")], axis=2)
    out["vecs"] = np.ascontiguousarray(vs)
    out["outg"] = np.ascontiguousarray(np.asarray(inp["out_norm_g"], np.float32)[:L])
    out["hvec"] = np.ascontiguousarray(np.stack([np.asarray(inp[k], np.float32)[:L] for k in
                                                 ("gdn_A_log", "gdn_dt_bias", "mlstm_b_i", "mlstm_b_f")], axis=1))
    out["lbl"] = np.ascontiguousarray(np.asarray(inp["hgrn_lb_logits"], np.float32)[:L])
    out.update(make_consts())
    return out


_PROG_CACHE = {}


def run(inp, n_cores, NPS, TP, L, SAMPLE):
    key = (NPS, TP, L, SAMPLE)
    if key not in _PROG_CACHE:
        _PROG_CACHE[key] = build(NPS, TP, L, SAMPLE)
    nc = _PROG_CACHE[key]
    shared = host_layout_weights(inp, L)
    xpr = np.asarray(inp["x_prompt"], np.float32)
    in_maps = []
    for c in range(n_cores):
        m = dict(shared)
        m["xp"] = np.ascontiguousarray(xpr[c * NPS:(c + 1) * NPS])
        if SAMPLE:
            m["xs"] = np.ascontiguousarray(np.asarray(inp["x_sample"], np.float32)[c])
            m["ck"] = np.ascontiguousarray(np.asarray(inp["cache_sb_k"], np.float32)[:L, c].reshape(L, PAST, 256))
            m["cv"] = np.ascontiguousarray(np.asarray(inp["cache_sb_v"], np.float32)[:L, c].reshape(L, PAST, 256))
            m["sconv"] = np.ascontiguousarray(np.asarray(inp["state_gdn_conv"], np.float32)[:L, c])
            m["gS"] = np.ascontiguousarray(np.asarray(inp["state_gdn_S"], np.float32)[:L, c])
            m["mC"] = np.ascontiguousarray(np.asarray(inp["state_mlstm_C"], np.float32)[:L, c])
            m["mn"] = np.ascontiguousarray(np.asarray(inp["state_mlstm_n"], np.float32)[:L, c])
            m["mm"] = np.ascontiguousarray(np.asarray(inp["state_mlstm_m"], np.float32)[:L, c])
            m["hS"] = np.ascontiguousarray(np.asarray(inp["state_hgrn_S"], np.float32)[:L, c])
        in_maps.append(m)
    res = run_bass_kernel_spmd(nc, in_maps, core_ids=list(range(n_cores)))
    R = res.results
    global LAST_RESULTS
    LAST_RESULTS = R

    def cat(name, axis):
        return np.concatenate([np.asarray(r[name], np.float32) for r in R], axis=axis)

    y_p = cat("y_p", 0)
    outs = [y_p]
    if SAMPLE:
        outs.append(np.stack([np.asarray(r["y_s"], np.float32) for r in R], 0))
    else:
        outs.append(None)
    B = n_cores * NPS
    outs += [cat("p_k", 1).reshape(L, B, TP, 4, 64), cat("p_v", 1).reshape(L, B, TP, 4, 64), cat("p_conv", 1), cat("p_gdn", 1),
             cat("p_c", 1), cat("p_n", 1), cat("p_m", 1), cat("p_hg", 1)]
    if SAMPLE:
        outs += [cat("s_k", 1).reshape(L, n_cores, 64, 4, 64), cat("s_v", 1).reshape(L, n_cores, 64, 4, 64), cat("s_conv", 1),
                 cat("s_gdn", 1), cat("s_c", 1), cat("s_n", 1), cat("s_m", 1), cat("s_hg", 1)]
    return tuple(outs)


def kernel(**inputs):
    return run(inputs, 8, 2, 2048, DEPTH, True)
```

```python
import contextlib
import os
import math
import numpy as np
import concourse.bass as bass
import concourse.mybir as mybir
from concourse.bass_utils import run_bass_kernel_spmd

F32 = mybir.dt.float32
BF16 = mybir.dt.bfloat16
ALU = mybir.AluOpType
AF = mybir.ActivationFunctionType
AX = mybir.AxisListType

D_MODEL = 1024
DEPTH = 4
PAST = 2048
DN_ALPHA = (2 * DEPTH) ** 0.25
OFF = dict(gq=0, gk=256, gv=512, ggate=768, gbeta=1024, ga=1028, sq=1032, sk=1288, sv=1544,
           mq=1800, mk=2056, mv=2312, mo=2568, mi=2824, mf=2828, hq=2832, hf=3088, hi=3344, hg=3600)


def _cols(*specs):
    out = []
    for name, n in specs:
        out.extend(range(OFF[name], OFF[name] + n))
    return np.array(out)


FA_COLS = _cols(("gq", 256), ("gk", 256), ("gv", 256), ("sq", 256), ("sk", 256))
TA_COLS = _cols(("sk", 256), ("sv", 256), ("ggate", 256), ("gbeta", 4), ("ga", 4))
FB_COLS = _cols(("mq", 256), ("mk", 256), ("hq", 256), ("hf", 256))
RB_COLS = _cols(("mi", 4), ("mf", 4))
TB_COLS = _cols(("mk", 256), ("mv", 256), ("hi", 256), ("mo", 256), ("hg", 256))

ENGS = ("pe", "act", "dve", "pool", "sp")
EPOCH = 30000


class Op:
    __slots__ = ("eng", "fn", "deps", "marked", "chan", "val", "n")

    def __init__(self, eng, fn, chan=None):
        self.eng = eng
        self.fn = fn
        self.deps = ()
        self.marked = False
        self.chan = chan
        self.val = 0
        self.n = 0


class Sched:
    def __init__(self, nc, same_sync=True):
        self.nc = nc
        self.same_sync = same_sync
        self.ops = {e: [] for e in ENGS}
        self.last_w = {}
        self.readers = {}
        self.chan_cnt = {}
        self.chan_last = {}
        self.final = []
        self.fence = None
        self.fenced = set()

    def add(self, eng, fn, r=(), w=(), chan=None):
        r2 = []
        w2 = []
        for k in r:
            if k.startswith("pb"):
                if k[:3] not in w2:
                    w2.append(k[:3])
            else:
                r2.append(k)
        for k in w:
            k = k[:3] if k.startswith("pb") else k
            if k not in w2:
                w2.append(k)
        r, w = r2, w2
        op = Op(eng, fn, chan)
        deps = set()
        raw = set()
        for k in r:
            lw = self.last_w.get(k)
            if lw is not None:
                deps.add(lw)
                raw.add(lw)
        for k in w:
            lw = self.last_w.get(k)
            if lw is not None:
                deps.add(lw)
            for rd in self.readers.get(k, ()):
                deps.add(rd)
        if self.fence is not None and eng not in self.fenced:
            deps.add(self.fence)
            self.fenced.add(eng)
        out = []
        for d in deps:
            if d.eng == eng and d.chan is None and chan is None:
                if eng == "pe" or not self.same_sync:
                    continue
                if eng in ("act", "dve") and d not in raw:
                    continue
            out.append(d)
            d.marked = True
        op.deps = out
        for k in r:
            self.readers.setdefault(k, []).append(op)
        for k in w:
            self.last_w[k] = op
            self.readers[k] = []
        if chan is not None:
            c = self.chan_cnt.get(chan, 0) + 1
            self.chan_cnt[chan] = c
            op.val = 16 * c
            op.marked = True
            self.chan_last[chan] = op
        self.ops[eng].append(op)
        return op

    def dma(self, eng, out, in_, r=(), w=(), chan=None, final=False, nc_ok=False):
        assert chan is not None
        if nc_ok:
            fn = lambda e: e.dma_start(out=out, in_=in_, allow_slow_non_contiguous=True)
        else:
            fn = lambda e: e.dma_start(out=out, in_=in_)
        op = self.add(eng, fn, r=r, w=w, chan=chan)
        if final:
            self.final.append(op)
        return op

    def barrier(self, dummy):
        lasts = [ops[-1] for ops in self.ops.values() if ops and ops[-1].chan is None]
        lasts += [o for c, o in self.chan_last.items() if not c.startswith("pc")]
        op = Op("dve", lambda e: e.memset(dummy, 0.0))
        deps = []
        for d in set(lasts):
            if d.eng == "dve" and d.chan is None:
                continue
            deps.append(d)
            d.marked = True
        op.deps = deps
        op.marked = True
        self.ops["dve"].append(op)
        self.fence = op
        self.fenced = {"dve"}
        self.last_w = {k: v for k, v in self.last_w.items() if k.startswith("wbf")}
        self.readers = {}

    def emit(self):
        nc = self.nc
        with contextlib.ExitStack() as st:
            esems = {}
            for e in ENGS:
                n = 0
                for op in self.ops[e]:
                    if op.marked and op.chan is None:
                        n += 1
                        op.n = n
                nep = (n + EPOCH - 1) // EPOCH
                esems[e] = [st.enter_context(nc.semaphore(f"s_{e}{i}")) for i in range(nep)]
            csems = {c: st.enter_context(nc.semaphore(f"c_{c}")) for c in self.chan_cnt}

            def semval(d):
                if d.chan is not None:
                    return csems[d.chan], d.val
                ep = (d.n - 1) // EPOCH
                return esems[d.eng][ep], d.n - ep * EPOCH

            block = st.enter_context(nc.Block())
            finals = [semval(o) for o in self.final]

            def run(e, eng):
                waited = {}
                for op in self.ops[e]:
                    for d in op.deps:
                        s, v = semval(d)
                        key = id(s)
                        if waited.get(key, 0) >= v:
                            continue
                        waited[key] = v
                        eng.wait_ge(s, v)
                    ins = op.fn(eng)
                    if op.marked:
                        if op.chan is not None:
                            ins.then_inc(csems[op.chan], 16)
                        else:
                            s, _ = semval(op)
                            ins.then_inc(s, 1)
                if e == "sp":
                    best = {}
                    for s, v in finals:
                        if best.get(id(s), (None, 0))[1] < v:
                            best[id(s)] = (s, v)
                    for s, v in best.values():
                        eng.wait_ge(s, v)

            @block.tensor
            def _(eng):
                run("pe", eng)

            @block.scalar
            def _(eng):
                run("act", eng)

            @block.vector
            def _(eng):
                run("dve", eng)

            @block.gpsimd
            def _(eng):
                run("pool", eng)

            @block.sync
            def _(eng):
                run("sp", eng)


class StopBuild(Exception):
    pass


def interleave(*gens):
    gens = list(gens)
    while gens:
        for g in list(gens):
            try:
                next(g)
            except StopIteration:
                gens.remove(g)


def stop_at(tag):
    import os
    if os.environ.get("KSTOP", "") == tag:
        raise StopBuild(tag)


class Arena:
    def __init__(self, ap, ncols):
        self.ap = ap
        self.n = ncols
        self.off = 0

    def reset(self):
        self.off = 0

    def alloc(self, shape, dt=F32):
        P = shape[0]
        free = int(np.prod(shape[1:]))
        nb = free * (4 if dt == F32 else 2)
        ncol = (nb + 3) // 4
        a = self.ap[0:P, self.off:self.off + ncol]
        self.off += (ncol + 15) // 16 * 16
        self.hi = max(getattr(self, "hi", 0), self.off)
        assert self.off <= self.n, f"arena overflow {self.off} > {self.n}"
        if dt != F32:
            a = a.bitcast(dt)
            if a.shape[1] != free:
                a = a[:, 0:free]
        if len(shape) == 3:
            a = a.rearrange("p (a b) -> p a b", a=shape[1])
        elif len(shape) == 4:
            a = a.rearrange("p (a b c) -> p a b c", a=shape[1], b=shape[2])
        return a


def make_consts():
    p = np.arange(64)[:, None]
    f = np.arange(64)[None, :]
    c64 = np.zeros((64, 8, 64), np.float32)
    c64[:, 0] = (p <= f)
    c64[:, 1] = -(p <= f).astype(np.float32)
    c64[:, 2] = -(p < f).astype(np.float32)
    c64[:, 3] = -(f < p).astype(np.float32)
    c64[:, 4] = np.eye(64)
    c64[:, 5] = 1.0
    P = np.arange(128)[:, None]
    Fq = np.arange(128)[None, :]
    c128 = np.zeros((128, 4, 128), np.float32)
    c128[:, 0] = np.eye(128)
    c128[:, 1] = (P < Fq)
    c128[:, 2] = (P >= Fq)
    c128[:, 3] = (P // 64 == Fq // 64)
    rmask = np.ones((64, 128), np.float32)
    rmask[:, ::64] = 0.0
    i4 = np.zeros((4, 4), np.float32)
    i4[:] = np.eye(4)
    return dict(c64=c64, c128=c128, rmask=rmask, i4=i4)


def build(NPS, TP, L, SAMPLE, TS_=64):
    nc = bass.Bass("TRN2", target_bir_lowering=False)
    dins = {}
    douts = {}

    def din(name, shape):
        dins[name] = nc.dram_tensor(name, list(shape), F32, kind="ExternalInput").ap()
        return dins[name]

    def dout(name, shape):
        douts[name] = nc.dram_tensor(name, list(shape), F32, kind="ExternalOutput").ap()
        return douts[name]

    xp = din("xp", (NPS, TP, 1024))
    wfa = din("wfa", (L, 128, 8, 1280))
    wta = din("wta", (L, 128, 8, 776))
    wfb = din("wfb", (L, 128, 8, 1024))
    wrb = din("wrb", (L, 128, 8, 8))
    wtb = din("wtb", (L, 128, 8, 1280))
    wdn = din("wdn", (L, 18, 128, 4096))
    convw = din("convw", (L, 128, 6, 4))
    b1c = din("b1c", (L, 128, 32))
    vecs = din("vecs", (L, 128, 5, 8))
    outg = din("outg", (L, 1024))
    hvec = din("hvec", (L, 4, 4))
    lbl = din("lbl", (L, 256))
    c64d = din("c64", (64, 8, 64))
    c128d = din("c128", (128, 4, 128))
    rmaskd = din("rmask", (64, 128))
    i4d = din("i4", (4, 4))
    if SAMPLE:
        xs_in = din("xs", (TS_, 1024))
        ck = din("ck", (L, PAST, 256))
        cv = din("cv", (L, PAST, 256))
        sconv = din("sconv", (L, 3, 768))
        gS_in = din("gS", (L, 4, 64, 64))
        mC_in = din("mC", (L, 4, 64, 64))
        mn_in = din("mn", (L, 4, 64))
        mm_in = din("mm", (L, 4))
        hS_in = din("hS", (L, 4, 64, 64))
    y_p = dout("y_p", (NPS, TP, 1024))
    p_k = dout("p_k", (L, NPS, TP, 256))
    p_v = dout("p_v", (L, NPS, TP, 256))
    p_conv = dout("p_conv", (L, NPS, 3, 768))
    p_gdn = dout("p_gdn", (L, NPS, 4, 64, 64))
    p_c = dout("p_c", (L, NPS, 4, 64, 64))
    p_n = dout("p_n", (L, NPS, 4, 64))
    p_m = dout("p_m", (L, NPS, 4))
    p_hg = dout("p_hg", (L, NPS, 4, 64, 64))
    if SAMPLE:
        y_s = dout("y_s", (TS_, 1024))
        s_k = dout("s_k", (L, 1, TS_, 256))
        s_v = dout("s_v", (L, 1, TS_, 256))
        s_conv = dout("s_conv", (L, 1, 3, 768))
        s_gdn = dout("s_gdn", (L, 1, 4, 64, 64))
        s_c = dout("s_c", (L, 1, 4, 64, 64))
        s_n = dout("s_n", (L, 1, 4, 64))
        s_m = dout("s_m", (L, 1, 4))
        s_hg = dout("s_hg", (L, 1, 4, 64, 64))
    TMAX = max(TP, TS_)
    xscr = nc.dram_tensor("xscr", [2, 128, 8, TMAX], F32, kind="Internal").ap()
    wdn_bf = nc.dram_tensor("wdn_bf", [L, 18, 128, 4096], BF16, kind="Internal").ap()
    wfa_bf = nc.dram_tensor("wfa_bf", [L, 128, 8 * 1280], BF16, kind="Internal").ap()
    wta_bf = nc.dram_tensor("wta_bf", [L, 128, 8 * 776], BF16, kind="Internal").ap()
    wfb_bf = nc.dram_tensor("wfb_bf", [L, 128, 8 * 1024], BF16, kind="Internal").ap()
    wrb_bf = nc.dram_tensor("wrb_bf", [L, 128, 8 * 8], BF16, kind="Internal").ap()
    wtb_bf = nc.dram_tensor("wtb_bf", [L, 128, 8 * 1280], BF16, kind="Internal").ap()
    import os
    KDEBUG = os.environ.get("KDEBUG", "") == "1"
    if KDEBUG:
        dbg_oT = nc.dram_tensor("dbg_oT", [128, 8, TMAX], BF16, kind="ExternalOutput").ap()

    seqs = []
    for i in range(NPS):
        seqs.append(dict(T=TP, past=0, x=xp[i], y=y_p[i], k=lambda l, i=i: p_k[l, i], v=lambda l, i=i: p_v[l, i],
                         conv=lambda l, i=i: p_conv[l, i], gdn=lambda l, i=i: p_gdn[l, i], c=lambda l, i=i: p_c[l, i],
                         n=lambda l, i=i: p_n[l, i], m=lambda l, i=i: p_m[l, i], hg=lambda l, i=i: p_hg[l, i], sample=False))
    if SAMPLE:
        seqs.append(dict(T=TS_, past=PAST, x=xs_in, y=y_s, k=lambda l: s_k[l, 0], v=lambda l: s_v[l, 0],
                         conv=lambda l: s_conv[l, 0], gdn=lambda l: s_gdn[l, 0], c=lambda l: s_c[l, 0],
                         n=lambda l: s_n[l, 0], m=lambda l: s_m[l, 0], hg=lambda l: s_hg[l, 0], sample=True))

    AW = 38100
    NKMAX = PAST + TS_ if SAMPLE else TP
    NKMAX = max(NKMAX, TP)
    NKT = (NKMAX + 127) // 128

    with contextlib.ExitStack() as st:
        def sb(name, shape, dt=F32):
            return st.enter_context(nc.sbuf_tensor(name, list(shape), dt))

        S = Sched(nc)
        arena_t = sb("arena", (128, AW))
        AR = Arena(arena_t, AW)
        dummy = sb("fence_dummy", (1, 8))
        oT = sb("oT", (128, 8, TMAX), BF16)
        c64 = sb("c64t", (64, 8, 64))
        c64b = sb("c64b", (64, 8, 64), BF16)
        c128 = sb("c128t", (128, 4, 128))
        c128b = sb("c128b", (128, 4, 128), BF16)
        rmask = sb("rmaskt", (64, 128))
        i4 = sb("i4t", (4, 4))
        cst = sb("cst", (128, 8))
        odiv = sb("odiv", (128, 128), BF16)
        convw_t = sb("convw_t", (128, L, 24))
        b1_t = sb("b1_t", (128, L, 32))
        vec_t = sb("vec_t", (128, L, 40))
        outg_t = sb("outg_t", (64, L, 1024))
        hrow_t = sb("hrow_t", (4, L, 4))
        hbc_t = sb("hbc_t", (64, L, 16))
        negA_t = sb("negA_t", (64, L, 4))
        nbf_t = sb("nbf_t", (4, L, 1))
        lb_t = sb("lb_t", (64, 4, L))
        oml_t = sb("oml_t", (64, 4, L))
        noml_t = sb("noml_t", (64, 4, L))
        lbtmp = sb("lbtmp", (64, 4, L))
        lbsum = sb("lbsum", (64, 4))
        pbank = [st.enter_context(nc.psum_tensor(f"pb{i}", [128, 512], F32)) for i in range(8)]

        def PB(i, p0, p1, c0, c1):
            return pbank[i][p0:p1, c0:c1]

        def PB16(i, p0, p1, c0, c1):
            return pbank[i][p0:p1, c0:c1].bitcast(BF16)

        def ACT(out, in_, func, r, w, bias=None, scale=None):
            kw = {}
            if bias is not None:
                kw["bias"] = bias
            if scale is not None:
                kw["scale"] = scale
            S.add("act", lambda e: e.activation(out=out, in_=in_, func=func, **kw), r=r, w=w)

        def TT(eng, out, in0, in1, op, r, w):
            S.add(eng, lambda e: e.tensor_tensor(out=out, in0=in0, in1=in1, op=op), r=r, w=w)

        def TS(eng, out, in0, s1, op0, r, w, s2=None, op1=None):
            if op1 is None:
                S.add(eng, lambda e: e.tensor_scalar(out=out, in0=in0, scalar1=s1, scalar2=None, op0=op0), r=r, w=w)
            else:
                S.add(eng, lambda e: e.tensor_scalar(out=out, in0=in0, scalar1=s1, scalar2=s2, op0=op0, op1=op1), r=r, w=w)

        def STT(out, in0, scalar, in1, op0, op1, r, w):
            S.add("dve", lambda e: e.scalar_tensor_tensor(out=out, in0=in0, scalar=scalar, in1=in1, op0=op0, op1=op1), r=r, w=w)

        def CP(eng, out, in_, r, w):
            if eng == "act":
                S.add("act", lambda e: e.activation(out=out, in_=in_, func=AF.Copy), r=r, w=w)
            else:
                S.add(eng, lambda e: e.tensor_copy(out=out, in_=in_), r=r, w=w)

        def MM(out, lhsT, rhs, r, w, start=True, stop=True):
            S.add("pe", lambda e: e.matmul(out, lhsT=lhsT, rhs=rhs, start=start, stop=stop), r=r, w=w)

        def TR(out, in_, ident, r, w):
            S.add("pe", lambda e: e.transpose(out, in_, ident), r=r, w=w)

        def MSET(eng, ap, val, w):
            S.add(eng, lambda e: e.memset(ap, val), w=w)

        def bc_mid(ap2, n):
            return ap2.unsqueeze(1).to_broadcast([ap2.shape[0], n, ap2.shape[1]])

        def bc_last(ap2, n):
            return ap2.unsqueeze(2).to_broadcast([ap2.shape[0], ap2.shape[1], n])

        ONE = cst[:, 0:1]

        def SIGM(out, in_, r, w):
            np_ = out.shape[0]
            ACT(out, in_, AF.Exp, r, w, scale=-1.0)
            ACT(out, out, AF.Ln, list(w) + ["cst"], w, bias=cst[0:np_, 0:1], scale=1.0)
            ACT(out, out, AF.Exp, w, w, scale=-1.0)

        def RSQRT(out, in_, eps_ap, r, w, scale=1.0):
            ACT(out, in_, AF.Ln, list(r) + ["cst"], w, bias=eps_ap, scale=scale)
            ACT(out, out, AF.Exp, w, w, scale=-0.5)
        EPS6 = cst[:, 1:2]
        EPS5 = cst[:, 2:3]

        S.dma("sp", c64[:], c64d, w=["c64"], chan="ld0")
        S.dma("sp", c128[:], c128d, w=["c128"], chan="ld1")
        S.dma("sp", rmask[:], rmaskd, w=["rmask"], chan="ld2")
        S.dma("sp", i4[:], i4d, w=["i4"], chan="ld3")
        S.dma("sp", convw_t[:].rearrange("p l (b j) -> p l b j", b=6), convw.rearrange("l p b j -> p l b j"), w=["convw"], chan="ld4")
        S.dma("sp", b1_t[:], b1c.rearrange("l p m -> p l m"), w=["b1"], chan="ld5")
        S.dma("sp", vec_t[:].rearrange("p l (a b) -> p l a b", a=5), vecs.rearrange("l p a b -> p l a b"), w=["vec"], chan="ld6")
        for l in range(L):
            S.dma("sp", outg_t[:, l, :], outg[l:l + 1, :].to_broadcast([64, 1024]), w=["outg"], chan="ld7")
            S.dma("sp", hbc_t[:, l, :], hvec[l:l + 1].rearrange("o a b -> o (a b)").to_broadcast([64, 16]), w=["hbc"], chan="ld8")
        S.dma("sp", hrow_t[:], hvec.rearrange("l a h -> h l a"), w=["hrow"], chan="ld9", nc_ok=True)
        for l in range(L):
            S.dma("sp", lb_t[:, :, l], lbl[l].rearrange("(h d) -> d h", h=4), w=["lb"], chan="ld10", nc_ok=True)
        MSET("pool", cst[:, 0:1], 1.0, ["cst"])
        MSET("pool", cst[:, 1:2], 1e-6, ["cst"])
        MSET("pool", cst[:, 2:3], 1e-5, ["cst"])
        MSET("pool", odiv[:], 1.0 / 1024.0, ["odiv"])
        CP("dve", c64b[:], c64[:], ["c64"], ["c64b"])
        CP("dve", c128b[:], c128[:], ["c128"], ["c128b"])
        ACT(negA_t[:], hbc_t[:, :, 0:4], AF.Exp, ["hbc"], ["negA"])
        TS("dve", negA_t[:], negA_t[:], -1.0, ALU.mult, ["negA"], ["negA"])
        TS("dve", nbf_t[:], hrow_t[:, :, 3:4], -1.0, ALU.mult, ["hrow"], ["nbf"])
        ACT(lbtmp[:], lb_t[:], AF.Exp, ["lb"], ["lbtmp"])
        S.add("dve", lambda e: e.tensor_reduce(out=lbsum[:], in_=lbtmp[:], axis=AX.X, op=ALU.add), r=["lbtmp"], w=["lbsum"])
        S.add("dve", lambda e: e.reciprocal(out=lbsum[:], in_=lbsum[:]), r=["lbsum"], w=["lbsum"])
        TT("dve", lbtmp[:], lbtmp[:], bc_last(lbsum[:], L), ALU.mult, ["lbtmp", "lbsum"], ["lbtmp"])
        MSET("pool", lb_t[:, :, 0:1], 0.0, ["lb"])
        for l in range(1, L):
            if l == 1:
                CP("dve", lb_t[:, :, 1:2], lbtmp[:, :, 1:2], ["lbtmp", "lb"], ["lb"])
            else:
                TT("dve", lb_t[:, :, l:l + 1], lb_t[:, :, l - 1:l], lbtmp[:, :, l:l + 1], ALU.add, ["lbtmp", "lb"], ["lb"])
        TS("dve", oml_t[:], lb_t[:], -1.0, ALU.mult, ["lb"], ["oml"], s2=1.0, op1=ALU.add)
        TS("dve", noml_t[:], oml_t[:], -1.0, ALU.mult, ["oml"], ["noml"])

        for l in range(L):
            for src, dst in ((wfa, wfa_bf), (wta, wta_bf)):
                S.dma("pool", dst[l], src[l].rearrange("p k c -> p (k c)"), w=[f"wbfA{l}"], chan=f"pcA{l}")
            for src, dst in ((wfb, wfb_bf), (wrb, wrb_bf), (wtb, wtb_bf)):
                S.dma("pool", dst[l], src[l].rearrange("p k c -> p (k c)"), w=[f"wbfB{l}"], chan=f"pcB{l}")
            for ci in range(18):
                S.dma("pool", wdn_bf[l, ci], wdn[l, ci], w=[f"wbfD{l}"], chan=f"pcD{l}")
        C_TRIU = c64[:, 0, :]
        C_NTRIU = c64[:, 1, :]
        C_ONES64 = c64[:, 5, :]
        C_ID64 = c64[:, 4, :]
        ID128 = c128[:, 0, :]
        ID128B = c128b[:, 0, :]
        AMASK = c128[:, 1, :]
        TRIINC_B = c128b[:, 2, :]
        BLKONES_B = c128b[:, 3, :]
        ones128b = sb("ones128b", (128, 128), BF16)
        MSET("pool", ones128b[:], 1.0, ["ones128b"])

        def phase0(sq):
            T = sq["T"]
            AR.reset()
            S.barrier(dummy[:])
            xtm = [AR.alloc((128, 1024)) for _ in range(2)]
            xfm = [AR.alloc((128, 8, 128)) for _ in range(2)]
            nt = (T + 127) // 128
            for ti in range(nt):
                n = min(128, T - ti * 128)
                a = ti % 2
                S.dma("sp", xtm[a][0:n, :], sq["x"][ti * 128:ti * 128 + n, :], w=[f"xtm{a}"], chan=f"p0l{a}")
                for b in range(8):
                    bank = (2 * a) + b // 4
                    TR(PB(bank, 0, 128, (b % 4) * 128, (b % 4) * 128 + n), xtm[a][0:n, b * 128:(b + 1) * 128], ID128[0:n, 0:n],
                       [f"xtm{a}", "c128"], [f"pb{bank}"])
                for hb in range(2):
                    bank = 2 * a + hb
                    src = pbank[bank][:].rearrange("p (b t) -> p b t", b=4)[:, :, 0:n]
                    CP("act" if hb == 0 else "dve", xfm[a][:, 4 * hb:4 * hb + 4, 0:n], src, [f"pb{bank}"], [f"xfm{a}"])
                S.dma("sp", xscr[0, :, :, ti * 128:ti * 128 + n], xfm[a][:, :, 0:n], r=[f"xfm{a}"], w=["xscr0"], chan=f"p0s{a}")

        def load_w(dst3, src2, l, key, chan):
            grp = "A" if chan.startswith("wA") else "B"
            S.dma("sp", dst3.rearrange("p k c -> p (k c)"), src2, r=[f"wbf{grp}{l}"], w=[key], chan=chan)

        def passA(sq, l):
            T = sq["T"]
            past = sq["past"]
            ntok = min(128, T)
            nch = ntok // 64
            ntile = T // ntok
            AR.reset()
            S.barrier(dummy[:])
            Wf = AR.alloc((128, 8, 1280), BF16)
            Wt = AR.alloc((128, 8, 776), BF16)
            NK = past + T
            kTs = AR.alloc((64, 4, NK), BF16)
            nkt = (NK + 127) // 128
            Vs = AR.alloc((128, nkt, 256), BF16)
            xb = AR.alloc((128, 8, ntok), BF16)
            ext = AR.alloc((128, 6, ntok + 3))
            yv = AR.alloc((128, 6, ntok))
            ys = AR.alloc((128, 6, ntok))
            vb16 = AR.alloc((128, 2, ntok), BF16)
            sqb = AR.alloc((128, 4, ntok), BF16)
            rn = AR.alloc((128, 4, ntok))
            qT2 = [AR.alloc((64, 4, ntok), BF16) for _ in range(2)]
            kT2 = [AR.alloc((64, 4, ntok), BF16) for _ in range(2)]
            qsb2 = [AR.alloc((64, 4, ntok), BF16) for _ in range(2)]
            kvout2 = [AR.alloc((128, 512)) for _ in range(2)]
            gg2 = [AR.alloc((64, nch, 256)) for _ in range(2)]
            sc8 = AR.alloc((64, nch, 8))
            bet2 = [AR.alloc((64, nch, 4)) for _ in range(2)]
            tg = AR.alloc((64, nch, 4))
            gvec2 = [AR.alloc((64, nch, 4)) for _ in range(2)]
            kvtm2 = [AR.alloc((64, nch, 512), BF16) for _ in range(2)]
            arg = AR.alloc((64, nch, 12))
            ex = AR.alloc((64, nch, 12))
            bg = AR.alloc((64, nch, 4))
            vbt = AR.alloc((64, nch, 4, 64), BF16)
            kbg = AR.alloc((64, nch, 4, 64), BF16)
            kdec = AR.alloc((64, nch, 4, 64), BF16)
            kb = AR.alloc((64, nch, 4, 64), BF16)
            kbT = AR.alloc((64, nch, 4, 64), BF16)
            absD = AR.alloc((64, nch * 256))
            G = AR.alloc((64, nch, 4, 64))
            eB = AR.alloc((64, nch, 4, 64))
            qg = AR.alloc((64, nch, 4, 64), BF16)
            Gu = AR.alloc((64, nch, 4, 64))
            nGs = AR.alloc((64, nch, 4, 64))
            nGl = AR.alloc((64, nch, 4, 64))
            PP = AR.alloc((64, 2 * nch, 4, 64)).rearrange("p (a c) h t -> p a c h t", a=2)
            ATb = AR.alloc((64, nch, 4, 64), BF16)
            TTm = AR.alloc((64, nch, 4, 64))
            TTb = AR.alloc((64, nch, 4, 64), BF16)
            Uf = AR.alloc((64, nch, 4, 64))
            WTb = AR.alloc((64, nch, 4, 64), BF16)
            vnew = AR.alloc((64, 4, 64), BF16)
            Sg = AR.alloc((64, 4, 64))
            Sgb = AR.alloc((64, 4, 64), BF16)
            O = AR.alloc((64, nch, 8, 64))
            Osq = AR.alloc((64, nch, 8, 64))
            ssq = AR.alloc((64, nch * 8))
            sgate = AR.alloc((64, nch, 256))
            Ob = AR.alloc((64, nch, 512), BF16)
            zs2 = [AR.alloc((128, 4, ntok)) for _ in range(2)]
            et = AR.alloc((128, 4, ntok))
            spb2 = [AR.alloc((128, 4, ntok), BF16) for _ in range(2)]
            Rb2 = [AR.alloc((128, 4, ntok), BF16) for _ in range(2)]
            tf = AR.alloc((128, 4, ntok))
            af = AR.alloc((128, 4, ntok))
            ab2 = [AR.alloc((128, 4, ntok), BF16) for _ in range(2)]
            if sq["sample"]:
                ktm = [AR.alloc((128, 256)) for _ in range(2)]

            load_w(Wf, wfa_bf[l], l, "Wf", "wA0")
            load_w(Wt, wta_bf[l], l, "Wt", "wA1")
            cw = convw_t[:, l, :].rearrange("p (b j) -> p b j", b=6)
            if sq["sample"]:
                S.dma("sp", Sg[:], gS_in[l].rearrange("h k v -> k h v"), w=["Sg"], chan="stA")
                for b in range(6):
                    S.dma("sp", ext[:, b, 0:3], sconv[l, :, b * 128:(b + 1) * 128].rearrange("r p -> p r"), w=["ext"], chan="stB", nc_ok=True)
                S.dma("pool", Vs[:, 0:PAST // 128, :], cv[l].rearrange("(t p) c -> p t c", p=128), w=[f"Vs{kt_}" for kt_ in range(PAST // 128)], chan="stC")
                for kt in range(PAST // 128):
                    a = kt % 2
                    S.dma("sp", ktm[a][:], ck[l, kt * 128:(kt + 1) * 128, :], w=[f"ktm{a}"], chan=f"stK{a}")
                    for h in range(4):
                        TR(PB(2 + a, 0, 64, h * 128, (h + 1) * 128), ktm[a][:, h * 64:(h + 1) * 64], ID128, [f"ktm{a}", "c128"], [f"pb{2 + a}"])
                    CP("act" if a == 0 else "dve", kTs[:, :, kt * 128:(kt + 1) * 128],
                       pbank[2 + a][0:64, :].rearrange("p (h t) -> p h t", h=4), [f"pb{2 + a}"], [f"kTs{kt}"])
            else:
                MSET("pool", Sg[:], 0.0, ["Sg"])
                MSET("pool", ext[:, :, 0:3], 0.0, ["ext"])
            CP("dve", Sgb[:], Sg[:], ["Sg"], ["Sgb"])

            stop_at("A0a")
            def pre_stream(ti):
                p = ti % 2
                pos0 = ti * ntok
                kt_new = (past + pos0) // 128
                pr0 = (past + pos0) % 128
                pos0 = ti * ntok
                S.dma("pool", xb[:], xscr[l % 2, :, :, pos0:pos0 + ntok], r=[f"xscr{l % 2}"], w=["xb"], chan="xbA")
                stop_at("A0b")
                nblk_bank = 512 // ntok
                groups = [(0, 4), (4, 8), (8, 10)]
                for gi, (b0, b1) in enumerate(groups):
                    bank = gi % 2
                    for b in range(b0, b1):
                        for k in range(8):
                            MM(PB(bank, 0, 128, (b - b0) * ntok, (b - b0 + 1) * ntok), Wf[:, k, b * 128:(b + 1) * 128], xb[:, k, :],
                               ["Wf", "xb"], [f"pb{bank}"], start=(k == 0), stop=(k == 7))
                    pv = pbank[bank][:, 0:(b1 - b0) * ntok].rearrange("p (b t) -> p b t", t=ntok)
                    import os
                    if os.environ.get("KDBG", "") == "noevac":
                        continue
                    if os.environ.get("KDBG", "") == f"only{gi}":
                        stop_at("A0c")
                    if gi == 0:
                        CP("act", ext[:, 0:4, 3:3 + ntok], pv, [f"pb{bank}"], ["ext"])
                    elif gi == 1:
                        dbg = os.environ.get("KDBG2", "abc")
                        if "a" in dbg:
                            CP("act", ext[:, 4:6, 3:3 + ntok], pv[:, 0:2, :], [f"pb{bank}"], ["ext"])
                        q4 = qsb2[p][:].rearrange("p (b h) t -> p b h t", b=2)
                        if "b" in dbg:
                            TS("dve", q4[:, :, 0, :], pv[0:64, 2:4, :], 0.125, ALU.mult, [f"pb{bank}"], [f"qsb{p}"])
                        if "c" in dbg:
                            TS("dve", q4[:, :, 1, :], pv[64:128, 2:4, :], 0.125, ALU.mult, [f"pb{bank}"], [f"qsb{p}"])
                    else:
                        k4 = kTs[:, :, past + pos0:past + pos0 + ntok].rearrange("p (b h) t -> p b h t", b=2)
                        CP("act", k4[:, :, 0, :], pv[0:64, 0:2, :], [f"pb{bank}"], [f"kTs{kt_new}a"])
                        CP("dve", k4[:, :, 1, :], pv[64:128, 0:2, :], [f"pb{bank}"], [f"kTs{kt_new}b"])
                stop_at("A0c")
                for k in range(8):
                    MM(PB(1, 0, ntok, 0, 512), xb[:, k, :], Wt[:, k, 0:512], ["Wt", "xb"], ["pb1"], start=(k == 0), stop=(k == 7))
                CP("act", kvout2[p][0:ntok, :], PB(1, 0, ntok, 0, 512), ["pb1"], [f"kvout{p}"])
                kt_new = (past + pos0) // 128
                pr0 = (past + pos0) % 128
                CP("dve", Vs[pr0:pr0 + ntok, kt_new, :], PB(1, 0, ntok, 256, 512), ["pb1"], [f"Vs{kt_new}"])
                S.dma("sp", sq["k"](l)[pos0:pos0 + ntok, :], kvout2[p][0:ntok, 0:256], r=[f"kvout{p}"], chan="okA", final=True)
                S.dma("sp", sq["v"](l)[pos0:pos0 + ntok, :], kvout2[p][0:ntok, 256:512], r=[f"kvout{p}"], chan="ovA", final=True)
                for k in range(8):
                    MM(PB(0, 0, ntok, 0, 264), xb[:, k, :], Wt[:, k, 512:776], ["Wt", "xb"], ["pb0"], start=(k == 0), stop=(k == 7))
                for c in range(nch):
                    CP("act", gg2[p][:, c, :], PB(0, 64 * c, 64 * c + 64, 0, 256), ["pb0"], [f"gg{p}"])
                    CP("dve", sc8[:, c, :], PB(0, 64 * c, 64 * c + 64, 256, 264), ["pb0"], ["sc8"])
                    yield
                stop_at("A1")
                for b in range(6):
                    TS("dve", yv[:, b, :], ext[:, b, 0:ntok], cw[:, b, 0:1], ALU.mult, ["ext", "convw"], [f"yv{b}"])
                for j in range(1, 4):
                    for b in range(6):
                        STT(yv[:, b, :], ext[:, b, j:j + ntok], cw[:, b, j:j + 1], yv[:, b, :], ALU.mult, ALU.add,
                            ["ext", "convw", f"yv{b}"], [f"yv{b}"])
                if ti == ntile - 1:
                    for b in range(6):
                        S.dma("sp", sq["conv"](l)[:, b * 128:(b + 1) * 128].rearrange("r p -> p r"), ext[:, b, ntok:ntok + 3], r=["ext"],
                              chan="ocA", final=True, nc_ok=True)
                else:
                    CP("pool", ext[:, :, 0:3], ext[:, :, ntok:ntok + 3], ["ext"] + [f"yv{b}" for b in range(6)], ["ext"])
                    yield
                SIGM(ys[:], yv[:], [f"yv{b}" for b in range(6)], ["ys"])
                TT("dve", ys[:], ys[:], yv[:], ALU.mult, ["ys"] + [f"yv{b}" for b in range(6)], ["ys"])
                CP("pool", vb16[:], ys[:, 4:6, :], ["ys"], ["vb16"])
                ACT(sqb[:], ys[:, 0:4, :], AF.Square, ["ys"], ["sqb"])
                yield
                for b in range(4):
                    MM(PB(1, 0, 128, b * ntok, (b + 1) * ntok), BLKONES_B, sqb[:, b, :], ["sqb", "c128b"], ["pb1"])
                RSQRT(rn[:], pbank[1][:, 0:4 * ntok].rearrange("p (b t) -> p b t", b=4), EPS6, ["pb1"], ["rn"])
                yield
                q4 = qT2[p][:].rearrange("p (b h) t -> p b h t", b=2)
                k4 = kT2[p][:].rearrange("p (b h) t -> p b h t", b=2)
                for hf in range(2):
                    STT(q4[:, :, hf, :], ys[64 * hf:64 * hf + 64, 0:2, :], 0.125, rn[64 * hf:64 * hf + 64, 0:2, :], ALU.mult, ALU.mult,
                        ["ys", "rn"], [f"qT{p}"])
                    TT("dve", k4[:, :, hf, :], ys[64 * hf:64 * hf + 64, 2:4, :], rn[64 * hf:64 * hf + 64, 2:4, :], ALU.mult, ["ys", "rn"], [f"kT{p}"])
                SIGM(bet2[p][:], sc8[:, :, 0:4], ["sc8"], [f"bet{p}"])
                TT("dve", tg[:], sc8[:, :, 4:8], bc_mid(hbc_t[:, l, 4:8], nch), ALU.add, ["sc8", "hbc"], ["tg"])
                ACT(tg[:], tg[:], AF.Exp, ["tg"], ["tg"])
                ACT(tg[:], tg[:], AF.Ln, ["tg", "cst"], ["tg"], bias=ONE[0:64], scale=1.0)
                TT("dve", gvec2[p][:], tg[:], bc_mid(negA_t[:, l, :], nch), ALU.mult, ["tg", "negA"], [f"gvec{p}"])
                yield
                for c in range(nch):
                    cs = slice(64 * c, 64 * c + 64)
                    pT = PB16(3, 0, 64, 0, 256)
                    for h in range(4):
                        TR(pT[:, h * 64:(h + 1) * 64], kT2[p][:, h, cs], c64b[:, 4, :], [f"kT{p}", "c64b"], ["pb3a"])
                    for bv in range(2):
                        TR(pT[:, 256 + bv * 128:256 + (bv + 1) * 128], vb16[:, bv, cs], ID128B, ["vb16", "c128b"], ["pb3a"])
                    CP("act", kvtm2[p][:, c, :], pT, ["pb3a"], [f"kvtm{p}_{c}"])
                    yield
                yield
            def gdn_stream(ti):
                p = ti % 2
                NH_ = nch * 4
                W_ = NH_ * 64
                fl = lambda ap: ap.rearrange("p c h t -> p (c h t)")
                kv = kvtm2[p]
                ktm4 = kv[:, :, 0:256].rearrange("p c (h d) -> p c h d", h=4)
                vtm4 = kv[:, :, 256:512].rearrange("p c (h d) -> p c h d", h=4)
                gv = gvec2[p]
                for c in range(nch):
                    MM(PB(4, 0, 64, 8 * c, 8 * c + 4), C_TRIU, gv[:, c, :], ["c64", f"gvec{p}"], ["pb4"])
                    MM(PB(4, 0, 64, 8 * c + 4, 8 * c + 8), C_ONES64, gv[:, c, :], ["c64", f"gvec{p}"], ["pb4"])
                CP("act", arg[:, :, 0:8], pbank[4][0:64, 0:8 * nch].rearrange("p (c e) -> p c e", e=8), ["pb4"], ["arg"])
                TT("dve", arg[:, :, 8:12], arg[:, :, 4:8], arg[:, :, 0:4], ALU.subtract, ["arg"], ["arg"])
                ACT(ex[:], arg[:], AF.Exp, ["arg"], ["ex"])
                yield
                TT("dve", bg[:], bet2[p][:], ex[:, :, 0:4], ALU.mult, [f"bet{p}", "ex"], ["bg"])
                b3 = lambda a3: a3.unsqueeze(3).to_broadcast([64, nch, 4, 64])
                TT("pool", vbt[:], vtm4, b3(bet2[p][:]), ALU.mult, [f"kvtm{p}_{c_}" for c_ in range(nch)] + [f"bet{p}"], ["vbt"])
                TT("dve", kb[:], ktm4, b3(bet2[p][:]), ALU.mult, [f"kvtm{p}_{c_}" for c_ in range(nch)] + [f"bet{p}"], ["kb"])
                TT("dve", kbg[:], ktm4, b3(bg[:]), ALU.mult, [f"kvtm{p}_{c_}" for c_ in range(nch)] + ["bg"], ["kbg"])
                TT("pool", kdec[:], ktm4, b3(ex[:, :, 8:12]), ALU.mult, [f"kvtm{p}_{c_}" for c_ in range(nch)] + ["ex"], ["kdec"])
                yield
                pT2 = PB16(3, 0, 64, 256, 256 + W_ // 2)
                for c in range(nch):
                    for h in range(4):
                        j = c * 4 + h
                        TR(pT2[:, j * 64:(j + 1) * 64], kb[:, c, h, :], c64b[:, 4, :], ["kb", "c64b"], ["pb3b"])
                CP("dve", fl(kbT[:]), pT2, ["pb3b"], ["kbT"])
                yield
                for c in range(nch):
                    for h in range(4):
                        j = c * 4 + h
                        gbc = gv[:, c, h:h + 1].to_broadcast([64, 64])
                        MM(PB(4, 0, 64, j * 64, (j + 1) * 64), gbc, C_TRIU, [f"gvec{p}", "c64"], ["pb4"])
                        MM(PB(5, 0, 64, j * 64, (j + 1) * 64), gbc, C_TRIU, [f"gvec{p}", "c64"], ["pb5"], start=True, stop=False)
                        MM(PB(5, 0, 64, j * 64, (j + 1) * 64), C_NTRIU, gbc, [f"gvec{p}", "c64"], ["pb5"], start=False, stop=True)
                CP("dve", absD[:], PB(5, 0, 64, 0, W_), ["pb5"], ["absD"])
                STT(absD[:], absD[:], -1.0, absD[:], ALU.mult, ALU.min, ["absD"], ["absD"])
                yield
                ACT(fl(G[:]), absD[:], AF.Exp, ["absD"], ["G"])
                ACT(fl(eB[:]), PB(4, 0, 64, 0, W_), AF.Exp, ["pb4"], ["eB"])
                qTv = qT2[p][:].rearrange("p h (c t) -> p c h t", t=64)
                TT("dve", qg[:], qTv, eB[:], ALU.mult, [f"qT{p}", "eB"], ["qg"])
                yield
                G3 = G[:].rearrange("p c h t -> p (c h) t")
                TT("dve", Gu[:].rearrange("p c h t -> p (c h) t"), G3, bc_mid(C_TRIU, NH_), ALU.mult, ["G", "c64"], ["Gu"])
                TT("dve", nGs[:].rearrange("p c h t -> p (c h) t"), G3, bc_mid(c64[:, 2, :], NH_), ALU.mult, ["G", "c64"], ["nGs"])
                TT("dve", nGl[:].rearrange("p c h t -> p (c h) t"), G3, bc_mid(c64[:, 3, :], NH_), ALU.mult, ["G", "c64"], ["nGl"])
                yield
                for c in range(nch):
                    cs = slice(64 * c, 64 * c + 64)
                    for h in range(4):
                        j = c * 4 + h
                        MM(PB(6, 0, 64, j * 64, (j + 1) * 64), kT2[p][:, h, cs], kbT[:, c, h, :], [f"kT{p}", "kbT"], ["pb6"])
                        MM(PB(7, 0, 64, j * 64, (j + 1) * 64), kbT[:, c, h, :], kT2[p][:, h, cs], [f"kT{p}", "kbT"], ["pb7"])
                        MM(PB(4, 0, 64, j * 64, (j + 1) * 64), kT2[p][:, h, cs], qT2[p][:, h, cs], [f"kT{p}", f"qT{p}"], ["pb4"])
                TT("dve", fl(PP[:, 1]), PB(6, 0, 64, 0, W_), fl(nGs[:]), ALU.mult, ["pb6", "nGs"], ["PP1"])
                TT("dve", fl(PP[:, 0]), PB(7, 0, 64, 0, W_), fl(nGl[:]), ALU.mult, ["pb7", "nGl"], ["PP0"])
                TT("dve", fl(ATb[:]), PB(4, 0, 64, 0, W_), fl(Gu[:]), ALU.mult, ["pb4", "Gu"], ["ATb"])
                yield
                TT("dve", TTm[:].rearrange("p c h t -> p (c h) t"), PP[:, 1].rearrange("p c h t -> p (c h) t"), bc_mid(C_ID64, NH_), ALU.add,
                   ["PP1", "c64"], ["TTm"])
                yield
                for lev in range(1, 6):
                    for c in range(nch):
                        for h in range(4):
                            j = c * 4 + h
                            MM(PB(5, 0, 64, j * 64, (j + 1) * 64), PP[:, 1, c, h, :], PP[:, 0, c, h, :], ["PP0", "PP1"], ["pb5"])
                            if lev < 5:
                                MM(PB(6, 0, 64, j * 64, (j + 1) * 64), PP[:, 0, c, h, :], PP[:, 1, c, h, :], ["PP0", "PP1"], ["pb6"])
                    CP("act", fl(PP[:, 0]), PB(5, 0, 64, 0, W_), ["pb5"], ["PP0"])
                    if lev < 5:
                        CP("dve", fl(PP[:, 1]), PB(6, 0, 64, 0, W_), ["pb6"], ["PP1"])
                    yield
                    for c in range(nch):
                        for h in range(4):
                            j = c * 4 + h
                            MM(PB(7, 0, 64, j * 64, (j + 1) * 64), PP[:, 0, c, h, :], TTm[:, c, h, :], ["PP0", "TTm"], ["pb7"])
                    TT("dve", fl(TTm[:]), fl(TTm[:]), PB(7, 0, 64, 0, W_), ALU.add, ["TTm", "pb7"], ["TTm"])
                    yield
                CP("act", TTb[:], TTm[:], ["TTm"], ["TTb"])
                yield
                for c in range(nch):
                    for h in range(4):
                        j = c * 4 + h
                        MM(PB(4, 0, 64, j * 64, (j + 1) * 64), TTb[:, c, h, :], vbt[:, c, h, :], ["TTb", "vbt"], ["pb4"])
                        MM(PB(5, 0, 64, j * 64, (j + 1) * 64), kbg[:, c, h, :], TTb[:, c, h, :], ["TTb", "kbg"], ["pb5"])
                CP("act", fl(Uf[:]), PB(4, 0, 64, 0, W_), ["pb4"], ["Uf"])
                CP("dve", fl(WTb[:]), PB(5, 0, 64, 0, W_), ["pb5"], ["WTb"])
                yield
                f3 = lambda ap: ap.rearrange("p h t -> p (h t)")
                for c in range(nch):
                    for h in range(4):
                        MM(PB(6, 0, 64, h * 64, (h + 1) * 64), WTb[:, c, h, :], Sgb[:, h, :], ["WTb", "Sgb"], ["pb6"])
                    TT("dve", f3(vnew[:]), f3(Uf[:, c]), PB(6, 0, 64, 0, 256), ALU.subtract, ["Uf", "pb6"], ["vnew"])
                    yield
                    for h in range(4):
                        MM(PB(7, 0, 64, h * 64, (h + 1) * 64), ATb[:, c, h, :], vnew[:, h, :], ["ATb", "vnew"], ["pb7"], start=True, stop=False)
                        MM(PB(7, 0, 64, h * 64, (h + 1) * 64), qg[:, c, h, :], Sgb[:, h, :], ["qg", "Sgb"], ["pb7"], start=False, stop=True)
                    for h in range(4):
                        MM(PB(6, 0, 64, 256 + h * 64, 256 + (h + 1) * 64), kdec[:, c, h, :], vnew[:, h, :], ["kdec", "vnew"], ["pb6"])
                    CP("act", f3(O[:, c, 0:4, :]), PB(7, 0, 64, 0, 256), ["pb7"], ["O"])
                    TT("dve", Sg[:], Sg[:], bc_last(ex[:, c, 4:8], 64), ALU.mult, ["Sg", "ex"], ["Sg"])
                    TT("dve", f3(Sg[:]), f3(Sg[:]), PB(6, 0, 64, 256, 512), ALU.add, ["Sg", "pb6"], ["Sg"])
                    CP("act", Sgb[:], Sg[:], ["Sg"], ["Sgb"])
                    yield
            def attn_stream(ti):
                p = ti % 2
                pos0 = ti * ntok
                kt_new = (past + pos0) // 128
                pr0 = (past + pos0) % 128
                kts = [(kt_new, ntok, True)] + [(kt, 128, False) for kt in range(kt_new - 1, -1, -1)]
                nkts = len(kts)
                W4 = 4 * ntok

                def stage1(i):
                    kt, nk, diag = kts[i]
                    a = i % 2
                    for h in range(4):
                        MM(PB(0, 0, nk, h * ntok, (h + 1) * ntok), kTs[:, h, kt * 128:kt * 128 + nk], qsb2[p][:, h, :], [f"kTs{kt}", f"kTs{kt}a", f"kTs{kt}b", f"qsb{p}"], ["pb0"])
                    pz = pbank[0][0:nk, 0:W4].rearrange("p (h t) -> p h t", h=4)
                    CP("dve", zs2[a][0:nk], pz, ["pb0"], [f"zs{a}"])
                    ACT(et[0:nk], pz, AF.Exp, ["pb0"], ["et"])
                    ACT(spb2[a][0:nk], et[0:nk], AF.Ln, ["et", "cst"], [f"spb{a}"], bias=ONE[0:nk], scale=1.0)
                    if diag:
                        TT("pool", spb2[a][0:nk], spb2[a][0:nk], bc_mid(AMASK[0:nk, 0:ntok], 4), ALU.mult, [f"spb{a}", "c128"], [f"spb{a}"])

                def stage2(i):
                    kt, nk, diag = kts[i]
                    a = i % 2
                    first = i == 0
                    last = i == nkts - 1
                    pa = pbank[1][0:nk, 0:W4]
                    MM(pa, TRIINC_B[0:nk, 0:nk], spb2[a][0:nk].rearrange("p h t -> p (h t)"), [f"spb{a}", "c128b"], ["pb1"], start=True, stop=first)
                    if not first:
                        MM(pa, ones128b[:, 0:nk], Rb2[a][:].rearrange("p h t -> p (h t)"), [f"Rb{a}", "ones128b"], ["pb1"], start=False, stop=True)
                    TT("dve", tf[0:nk].rearrange("p h t -> p (h t)"), zs2[a][0:nk].rearrange("p h t -> p (h t)"), pa, ALU.subtract,
                       [f"zs{a}", "pb1"], ["tf"])
                    if diag:
                        ACT(af[0:nk], tf[0:nk], AF.Exp, ["tf"], ["af"])
                        TT("pool", ab2[a][0:nk], af[0:nk], bc_mid(AMASK[0:nk, 0:ntok], 4), ALU.mult, ["af", "c128"], [f"ab{a}"])
                    else:
                        ACT(ab2[a][0:nk], tf[0:nk], AF.Exp, ["tf"], [f"ab{a}"])
                    if not last:
                        nb_ = (i + 1) % 2
                        if first:
                            if nk < 128:
                                MSET("pool", Rb2[nb_][:], 0.0, [f"Rb{nb_}"])
                            CP("dve", Rb2[nb_][0:nk], spb2[a][0:nk], [f"spb{a}", f"Rb{nb_}"], [f"Rb{nb_}"])
                        else:
                            TT("dve", Rb2[nb_][:], Rb2[a][:], spb2[a][:], ALU.add, [f"spb{a}", f"Rb{a}"], [f"Rb{nb_}"])
                    for h in range(4):
                        S.add("pe", lambda e, h=h, kt=kt, nk=nk, a=a, st_=(first and h == 0), sp_=last: e.matmul(
                            PB(2, 0, ntok, h * 64, (h + 1) * 64), lhsT=ab2[a][0:nk, h, :], rhs=Vs[0:nk, kt, h * 64:(h + 1) * 64],
                            start=st_, stop=sp_, skip_group_check=True), r=[f"ab{a}", f"Vs{kt}"], w=["pb2"])

                stage1(0)
                yield
                for i in range(nkts):
                    if i + 1 < nkts:
                        stage1(i + 1)
                        yield
                    stage2(i)
                    yield
                for c in range(nch):
                    CP("act", O[:, c, 4:8, :].rearrange("p h t -> p (h t)"), PB(2, 64 * c, 64 * c + 64, 0, 256), ["pb2"], ["O"])
            def merge_tile(ti):
                p = ti % 2
                pos0 = ti * ntok
                kt_new = (past + pos0) // 128
                pr0 = (past + pos0) % 128
                merge(O, Osq, ssq, Ob, nch, l, 0, pos0,
                      gate0=("silu", gg2[p], sgate, f"gg{p}", 0), gate1=None)
            for _ in pre_stream(0):
                pass
            for ti in range(ntile):
                interleave(gdn_stream(ti), attn_stream(ti))
                merge_tile(ti)
                if ti + 1 < ntile:
                    for _ in pre_stream(ti + 1):
                        pass
            S.dma("sp", sq["gdn"](l).rearrange("h k v -> k h v"), Sg[:], r=["Sg"], chan="ogA", final=True)

        def merge(O, Osq, ssq, Ob, nch, l, blk0, pos0, gate0, gate1):
            Of = O[:].rearrange("p c h d -> p (c h d)")
            ACT(Osq[:].rearrange("p c h d -> p (c h d)"), Of, AF.Square, ["O"], ["Osq"])
            S.add("dve", lambda e: e.tensor_reduce(out=ssq[:], in_=Osq[:].rearrange("p c h d -> p (c h) d"), axis=AX.X, op=ALU.add),
                  r=["Osq"], w=["ssq"])
            RSQRT(ssq[:], ssq[:], EPS5[0:64], ["ssq"], ["ssq"], scale=1.0 / 64.0)
            O3 = O[:].rearrange("p c h d -> p (c h) d")
            TT("dve", O3, O3, bc_last(ssq[:], 64), ALU.mult, ["O", "ssq"], ["O"])
            O2 = O[:].rearrange("p c h d -> p c (h d)")
            TT("pool", O2, O2, bc_mid(outg_t[:, l, blk0 * 128:blk0 * 128 + 512], nch), ALU.mult, ["O", "outg"], ["O"])
            for gi, gate in enumerate((gate0, gate1)):
                sl = slice(256 * gi, 256 * gi + 256)
                if gate is None:
                    CP("pool", Ob[:, :, sl], O2[:, :, sl], ["O"], [f"Ob{gi}"])
                else:
                    kind, gsrc, gdst, gkey, goff = gate
                    SIGM(gdst[:], gsrc[:, :, goff:goff + 256], [gkey], ["sgate"])
                    if kind == "silu":
                        TT("pool", gdst[:], gdst[:], gsrc[:, :, goff:goff + 256], ALU.mult, [gkey, "sgate"], ["sgate"])
                    TT("dve", Ob[:, :, sl], O2[:, :, sl], gdst[:], ALU.mult, ["O", "sgate"], [f"Ob{gi}"])
            for c in range(nch):
                pT = PB16(3, 0, 128, 0, 128)
                for b in range(4):
                    TR(pT[:, b * 64:(b + 1) * 64], Ob[:, c, b * 128:(b + 1) * 128], c64b[:, 4, :], ["Ob0", "Ob1", "c64b"], ["pb3a"])
                CP("act", oT[:, blk0:blk0 + 4, pos0 + 64 * c:pos0 + 64 * c + 64], pT.rearrange("p (b t) -> p b t", b=4), ["pb3a"], ["oT"])

        def passB(sq, l):
            T = sq["T"]
            ntok = min(128, T)
            nch = ntok // 64
            ntile = T // ntok
            AR.reset()
            S.barrier(dummy[:])
            Wf = AR.alloc((128, 8, 1024), BF16)
            Wr = AR.alloc((128, 8, 8), BF16)
            Wt = AR.alloc((128, 8, 1280), BF16)
            xb = AR.alloc((128, 8, ntok), BF16)
            mqT = AR.alloc((64, 4, ntok), BF16)
            mkT = AR.alloc((64, 4, ntok), BF16)
            hqf = AR.alloc((64, 2, 4, ntok))
            ri = AR.alloc((4, ntok))
            rf = AR.alloc((4, ntok))
            cl = AR.alloc((4, ntok))
            wv = AR.alloc((4, ntok))
            wsr = AR.alloc((4, ntok))
            gfr = AR.alloc((4, ntok))
            wmax = AR.alloc((4, nch))
            blast = AR.alloc((4, nch))
            mcur = AR.alloc((4, nch))
            Mc = AR.alloc((4, nch))
            mprev = AR.alloc((4, nch))
            decr = AR.alloc((4, nch))
            decd = AR.alloc((4, nch, 4))
            mstate = AR.alloc((4, 1))
            wsg = AR.alloc((64, nch, 8))
            decbc = AR.alloc((64, nch, 4))
            mktm = AR.alloc((64, nch, 256), BF16)
            V1 = AR.alloc((64, nch, 4, 65), BF16)
            V2 = AR.alloc((64, 4, 65), BF16)
            hgv = AR.alloc((64, nch, 256), BF16)
            gmo = AR.alloc((64, nch, 256))
            ghg = AR.alloc((64, nch, 256))
            sgo = AR.alloc((64, nch, 256))
            STb = AR.alloc((64, 4, 64), BF16)
            Cm = AR.alloc((64, 4, 65))
            Cmb = AR.alloc((64, 4, 65), BF16)
            aden = AR.alloc((64, 4))
            rec = AR.alloc((64, 4))
            hC = AR.alloc((64, 4, 64))
            qsil = AR.alloc((64, 4, ntok))
            sig = AR.alloc((64, 4, ntok))
            logf = AR.alloc((64, 4, ntok))
            kk = AR.alloc((64, 4, ntok))
            lcum = AR.alloc((64, 4, ntok))
            t1 = AR.alloc((64, 4, ntok))
            t2 = AR.alloc((64, 4, ntok))
            qs_ = AR.alloc((64, 4, ntok), BF16)
            qe_ = AR.alloc((64, 4, ntok), BF16)
            ks_ = AR.alloc((64, 4, ntok), BF16)
            kdT = AR.alloc((64, 4, ntok), BF16)
            kdtm = AR.alloc((64, nch, 4, 64), BF16)
            el = AR.alloc((64, 4, nch))
            aTb = AR.alloc((64, 4, 64), BF16)
            Sh = AR.alloc((64, 4, 64))
            Shb = AR.alloc((64, 4, 64), BF16)
            O = AR.alloc((64, nch, 8, 64))
            Osq = AR.alloc((64, nch, 8, 64))
            ssq = AR.alloc((64, nch * 8))
            sgate = AR.alloc((64, nch, 256))
            Ob = AR.alloc((64, nch, 512), BF16)

            load_w(Wf, wfb_bf[l], l, "Wf", "wB0")
            load_w(Wr, wrb_bf[l], l, "Wr", "wB2")
            load_w(Wt, wtb_bf[l], l, "Wt", "wB1")
            if sq["sample"]:
                S.dma("sp", Cm[:, :, 0:64], mC_in[l].rearrange("h k v -> k h v"), w=["Cm"], chan="stA")
                S.dma("sp", Cm[:, :, 64:65], mn_in[l].rearrange("h (k o) -> k h o", o=1), w=["Cm"], chan="stB", nc_ok=True)
                S.dma("sp", mstate[:], mm_in[l].rearrange("(h o) -> h o", o=1), w=["mstate"], chan="stC", nc_ok=True)
                S.dma("sp", Sh[:], hS_in[l].rearrange("h k v -> k h v"), w=["Sh"], chan="stD")
            else:
                MSET("pool", Cm[:], 0.0, ["Cm"])
                MSET("pool", mstate[:], 0.0, ["mstate"])
                MSET("pool", Sh[:], 0.0, ["Sh"])
            CP("dve", Shb[:], Sh[:], ["Sh"], ["Shb"])
            MSET("pool", V1[:], 1.0, ["V1"])
            bi_col = hrow_t[:, l, 2:3]
            nbf_col = nbf_t[:, l, :]

            for ti in range(ntile):
                pos0 = ti * ntok
                S.dma("pool", xb[:], xscr[l % 2, :, :, pos0:pos0 + ntok], r=[f"xscr{l % 2}"], w=["xb"], chan="xbA")
                for gi in range(2):
                    bank = gi
                    for bb in range(4):
                        b = gi * 4 + bb
                        for k in range(8):
                            MM(PB(bank, 0, 128, bb * ntok, (bb + 1) * ntok), Wf[:, k, b * 128:(b + 1) * 128], xb[:, k, :],
                               ["Wf", "xb"], [f"pb{bank}"], start=(k == 0), stop=(k == 7))
                    pv = pbank[bank][:, 0:4 * ntok].rearrange("p (b t) -> p b t", t=ntok)
                    if gi == 0:
                        q4 = mqT[:].rearrange("p (b h) t -> p b h t", b=2)
                        k4 = mkT[:].rearrange("p (b h) t -> p b h t", b=2)
                        for hf in range(2):
                            CP("act", q4[:, :, hf, :], pv[64 * hf:64 * hf + 64, 0:2, :], ["pb0"], ["mqT"])
                            TS("dve", k4[:, :, hf, :], pv[64 * hf:64 * hf + 64, 2:4, :], 0.125, ALU.mult, ["pb0"], ["mkT"])
                    else:
                        for a in range(2):
                            d4 = hqf[:, a].rearrange("p (b h) t -> p b h t", b=2)
                            for hf in range(2):
                                CP("act" if hf == 0 else "dve", d4[:, :, hf, :], pv[64 * hf:64 * hf + 64, 2 * a:2 * a + 2, :], ["pb1"], ["hqf"])
                for a in range(2):
                    for k in range(8):
                        MM(PB(2, 0, 4, a * ntok, (a + 1) * ntok), Wr[:, k, 4 * a:4 * a + 4], xb[:, k, :], ["Wr", "xb"], ["pb2"],
                           start=(k == 0), stop=(k == 7))
                CP("act", ri[:], PB(2, 0, 4, 0, ntok), ["pb2"], ["ri"])
                ACT(rf[:], PB(2, 0, 4, ntok, 2 * ntok), AF.Exp, ["pb2", "nbf"], ["rf"], bias=nbf_col, scale=-1.0)
                ACT(rf[:], rf[:], AF.Ln, ["rf", "cst"], ["rf"], bias=ONE[0:4], scale=1.0)
                for k in range(8):
                    MM(PB(0, 0, ntok, 0, 512), xb[:, k, :], Wt[:, k, 0:512], ["Wt", "xb"], ["pb0"], start=(k == 0), stop=(k == 7))
                for c in range(nch):
                    TS("dve", mktm[:, c, :], PB(0, 64 * c, 64 * c + 64, 0, 256), 0.125, ALU.mult, ["pb0"], ["mktm"])
                    CP("act", V1[:, c, :, 0:64], PB(0, 64 * c, 64 * c + 64, 256, 512).rearrange("p (h d) -> p h d", h=4), ["pb0"], ["V1"])
                for k in range(8):
                    MM(PB(1, 0, ntok, 0, 512), xb[:, k, :], Wt[:, k, 512:1024], ["Wt", "xb"], ["pb1"], start=(k == 0), stop=(k == 7))
                for c in range(nch):
                    CP("dve", hgv[:, c, :], PB(1, 64 * c, 64 * c + 64, 0, 256), ["pb1"], ["hgv"])
                    CP("act", gmo[:, c, :], PB(1, 64 * c, 64 * c + 64, 256, 512), ["pb1"], ["gmo"])
                for k in range(8):
                    MM(PB(0, 0, ntok, 0, 256), xb[:, k, :], Wt[:, k, 1024:1280], ["Wt", "xb"], ["pb0"], start=(k == 0), stop=(k == 7))
                for c in range(nch):
                    CP("act", ghg[:, c, :], PB(0, 64 * c, 64 * c + 64, 0, 256), ["pb0"], ["ghg"])
                SIGM(sgo[:], gmo[:], ["gmo"], ["sgo"])
                def rows_stream():
                    S.add("dve", lambda e: e.tensor_tensor_scan(out=cl[:], data0=rmask[0:4, 0:ntok], data1=rf[:], initial=0.0, op0=ALU.mult, op1=ALU.add),
                          r=["rf", "rmask"], w=["cl"])
                    yield
                    STT(wv[:], ri[:], bi_col, cl[:], ALU.add, ALU.add, ["ri", "cl", "hrow"], ["wv"])
                    yield
                    S.add("dve", lambda e: e.tensor_reduce(out=wmax[:], in_=wv[:].rearrange("p (c t) -> p c t", t=64), axis=AX.X, op=ALU.max),
                          r=["wv"], w=["wmax"])
                    yield
                    TS("dve", blast[:], cl[:].rearrange("p (c t) -> p c t", t=64)[:, :, 63], -1.0, ALU.mult, ["cl"], ["blast"])
                    yield
                    S.add("dve", lambda e: e.tensor_tensor_scan(out=mcur[:], data0=wmax[:], data1=blast[:], initial=mstate[:], op0=ALU.max, op1=ALU.add),
                          r=["wmax", "blast", "mstate"], w=["mcur"])
                    yield
                    TT("dve", Mc[:], mcur[:], blast[:], ALU.subtract, ["mcur", "blast"], ["Mc"])
                    yield
                    CP("dve", mprev[:, 0:1], mstate[:], ["mstate"], ["mprev"])
                    yield
                    if nch > 1:
                        CP("dve", mprev[:, 1:nch], mcur[:, 0:nch - 1], ["mcur", "mprev"], ["mprev"])
                        yield
                    CP("dve", mstate[:], mcur[:, nch - 1:nch], ["mcur", "mprev"], ["mstate"])
                    yield
                    TT("dve", decr[:], mprev[:], Mc[:], ALU.subtract, ["mprev", "Mc"], ["decr"])
                    yield
                    ACT(decr[:], decr[:], AF.Exp, ["decr"], ["decr"])
                    yield
                    TT("dve", wsr[:].rearrange("p (c t) -> p c t", t=64), wv[:].rearrange("p (c t) -> p c t", t=64), bc_last(Mc[:], 64), ALU.subtract,
                       ["wv", "Mc"], ["wsr"])
                    yield
                    ACT(wsr[:], wsr[:], AF.Exp, ["wsr"], ["wsr"])
                    yield
                    TT("dve", gfr[:].rearrange("p (c t) -> p c t", t=64), cl[:].rearrange("p (c t) -> p c t", t=64), bc_last(Mc[:], 64), ALU.subtract,
                       ["cl", "Mc"], ["gfr"])
                    yield
                    ACT(gfr[:], gfr[:], AF.Exp, ["gfr"], ["gfr"])
                    yield
                    for c in range(nch):
                        TR(PB(2, 0, 64, 8 * c, 8 * c + 4), wsr[:, 64 * c:64 * c + 64], i4[:], ["wsr", "i4"], ["pb2"])
                        yield
                        TR(PB(2, 0, 64, 8 * c + 4, 8 * c + 8), gfr[:, 64 * c:64 * c + 64], i4[:], ["gfr", "i4"], ["pb2"])
                        yield
                    CP("act", wsg[:].rearrange("p c e -> p (c e)"), PB(2, 0, 64, 0, 8 * nch), ["pb2"], ["wsg"])
                    yield
                    TT("dve", decd[:], bc_last(decr[:], 4), bc_mid(i4[:], nch), ALU.mult, ["decr", "i4"], ["decd"])
                    yield
                    MM(PB(2, 0, 64, 64, 64 + 4 * nch), c64[0:4, 5, :], decd[:].rearrange("p c h -> p (c h)"), ["decd", "c64"], ["pb2b"])
                    yield
                    CP("act", decbc[:].rearrange("p c h -> p (c h)"), PB(2, 0, 64, 64, 64 + 4 * nch), ["pb2b"], ["decbc"])
                    yield
                def hpre_stream():
                    SIGM(qsil[:], hqf[:, 0], ["hqf"], ["qsil"])
                    TT("dve", qsil[:], qsil[:], hqf[:, 0], ALU.mult, ["hqf", "qsil"], ["qsil"])
                    yield
                    SIGM(sig[:], hqf[:, 1], ["hqf"], ["sig"])
                    yield
                    for h in range(4):
                        ACT(logf[:, h, :], sig[:, h, :], AF.Ln, ["sig", "oml", "lb"], ["logf"], bias=lb_t[:, h, l:l + 1], scale=oml_t[:, h, l:l + 1])
                        yield
                        TS("pool", kk[:, h, :], sig[:, h, :], noml_t[:, h, l:l + 1], ALU.mult, ["sig", "noml", "oml"], ["kk"],
                           s2=oml_t[:, h, l:l + 1], op1=ALU.add)
                        yield
                    for h in range(4):
                        S.add("dve", lambda e, h=h: e.tensor_tensor_scan(out=lcum[:, h, :], data0=rmask[0:64, 0:ntok], data1=logf[:, h, :], initial=0.0,
                                                                         op0=ALU.mult, op1=ALU.add), r=["logf", "rmask"], w=["lcum"])
                        yield
                    lc4 = lcum[:].rearrange("p h (c t) -> p (h c) t", t=64)
                    yield
                    nhc = 4 * nch
                    yield
                    TT("dve", t1[:].rearrange("p h (c t) -> p (h c) t", t=64), lc4, bc_last(lc4[:, :, 31], 64), ALU.subtract, ["lcum"], ["t1"])
                    yield
                    ACT(t2[:], t1[:], AF.Exp, ["t1"], ["t2"])
                    yield
                    TT("dve", qs_[:], qsil[:], t2[:], ALU.mult, ["qsil", "t2"], ["qs_"])
                    yield
                    ACT(t2[:], t1[:], AF.Exp, ["t1", "qs_"], ["t2"], scale=-1.0)
                    yield
                    TT("dve", ks_[:], kk[:], t2[:], ALU.mult, ["kk", "t2"], ["ks_"])
                    yield
                    ACT(t2[:], lcum[:], AF.Exp, ["lcum", "ks_"], ["t2"])
                    yield
                    TT("dve", qe_[:], qsil[:], t2[:], ALU.mult, ["qsil", "t2"], ["qe_"])
                    yield
                    TT("dve", t1[:].rearrange("p h (c t) -> p (h c) t", t=64), bc_last(lc4[:, :, 63], 64), lc4, ALU.subtract, ["lcum", "t2"], ["t1"])
                    yield
                    ACT(t2[:], t1[:], AF.Exp, ["t1", "qe_"], ["t2"])
                    yield
                    TT("dve", kdT[:], kk[:], t2[:], ALU.mult, ["kk", "t2"], ["kdT"])
                    yield
                    ACT(el[:].rearrange("p h c -> p (h c)"), lc4[:, :, 63], AF.Exp, ["lcum"], ["el"])
                    yield
                    for c in range(nch):
                        cs = slice(64 * c, 64 * c + 64)
                        yield
                        pT = PB16(3, 0, 64, 0, 128)
                        yield
                        for h in range(4):
                            TR(pT[:, h * 64:(h + 1) * 64], kdT[:, h, cs], c64b[:, 4, :], ["kdT", "c64b"], ["pb3a"])
                            yield
                        CP("act", kdtm[:, c].rearrange("p h t -> p (h t)"), pT, ["pb3a"], ["kdtm"])
                        yield
                interleave(rows_stream(), hpre_stream())
                def mlstm_stream():
                    for c in range(nch):
                        cs = slice(64 * c, 64 * c + 64)
                        TT("pool", V2[:], V1[:, c], bc_last(wsg[:, c, 0:4], 65), ALU.mult, ["V1", "wsg"], ["V2"])
                        yield
                        for h in range(4):
                            MM(PB(4, 0, 64, h * 64, (h + 1) * 64), mkT[:, h, cs], mqT[:, h, cs], ["mkT", "mqT"], ["pb4a"])
                        TT("dve", STb[:], pbank[4][0:64, 0:256].rearrange("p (h t) -> p h t", h=4), bc_mid(C_TRIU, 4), ALU.mult, ["pb4a", "c64"], ["STb"])
                        yield
                        TT("dve", Cm[:], Cm[:], bc_last(decbc[:, c, :], 65), ALU.mult, ["Cm", "decbc"], ["Cm"])
                        CP("act", Cmb[:], Cm[:], ["Cm"], ["Cmb"])
                        yield
                        for h in range(4):
                            MM(PB(5, 0, 64, h * 65, (h + 1) * 65), STb[:, h, :], V2[:, h, :], ["STb", "V2"], ["pb5"], start=True, stop=False)
                            MM(PB(5, 0, 64, h * 65, (h + 1) * 65), mqT[:, h, cs], Cmb[:, h, :], ["mqT", "Cmb"], ["pb5"], start=False, stop=True)
                        pn = pbank[5][0:64, 0:260].rearrange("p (h e) -> p h e", h=4)
                        CP("act", hC[:], pn[:, :, 0:64], ["pb5"], ["hC"])
                        CP("act", aden[:], pn[:, :, 64], ["pb5"], ["aden"])
                        yield
                        STT(rec[:], aden[:], -1.0, aden[:], ALU.mult, ALU.max, ["aden"], ["rec"])
                        TT("dve", rec[:], rec[:], wsg[:, c, 4:8], ALU.max, ["rec", "wsg"], ["rec"])
                        S.add("dve", lambda e: e.reciprocal(out=rec[:], in_=rec[:]), r=["rec"], w=["rec"])
                        TT("dve", hC[:], hC[:], bc_last(rec[:], 64), ALU.mult, ["hC", "rec"], ["hC"])
                        TT("pool", O[:, c, 0:4, :], hC[:], sgo[:, c, :].rearrange("p (h d) -> p h d", h=4), ALU.mult, ["hC", "sgo"], ["O"])
                        yield
                        for h in range(4):
                            MM(PB(6, 0, 64, h * 65, (h + 1) * 65), mktm[:, c, h * 64:(h + 1) * 64], V2[:, h, :], ["mktm", "V2"], ["pb6"])
                        TT("dve", Cm[:], Cm[:], pbank[6][0:64, 0:260].rearrange("p (h e) -> p h e", h=4), ALU.add, ["Cm", "pb6", "Cmb"], ["Cm"])
                        yield
                def hgrn_stream():
                    for c in range(nch):
                        cs = slice(64 * c, 64 * c + 64)
                        for h in range(4):
                            MM(PB(2, 0, 64, h * 64, (h + 1) * 64), ks_[:, h, cs], qs_[:, h, cs], ["ks_", "qs_"], ["pb2"])
                        TT("dve", aTb[:], pbank[2][0:64, 0:256].rearrange("p (h t) -> p h t", h=4), bc_mid(C_TRIU, 4), ALU.mult, ["pb2", "c64"], ["aTb"])
                        yield
                        for h in range(4):
                            MM(PB(7, 0, 64, h * 64, (h + 1) * 64), aTb[:, h, :], hgv[:, c, h * 64:(h + 1) * 64], ["aTb", "hgv"], ["pb7a"], start=True, stop=False)
                            MM(PB(7, 0, 64, h * 64, (h + 1) * 64), qe_[:, h, cs], Shb[:, h, :], ["qe_", "Shb"], ["pb7a"], start=False, stop=True)
                        CP("act", O[:, c, 4:8, :].rearrange("p h t -> p (h t)"), PB(7, 0, 64, 0, 256), ["pb7a"], ["O"])
                        yield
                        for h in range(4):
                            MM(PB(7, 0, 64, 256 + h * 64, 256 + (h + 1) * 64), kdtm[:, c, h, :], hgv[:, c, h * 64:(h + 1) * 64], ["kdtm", "hgv"], ["pb7b"])
                        TT("dve", Sh[:], Sh[:], bc_last(el[:, :, c], 64), ALU.mult, ["Sh", "el", "Shb"], ["Sh"])
                        TT("dve", Sh[:].rearrange("p h t -> p (h t)"), Sh[:].rearrange("p h t -> p (h t)"), PB(7, 0, 64, 256, 512), ALU.add, ["Sh", "pb7b"], ["Sh"])
                        CP("act", Shb[:], Sh[:], ["Sh"], ["Shb"])
                        yield
                interleave(mlstm_stream(), hgrn_stream())
                merge(O, Osq, ssq, Ob, nch, l, 4, pos0, gate0=None, gate1=("sigmoid", ghg, sgate, "ghg", 0))
            S.dma("sp", sq["c"](l).rearrange("h k v -> k h v"), Cm[:, :, 0:64], r=["Cm"], chan="osA", final=True)
            S.dma("sp", sq["n"](l).rearrange("h (k o) -> k h o", o=1), Cm[:, :, 64:65], r=["Cm"], chan="osB", final=True, nc_ok=True)
            S.dma("sp", sq["m"](l).rearrange("(h o) -> h o", o=1), mstate[:], r=["mstate"], chan="osC", final=True, nc_ok=True)
            S.dma("sp", sq["hg"](l).rearrange("h k v -> k h v"), Sh[:], r=["Sh"], chan="osD", final=True)

        NW = 8

        def phaseD(sq, l, last_layer):
            T = sq["T"]
            nd = min(512, T)
            ntile = T // nd
            AR.reset()
            S.barrier(dummy[:])
            ring = [AR.alloc((128, 4096), BF16) for _ in range(NW)]
            rx = AR.alloc((128, 8, nd))
            x1b = AR.alloc((128, 8, nd), BF16)
            sqb = AR.alloc((128, 8, nd), BF16)
            hT = AR.alloc((128, 32, nd), BF16)
            tmp = [AR.alloc((128, nd)) for _ in range(2)]
            mean_sb = AR.alloc((128, nd))
            rstd = AR.alloc((128, nd))
            ytm = AR.alloc((128, 1024)) if last_layer else None
            vec = vec_t[:, l, :].rearrange("p (a b) -> p a b", a=5)
            total = ntile * 18
            state = dict(next=0)

            def fetch():
                i = state["next"]
                if i >= total:
                    return
                slot = i % NW
                ci = i % 18
                S.dma("sp", ring[slot][:], wdn_bf[l, ci], r=[f"wbfD{l}"], w=[f"ring{slot}"], chan=f"rg{slot}")
                state["next"] = i + 1

            for _ in range(NW):
                fetch()
            used = dict(n=0)

            def chunk():
                i = used["n"]
                used["n"] = i + 1
                return i % NW

            def layernorm(gi, bi_):
                CP("act", x1b[:], rx[:], ["rx"], ["x1b"])
                ACT(sqb[:], rx[:], AF.Square, ["rx"], ["sqb"])
                for m in range(8):
                    MM(PB(2, 0, 128, 0, nd), odiv[:], x1b[:, m, :], ["odiv", "x1b"], ["pb2"], start=(m == 0), stop=(m == 7))
                for m in range(8):
                    MM(PB(3, 0, 128, 0, nd), odiv[:], sqb[:, m, :], ["odiv", "sqb"], ["pb3"], start=(m == 0), stop=(m == 7))
                CP("act", mean_sb[:], PB(2, 0, 128, 0, nd), ["pb2"], ["mean"])
                ACT(rstd[:], PB(2, 0, 128, 0, nd), AF.Square, ["pb2"], ["rstd"])
                TT("dve", rstd[:], PB(3, 0, 128, 0, nd), rstd[:], ALU.subtract, ["pb3", "rstd"], ["rstd"])
                RSQRT(rstd[:], rstd[:], EPS5, ["rstd"], ["rstd"])
                for m in range(8):
                    t = tmp[m % 2]
                    tk = f"tmp{m % 2}"
                    TT("dve", t[:], rx[:, m, :], mean_sb[:], ALU.subtract, ["rx", "mean"], [tk])
                    TT("pool", t[:], t[:], rstd[:], ALU.mult, [tk, "rstd"], [tk])
                    ACT(rx[:, m, :], t[:], AF.Identity, [tk, "vec"], ["rx"], bias=vec[:, bi_, m:m + 1], scale=vec[:, gi, m:m + 1])
                    CP("pool", x1b[:, m, :], rx[:, m, :], ["rx"], ["x1b"])

            for ti in range(ntile):
                pos0 = ti * nd
                S.dma("sp", rx[:], xscr[l % 2, :, :, pos0:pos0 + nd], r=[f"xscr{l % 2}"], w=["rx"], chan="rxD")
                for j in range(2):
                    slot = chunk()
                    wv_ = ring[slot][:].rearrange("p (k c) -> p k c", k=8)
                    for mm_ in range(4):
                        m = 4 * j + mm_
                        bank = m % 2
                        for k in range(8):
                            MM(PB(bank, 0, 128, 0, nd), wv_[:, k, mm_ * 128:(mm_ + 1) * 128], oT[:, k, pos0:pos0 + nd], [f"ring{slot}", "oT"], [f"pb{bank}"],
                               start=(k == 0), stop=(k == 7))
                        STT(rx[:, m, :], rx[:, m, :], DN_ALPHA, PB(bank, 0, 128, 0, nd), ALU.mult, ALU.add, ["rx", f"pb{bank}"], ["rx"])
                    fetch()
                layernorm(1, 2)
                for j in range(8):
                    slot = chunk()
                    wv_ = ring[slot][:].rearrange("p (k c) -> p k c", k=8)
                    for mm_ in range(4):
                        m = 4 * j + mm_
                        bank = m % 2
                        for k in range(8):
                            MM(PB(bank, 0, 128, 0, nd), wv_[:, k, mm_ * 128:(mm_ + 1) * 128], x1b[:, k, :], [f"ring{slot}", "x1b"], [f"pb{bank}"],
                               start=(k == 0), stop=(k == 7))
                        t = tmp[m % 2]
                        tk = f"tmp{m % 2}"
                        ACT(t[:], PB(bank, 0, 128, 0, nd), AF.Relu, [f"pb{bank}", "b1"], [tk], bias=b1_t[:, l, m:m + 1], scale=1.0)
                        TT("pool", hT[:, m, :], t[:], t[:], ALU.mult, [tk], ["hT"])
                    fetch()
                for m in range(8):
                    slot = chunk()
                    wv_ = ring[slot][:].rearrange("p (k c) -> p k c", k=32)
                    bank = m % 2
                    for k in range(32):
                        MM(PB(bank, 0, 128, 0, nd), wv_[:, k, :], hT[:, k, :], [f"ring{slot}", "hT"], [f"pb{bank}"], start=(k == 0), stop=(k == 31))
                    t = tmp[m % 2]
                    tk = f"tmp{m % 2}"
                    ACT(t[:], PB(bank, 0, 128, 0, nd), AF.Identity, [f"pb{bank}", "vec"], [tk], bias=vec[:, 0, m:m + 1], scale=1.0)
                    STT(rx[:, m, :], rx[:, m, :], DN_ALPHA, t[:], ALU.mult, ALU.add, ["rx", tk], ["rx"])
                    fetch()
                layernorm(3, 4)
                if not last_layer:
                    S.dma("sp", xscr[(l + 1) % 2, :, :, pos0:pos0 + nd], rx[:], r=["rx"], w=[f"xscr{(l + 1) % 2}"], chan="xoD")
                else:
                    nsub = (nd + 127) // 128
                    for su in range(nsub):
                        ns = min(128, nd - su * 128)
                        for m in range(8):
                            bank = 4 + m // 4
                            TR(PB(bank, 0, ns, (m % 4) * 128, (m % 4 + 1) * 128), rx[:, m, su * 128:su * 128 + ns], ID128, ["rx", "c128"], [f"pb{bank}"])
                        CP("act", ytm[0:ns, 0:512], PB(4, 0, ns, 0, 512), ["pb4"], ["ytm"])
                        CP("dve", ytm[0:ns, 512:1024], PB(5, 0, ns, 0, 512), ["pb5"], ["ytm"])
                        S.dma("sp", sq["y"][pos0 + su * 128:pos0 + su * 128 + ns, :], ytm[0:ns, :], r=["ytm"], chan="yD", final=True)

        import os
        kstop = os.environ.get("KSTOP", "")
        try:
            for sq in seqs:
                phase0(sq)
                stop_at("p0")
                for l in range(L):
                    passA(sq, l)
                    stop_at("A")
                    passB(sq, l)
                    if KDEBUG and l == 0 and sq is seqs[0]:
                        S.dma("sp", dbg_oT, oT[:], r=["oT"], chan="dbgo", final=True)
                    stop_at("B")
                    phaseD(sq, l, l == L - 1)
                    stop_at("D")
        except StopBuild as e:
            print("build stopped at", e)
        print('arena high water', AR.hi, 'of', AW)
        S.emit()
    return nc


def host_layout_weights(inp, L):
    w_in = np.asarray(inp["w_in"], np.float32)[:L]

    def fm(cols):
        w = w_in[:, :, cols]
        return np.ascontiguousarray(w.reshape(L, 8, 128, len(cols)).transpose(0, 2, 1, 3))

    out = dict(wfa=fm(FA_COLS), wta=fm(TA_COLS), wfb=fm(FB_COLS), wrb=fm(RB_COLS), wtb=fm(TB_COLS))
    w_out = np.asarray(inp["w_out"], np.float32)[:L]
    w1 = np.asarray(inp["w1"], np.float32)[:L]
    w2 = np.asarray(inp["w2"], np.float32)[:L]
    wdn = np.empty((L, 18, 128, 4096), np.float32)
    wo = w_out.reshape(L, 8, 128, 2, 512).transpose(0, 3, 2, 1, 4)
    wdn[:, 0:2] = wo.reshape(L, 2, 128, 4096)
    w1r = w1.reshape(L, 8, 128, 8, 512).transpose(0, 3, 2, 1, 4)
    wdn[:, 2:10] = w1r.reshape(L, 8, 128, 4096)
    w2r = w2.reshape(L, 32, 128, 8, 128).transpose(0, 3, 2, 1, 4)
    wdn[:, 10:18] = w2r.reshape(L, 8, 128, 4096)
    out["wdn"] = wdn
    cw = np.asarray(inp["gdn_conv_w"], np.float32)[:L]
    out["convw"] = np.ascontiguousarray(cw.reshape(L, 4, 6, 128).transpose(0, 3, 2, 1))
    out["b1c"] = np.ascontiguousarray(np.asarray(inp["b1"], np.float32)[:L].reshape(L, 32, 128).transpose(0, 2, 1))
    vs = np.stack([np.asarray(inp[k], np.float32)[:L].reshape(L, 8, 128).transpose(0, 2, 1)
                   for k in ("b2", "ln1_g", "ln1_b", "ln2_g", "ln2_b")], axis=2)
    out["vecs"] = np.ascontiguousarray(vs)
    out["outg"] = np.ascontiguousarray(np.asarray(inp["out_norm_g"], np.float32)[:L])
    out["hvec"] = np.ascontiguousarray(np.stack([np.asarray(inp[k], np.float32)[:L] for k in
                                                 ("gdn_A_log", "gdn_dt_bias", "mlstm_b_i", "mlstm_b_f")], axis=1))
    out["lbl"] = np.ascontiguousarray(np.asarray(inp["hgrn_lb_logits"], np.float32)[:L])
    out.update(make_consts())
    return out


_PROG_CACHE = {}


def run(inp, n_cores, NPS, TP, L, SAMPLE):
    key = (NPS, TP, L, SAMPLE)
    if key not in _PROG_CACHE:
        _PROG_CACHE[key] = build(NPS, TP, L, SAMPLE)
    nc = _PROG_CACHE[key]
    shared = host_layout_weights(inp, L)
    xpr = np.asarray(inp["x_prompt"], np.float32)
    in_maps = []
    for c in range(n_cores):
        m = dict(shared)
        m["xp"] = np.ascontiguousarray(xpr[c * NPS:(c + 1) * NPS])
        if SAMPLE:
            m["xs"] = np.ascontiguousarray(np.asarray(inp["x_sample"], np.float32)[c])
            m["ck"] = np.ascontiguousarray(np.asarray(inp["cache_sb_k"], np.float32)[:L, c].reshape(L, PAST, 256))
            m["cv"] = np.ascontiguousarray(np.asarray(inp["cache_sb_v"], np.float32)[:L, c].reshape(L, PAST, 256))
            m["sconv"] = np.ascontiguousarray(np.asarray(inp["state_gdn_conv"], np.float32)[:L, c])
            m["gS"] = np.ascontiguousarray(np.asarray(inp["state_gdn_S"], np.float32)[:L, c])
            m["mC"] = np.ascontiguousarray(np.asarray(inp["state_mlstm_C"], np.float32)[:L, c])
            m["mn"] = np.ascontiguousarray(np.asarray(inp["state_mlstm_n"], np.float32)[:L, c])
            m["mm"] = np.ascontiguousarray(np.asarray(inp["state_mlstm_m"], np.float32)[:L, c])
            m["hS"] = np.ascontiguousarray(np.asarray(inp["state_hgrn_S"], np.float32)[:L, c])
        in_maps.append(m)
    res = run_bass_kernel_spmd(nc, in_maps, core_ids=list(range(n_cores)))
    R = res.results
    global LAST_RESULTS
    LAST_RESULTS = R

    def cat(name, axis):
        return np.concatenate([np.asarray(r[name], np.float32) for r in R], axis=axis)

    y_p = cat("y_p", 0)
    outs = [y_p]
    if SAMPLE:
        outs.append(np.stack([np.asarray(r["y_s"], np.float32) for r in R], 0))
    else:
        outs.append(None)
    B = n_cores * NPS
    outs += [cat("p_k", 1).reshape(L, B, TP, 4, 64), cat("p_v", 1).reshape(L, B, TP, 4, 64), cat("p_conv", 1), cat("p_gdn", 1),
             cat("p_c", 1), cat("p_n", 1), cat("p_m", 1), cat("p_hg", 1)]
    if SAMPLE:
        outs += [cat("s_k", 1).reshape(L, n_cores, 64, 4, 64), cat("s_v", 1).reshape(L, n_cores, 64, 4, 64), cat("s_conv", 1),
                 cat("s_gdn", 1), cat("s_c", 1), cat("s_n", 1), cat("s_m", 1), cat("s_hg", 1)]
    return tuple(outs)


def kernel(**inputs):
    return run(inputs, 8, 2, 2048, DEPTH, True)
```

```python
import contextlib
import os
import math
import numpy as np
import concourse.bass as bass
import concourse.mybir as mybir
from concourse.bass_utils import run_bass_kernel_spmd

F32 = mybir.dt.float32
BF16 = mybir.dt.bfloat16
ALU = mybir.AluOpType
AF = mybir.ActivationFunctionType
AX = mybir.AxisListType

D_MODEL = 1024
DEPTH = 4
PAST = 2048
DN_ALPHA = (2 * DEPTH) ** 0.25
OFF = dict(gq=0, gk=256, gv=512, ggate=768, gbeta=1024, ga=1028, sq=1032, sk=1288, sv=1544,
           mq=1800, mk=2056, mv=2312, mo=2568, mi=2824, mf=2828, hq=2832, hf=3088, hi=3344, hg=3600)


def _cols(*specs):
    out = []
    for name, n in specs:
        out.extend(range(OFF[name], OFF[name] + n))
    return np.array(out)


FA_COLS = _cols(("gq", 256), ("gk", 256), ("gv", 256), ("sq", 256), ("sk", 256))
TA_COLS = _cols(("sk", 256), ("sv", 256), ("ggate", 256), ("gbeta", 4), ("ga", 4))
FB_COLS = _cols(("mq", 256), ("mk", 256), ("hq", 256), ("hf", 256))
RB_COLS = _cols(("mi", 4), ("mf", 4))
TB_COLS = _cols(("mk", 256), ("mv", 256), ("hi", 256), ("mo", 256), ("hg", 256))

ENGS = ("pe", "act", "dve", "pool", "sp")
EPOCH = 30000


class Op:
    __slots__ = ("eng", "fn", "deps", "marked", "chan", "val", "n")

    def __init__(self, eng, fn, chan=None):
        self.eng = eng
        self.fn = fn
        self.deps = ()
        self.marked = False
        self.chan = chan
        self.val = 0
        self.n = 0


class Sched:
    def __init__(self, nc, same_sync=True):
        self.nc = nc
        self.same_sync = same_sync
        self.ops = {e: [] for e in ENGS}
        self.last_w = {}
        self.readers = {}
        self.chan_cnt = {}
        self.chan_last = {}
        self.final = []
        self.fence = None
        self.fenced = set()

    def add(self, eng, fn, r=(), w=(), chan=None):
        r2 = []
        w2 = []
        for k in r:
            if k.startswith("pb"):
                if k[:3] not in w2:
                    w2.append(k[:3])
            else:
                r2.append(k)
        for k in w:
            k = k[:3] if k.startswith("pb") else k
            if k not in w2:
                w2.append(k)
        r, w = r2, w2
        op = Op(eng, fn, chan)
        deps = set()
        raw = set()
        for k in r:
            lw = self.last_w.get(k)
            if lw is not None:
                deps.add(lw)
                raw.add(lw)
        for k in w:
            lw = self.last_w.get(k)
            if lw is not None:
                deps.add(lw)
            for rd in self.readers.get(k, ()):
                deps.add(rd)
        if self.fence is not None and eng not in self.fenced:
            deps.add(self.fence)
            self.fenced.add(eng)
        out = []
        for d in deps:
            if d.eng == eng and d.chan is None and chan is None:
                if eng == "pe" or not self.same_sync:
                    continue
                if eng in ("act", "dve") and d not in raw:
                    continue
            out.append(d)
            d.marked = True
        op.deps = out
        for k in r:
            self.readers.setdefault(k, []).append(op)
        for k in w:
            self.last_w[k] = op
            self.readers[k] = []
        if chan is not None:
            c = self.chan_cnt.get(chan, 0) + 1
            self.chan_cnt[chan] = c
            op.val = 16 * c
            op.marked = True
            self.chan_last[chan] = op
        self.ops[eng].append(op)
        return op

    def dma(self, eng, out, in_, r=(), w=(), chan=None, final=False, nc_ok=False):
        assert chan is not None
        if nc_ok:
            fn = lambda e: e.dma_start(out=out, in_=in_, allow_slow_non_contiguous=True)
        else:
            fn = lambda e: e.dma_start(out=out, in_=in_)
        op = self.add(eng, fn, r=r, w=w, chan=chan)
        if final:
            self.final.append(op)
        return op

    def barrier(self, dummy):
        lasts = [ops[-1] for ops in self.ops.values() if ops and ops[-1].chan is None]
        lasts += [o for c, o in self.chan_last.items() if not c.startswith("pc")]
        op = Op("dve", lambda e: e.memset(dummy, 0.0))
        deps = []
        for d in set(lasts):
            if d.eng == "dve" and d.chan is None:
                continue
            deps.append(d)
            d.marked = True
        op.deps = deps
        op.marked = True
        self.ops["dve"].append(op)
        self.fence = op
        self.fenced = {"dve"}
        self.last_w = {k: v for k, v in self.last_w.items() if k.startswith("wbf")}
        self.readers = {}

    def emit(self):
        nc = self.nc
        with contextlib.ExitStack() as st:
            esems = {}
            for e in ENGS:
                n = 0
                for op in self.ops[e]:
                    if op.marked and op.chan is None:
                        n += 1
                        op.n = n
                nep = (n + EPOCH - 1) // EPOCH
                esems[e] = [st.enter_context(nc.semaphore(f"s_{e}{i}")) for i in range(nep)]
            csems = {c: st.enter_context(nc.semaphore(f"c_{c}")) for c in self.chan_cnt}

            def semval(d):
                if d.chan is not None:
                    return csems[d.chan], d.val
                ep = (d.n - 1) // EPOCH
                return esems[d.eng][ep], d.n - ep * EPOCH

            block = st.enter_context(nc.Block())
            finals = [semval(o) for o in self.final]

            def run(e, eng):
                waited = {}
                for op in self.ops[e]:
                    for d in op.deps:
                        s, v = semval(d)
                        key = id(s)
                        if waited.get(key, 0) >= v:
                            continue
                        waited[key] = v
                        eng.wait_ge(s, v)
                    ins = op.fn(eng)
                    if op.marked:
                        if op.chan is not None:
                            ins.then_inc(csems[op.chan], 16)
                        else:
                            s, _ = semval(op)
                            ins.then_inc(s, 1)
                if e == "sp":
                    best = {}
                    for s, v in finals:
                        if best.get(id(s), (None, 0))[1] < v:
                            best[id(s)] = (s, v)
                    for s, v in best.values():
                        eng.wait_ge(s, v)

            @block.tensor
            def _(eng):
                run("pe", eng)

            @block.scalar
            def _(eng):
                run("act", eng)

            @block.vector
            def _(eng):
                run("dve", eng)

            @block.gpsimd
            def _(eng):
                run("pool", eng)

            @block.sync
            def _(eng):
                run("sp", eng)


class StopBuild(Exception):
    pass


def interleave(*gens):
    gens = list(gens)
    while gens:
        for g in list(gens):
            try:
                next(g)
            except StopIteration:
                gens.remove(g)


def stop_at(tag):
    import os
    if os.environ.get("KSTOP", "") == tag:
        raise StopBuild(tag)


class Arena:
    def __init__(self, ap, ncols):
        self.ap = ap
        self.n = ncols
        self.off = 0

    def reset(self):
        self.off = 0

    def alloc(self, shape, dt=F32):
        P = shape[0]
        free = int(np.prod(shape[1:]))
        nb = free * (4 if dt == F32 else 2)
        ncol = (nb + 3) // 4
        a = self.ap[0:P, self.off:self.off + ncol]
        self.off += (ncol + 15) // 16 * 16
        self.hi = max(getattr(self, "hi", 0), self.off)
        assert self.off <= self.n, f"arena overflow {self.off} > {self.n}"
        if dt != F32:
            a = a.bitcast(dt)
            if a.shape[1] != free:
                a = a[:, 0:free]
        if len(shape) == 3:
            a = a.rearrange("p (a b) -> p a b", a=shape[1])
        elif len(shape) == 4:
            a = a.rearrange("p (a b c) -> p a b c", a=shape[1], b=shape[2])
        return a


def make_consts():
    p = np.arange(64)[:, None]
    f = np.arange(64)[None, :]
    c64 = np.zeros((64, 8, 64), np.float32)
    c64[:, 0] = (p <= f)
    c64[:, 1] = -(p <= f).astype(np.float32)
    c64[:, 2] = -(p < f).astype(np.float32)
    c64[:, 3] = -(f < p).astype(np.float32)
    c64[:, 4] = np.eye(64)
    c64[:, 5] = 1.0
    P = np.arange(128)[:, None]
    Fq = np.arange(128)[None, :]
    c128 = np.zeros((128, 4, 128), np.float32)
    c128[:, 0] = np.eye(128)
    c128[:, 1] = (P < Fq)
    c128[:, 2] = (P >= Fq)
    c128[:, 3] = (P // 64 == Fq // 64)
    rmask = np.ones((64, 128), np.float32)
    rmask[:, ::64] = 0.0
    i4 = np.zeros((4, 4), np.float32)
    i4[:] = np.eye(4)
    return dict(c64=c64, c128=c128, rmask=rmask, i4=i4)


def build(NPS, TP, L, SAMPLE, TS_=64):
    nc = bass.Bass("TRN2", target_bir_lowering=False)
    dins = {}
    douts = {}

    def din(name, shape):
        dins[name] = nc.dram_tensor(name, list(shape), F32, kind="ExternalInput").ap()
        return dins[name]

    def dout(name, shape):
        douts[name] = nc.dram_tensor(name, list(shape), F32, kind="ExternalOutput").ap()
        return douts[name]

    xp = din("xp", (NPS, TP, 1024))
    wfa = din("wfa", (L, 128, 8, 1280))
    wta = din("wta", (L, 128, 8, 776))
    wfb = din("wfb", (L, 128, 8, 1024))
    wrb = din("wrb", (L, 128, 8, 8))
    wtb = din("wtb", (L, 128, 8, 1280))
    wdn = din("wdn", (L, 18, 128, 4096))
    convw = din("convw", (L, 128, 6, 4))
    b1c = din("b1c", (L, 128, 32))
    vecs = din("vecs", (L, 128, 5, 8))
    outg = din("outg", (L, 1024))
    hvec = din("hvec", (L, 4, 4))
    lbl = din("lbl", (L, 256))
    c64d = din("c64", (64, 8, 64))
    c128d = din("c128", (128, 4, 128))
    rmaskd = din("rmask", (64, 128))
    i4d = din("i4", (4, 4))
    if SAMPLE:
        xs_in = din("xs", (TS_, 1024))
        ck = din("ck", (L, PAST, 256))
        cv = din("cv", (L, PAST, 256))
        sconv = din("sconv", (L, 3, 768))
        gS_in = din("gS", (L, 4, 64, 64))
        mC_in = din("mC", (L, 4, 64, 64))
        mn_in = din("mn", (L, 4, 64))
        mm_in = din("mm", (L, 4))
        hS_in = din("hS", (L, 4, 64, 64))
    y_p = dout("y_p", (NPS, TP, 1024))
    p_k = dout("p_k", (L, NPS, TP, 256))
    p_v = dout("p_v", (L, NPS, TP, 256))
    p_conv = dout("p_conv", (L, NPS, 3, 768))
    p_gdn = dout("p_gdn", (L, NPS, 4, 64, 64))
    p_c = dout("p_c", (L, NPS, 4, 64, 64))
    p_n = dout("p_n", (L, NPS, 4, 64))
    p_m = dout("p_m", (L, NPS, 4))
    p_hg = dout("p_hg", (L, NPS, 4, 64, 64))
    if SAMPLE:
        y_s = dout("y_s", (TS_, 1024))
        s_k = dout("s_k", (L, 1, TS_, 256))
        s_v = dout("s_v", (L, 1, TS_, 256))
        s_conv = dout("s_conv", (L, 1, 3, 768))
        s_gdn = dout("s_gdn", (L, 1, 4, 64, 64))
        s_c = dout("s_c", (L, 1, 4, 64, 64))
        s_n = dout("s_n", (L, 1, 4, 64))
        s_m = dout("s_m", (L, 1, 4))
        s_hg = dout("s_hg", (L, 1, 4, 64, 64))
    TMAX = max(TP, TS_)
    xscr = nc.dram_tensor("xscr", [2, 128, 8, TMAX], F32, kind="Internal").ap()
    wdn_bf = nc.dram_tensor("wdn_bf", [L, 18, 128, 4096], BF16, kind="Internal").ap()
    wfa_bf = nc.dram_tensor("wfa_bf", [L, 128, 8 * 1280], BF16, kind="Internal").ap()
    wta_bf = nc.dram_tensor("wta_bf", [L, 128, 8 * 776], BF16, kind="Internal").ap()
    wfb_bf = nc.dram_tensor("wfb_bf", [L, 128, 8 * 1024], BF16, kind="Internal").ap()
    wrb_bf = nc.dram_tensor("wrb_bf", [L, 128, 8 * 8], BF16, kind="Internal").ap()
    wtb_bf = nc.dram_tensor("wtb_bf", [L, 128, 8 * 1280], BF16, kind="Internal").ap()
    import os
    KDEBUG = os.environ.get("KDEBUG", "") == "1"
    if KDEBUG:
        dbg_oT = nc.dram_tensor("dbg_oT", [128, 8, TMAX], BF16, kind="ExternalOutput").ap()

    seqs = []
    for i in range(NPS):
        seqs.append(dict(T=TP, past=0, x=xp[i], y=y_p[i], k=lambda l, i=i: p_k[l, i], v=lambda l, i=i: p_v[l, i],
                         conv=lambda l, i=i: p_conv[l, i], gdn=lambda l, i=i: p_gdn[l, i], c=lambda l, i=i: p_c[l, i],
                         n=lambda l, i=i: p_n[l, i], m=lambda l, i=i: p_m[l, i], hg=lambda l, i=i: p_hg[l, i], sample=False))
    if SAMPLE:
        seqs.append(dict(T=TS_, past=PAST, x=xs_in, y=y_s, k=lambda l: s_k[l, 0], v=lambda l: s_v[l, 0],
                         conv=lambda l: s_conv[l, 0], gdn=lambda l: s_gdn[l, 0], c=lambda l: s_c[l, 0],
                         n=lambda l: s_n[l, 0], m=lambda l: s_m[l, 0], hg=lambda l: s_hg[l, 0], sample=True))

    AW = 38100
    NKMAX = PAST + TS_ if SAMPLE else TP
    NKMAX = max(NKMAX, TP)
    NKT = (NKMAX + 127) // 128

    with contextlib.ExitStack() as st:
        def sb(name, shape, dt=F32):
            return st.enter_context(nc.sbuf_tensor(name, list(shape), dt))

        S = Sched(nc)
        arena_t = sb("arena", (128, AW))
        AR = Arena(arena_t, AW)
        dummy = sb("fence_dummy", (1, 8))
        oT = sb("oT", (128, 8, TMAX), BF16)
        c64 = sb("c64t", (64, 8, 64))
        c64b = sb("c64b", (64, 8, 64), BF16)
        c128 = sb("c128t", (128, 4, 128))
        c128b = sb("c128b", (128, 4, 128), BF16)
        rmask = sb("rmaskt", (64, 128))
        i4 = sb("i4t", (4, 4))
        cst = sb("cst", (128, 8))
        odiv = sb("odiv", (128, 128), BF16)
        convw_t = sb("convw_t", (128, L, 24))
        b1_t = sb("b1_t", (128, L, 32))
        vec_t = sb("vec_t", (128, L, 40))
        outg_t = sb("outg_t", (64, L, 1024))
        hrow_t = sb("hrow_t", (4, L, 4))
        hbc_t = sb("hbc_t", (64, L, 16))
        negA_t = sb("negA_t", (64, L, 4))
        nbf_t = sb("nbf_t", (4, L, 1))
        lb_t = sb("lb_t", (64, 4, L))
        oml_t = sb("oml_t", (64, 4, L))
        noml_t = sb("noml_t", (64, 4, L))
        lbtmp = sb("lbtmp", (64, 4, L))
        lbsum = sb("lbsum", (64, 4))
        pbank = [st.enter_context(nc.psum_tensor(f"pb{i}", [128, 512], F32)) for i in range(8)]

        def PB(i, p0, p1, c0, c1):
            return pbank[i][p0:p1, c0:c1]

        def PB16(i, p0, p1, c0, c1):
            return pbank[i][p0:p1, c0:c1].bitcast(BF16)

        def ACT(out, in_, func, r, w, bias=None, scale=None):
            kw = {}
            if bias is not None:
                kw["bias"] = bias
            if scale is not None:
                kw["scale"] = scale
            S.add("act", lambda e: e.activation(out=out, in_=in_, func=func, **kw), r=r, w=w)

        def TT(eng, out, in0, in1, op, r, w):
            S.add(eng, lambda e: e.tensor_tensor(out=out, in0=in0, in1=in1, op=op), r=r, w=w)

        def TS(eng, out, in0, s1, op0, r, w, s2=None, op1=None):
            if op1 is None:
                S.add(eng, lambda e: e.tensor_scalar(out=out, in0=in0, scalar1=s1, scalar2=None, op0=op0), r=r, w=w)
            else:
                S.add(eng, lambda e: e.tensor_scalar(out=out, in0=in0, scalar1=s1, scalar2=s2, op0=op0, op1=op1), r=r, w=w)

        def STT(out, in0, scalar, in1, op0, op1, r, w):
            S.add("dve", lambda e: e.scalar_tensor_tensor(out=out, in0=in0, scalar=scalar, in1=in1, op0=op0, op1=op1), r=r, w=w)

        def CP(eng, out, in_, r, w):
            if eng == "act":
                S.add("act", lambda e: e.activation(out=out, in_=in_, func=AF.Copy), r=r, w=w)
            else:
                S.add(eng, lambda e: e.tensor_copy(out=out, in_=in_), r=r, w=w)

        def MM(out, lhsT, rhs, r, w, start=True, stop=True):
            S.add("pe", lambda e: e.matmul(out, lhsT=lhsT, rhs=rhs, start=start, stop=stop), r=r, w=w)

        def TR(out, in_, ident, r, w):
            S.add("pe", lambda e: e.transpose(out, in_, ident), r=r, w=w)

        def MSET(eng, ap, val, w):
            S.add(eng, lambda e: e.memset(ap, val), w=w)

        def bc_mid(ap2, n):
            return ap2.unsqueeze(1).to_broadcast([ap2.shape[0], n, ap2.shape[1]])

        def bc_last(ap2, n):
            return ap2.unsqueeze(2).to_broadcast([ap2.shape[0], ap2.shape[1], n])

        ONE = cst[:, 0:1]

        def SIGM(out, in_, r, w):
            np_ = out.shape[0]
            ACT(out, in_, AF.Exp, r, w, scale=-1.0)
            ACT(out, out, AF.Ln, list(w) + ["cst"], w, bias=cst[0:np_, 0:1], scale=1.0)
            ACT(out, out, AF.Exp, w, w, scale=-1.0)

        def RSQRT(out, in_, eps_ap, r, w, scale=1.0):
            ACT(out, in_, AF.Ln, list(r) + ["cst"], w, bias=eps_ap, scale=scale)
            ACT(out, out, AF.Exp, w, w, scale=-0.5)
        EPS6 = cst[:, 1:2]
        EPS5 = cst[:, 2:3]

        S.dma("sp", c64[:], c64d, w=["c64"], chan="ld0")
        S.dma("sp", c128[:], c128d, w=["c128"], chan="ld1")
        S.dma("sp", rmask[:], rmaskd, w=["rmask"], chan="ld2")
        S.dma("sp", i4[:], i4d, w=["i4"], chan="ld3")
        S.dma("sp", convw_t[:].rearrange("p l (b j) -> p l b j", b=6), convw.rearrange("l p b j -> p l b j"), w=["convw"], chan="ld4")
        S.dma("sp", b1_t[:], b1c.rearrange("l p m -> p l m"), w=["b1"], chan="ld5")
        S.dma("sp", vec_t[:].rearrange("p l (a b) -> p l a b", a=5), vecs.rearrange("l p a b -> p l a b"), w=["vec"], chan="ld6")
        for l in range(L):
            S.dma("sp", outg_t[:, l, :], outg[l:l + 1, :].to_broadcast([64, 1024]), w=["outg"], chan="ld7")
            S.dma("sp", hbc_t[:, l, :], hvec[l:l + 1].rearrange("o a b -> o (a b)").to_broadcast([64, 16]), w=["hbc"], chan="ld8")
        S.dma("sp", hrow_t[:], hvec.rearrange("l a h -> h l a"), w=["hrow"], chan="ld9", nc_ok=True)
        for l in range(L):
            S.dma("sp", lb_t[:, :, l], lbl[l].rearrange("(h d) -> d h", h=4), w=["lb"], chan="ld10", nc_ok=True)
        MSET("pool", cst[:, 0:1], 1.0, ["cst"])
        MSET("pool", cst[:, 1:2], 1e-6, ["cst"])
        MSET("pool", cst[:, 2:3], 1e-5, ["cst"])
        MSET("pool", odiv[:], 1.0 / 1024.0, ["odiv"])
        CP("dve", c64b[:], c64[:], ["c64"], ["c64b"])
        CP("dve", c128b[:], c128[:], ["c128"], ["c128b"])
        ACT(negA_t[:], hbc_t[:, :, 0:4], AF.Exp, ["hbc"], ["negA"])
        TS("dve", negA_t[:], negA_t[:], -1.0, ALU.mult, ["negA"], ["negA"])
        TS("dve", nbf_t[:], hrow_t[:, :, 3:4], -1.0, ALU.mult, ["hrow"], ["nbf"])
        ACT(lbtmp[:], lb_t[:], AF.Exp, ["lb"], ["lbtmp"])
        S.add("dve", lambda e: e.tensor_reduce(out=lbsum[:], in_=lbtmp[:], axis=AX.X, op=ALU.add), r=["lbtmp"], w=["lbsum"])
        S.add("dve", lambda e: e.reciprocal(out=lbsum[:], in_=lbsum[:]), r=["lbsum"], w=["lbsum"])
        TT("dve", lbtmp[:], lbtmp[:], bc_last(lbsum[:], L), ALU.mult, ["lbtmp", "lbsum"], ["lbtmp"])
        MSET("pool", lb_t[:, :, 0:1], 0.0, ["lb"])
        for l in range(1, L):
            if l == 1:
                CP("dve", lb_t[:, :, 1:2], lbtmp[:, :, 1:2], ["lbtmp", "lb"], ["lb"])
            else:
                TT("dve", lb_t[:, :, l:l + 1], lb_t[:, :, l - 1:l], lbtmp[:, :, l:l + 1], ALU.add, ["lbtmp", "lb"], ["lb"])
        TS("dve", oml_t[:], lb_t[:], -1.0, ALU.mult, ["lb"], ["oml"], s2=1.0, op1=ALU.add)
        TS("dve", noml_t[:], oml_t[:], -1.0, ALU.mult, ["oml"], ["noml"])

        for l in range(L):
            for src, dst in ((wfa, wfa_bf), (wta, wta_bf)):
                S.dma("pool", dst[l], src[l].rearrange("p k c -> p (k c)"), w=[f"wbfA{l}"], chan=f"pcA{l}")
            for src, dst in ((wfb, wfb_bf), (wrb, wrb_bf), (wtb, wtb_bf)):
                S.dma("pool", dst[l], src[l].rearrange("p k c -> p (k c)"), w=[f"wbfB{l}"], chan=f"pcB{l}")
            for ci in range(18):
                S.dma("pool", wdn_bf[l, ci], wdn[l, ci], w=[f"wbfD{l}"], chan=f"pcD{l}")
        C_TRIU = c64[:, 0, :]
        C_NTRIU = c64[:, 1, :]
        C_ONES64 = c64[:, 5, :]
        C_ID64 = c64[:, 4, :]
        ID128 = c128[:, 0, :]
        ID128B = c128b[:, 0, :]
        AMASK = c128[:, 1, :]
        TRIINC_B = c128b[:, 2, :]
        BLKONES_B = c128b[:, 3, :]
        ones128b = sb("ones128b", (128, 128), BF16)
        MSET("pool", ones128b[:], 1.0, ["ones128b"])

        def phase0(sq):
            T = sq["T"]
            AR.reset()
            S.barrier(dummy[:])
            xtm = [AR.alloc((128, 1024)) for _ in range(2)]
            xfm = [AR.alloc((128, 8, 128)) for _ in range(2)]
            nt = (T + 127) // 128
            for ti in range(nt):
                n = min(128, T - ti * 128)
                a = ti % 2
                S.dma("sp", xtm[a][0:n, :], sq["x"][ti * 128:ti * 128 + n, :], w=[f"xtm{a}"], chan=f"p0l{a}")
                for b in range(8):
                    bank = (2 * a) + b // 4
                    TR(PB(bank, 0, 128, (b % 4) * 128, (b % 4) * 128 + n), xtm[a][0:n, b * 128:(b + 1) * 128], ID128[0:n, 0:n],
                       [f"xtm{a}", "c128"], [f"pb{bank}"])
                for hb in range(2):
                    bank = 2 * a + hb
                    src = pbank[bank][:].rearrange("p (b t) -> p b t", b=4)[:, :, 0:n]
                    CP("act" if hb == 0 else "dve", xfm[a][:, 4 * hb:4 * hb + 4, 0:n], src, [f"pb{bank}"], [f"xfm{a}"])
                S.dma("sp", xscr[0, :, :, ti * 128:ti * 128 + n], xfm[a][:, :, 0:n], r=[f"xfm{a}"], w=["xscr0"], chan=f"p0s{a}")

        def load_w(dst3, src2, l, key, chan):
            grp = "A" if chan.startswith("wA") else "B"
            S.dma("sp", dst3.rearrange("p k c -> p (k c)"), src2, r=[f"wbf{grp}{l}"], w=[key], chan=chan)

        def passA(sq, l):
            T = sq["T"]
            past = sq["past"]
            ntok = min(128, T)
            nch = ntok // 64
            ntile = T // ntok
            AR.reset()
            S.barrier(dummy[:])
            Wf = AR.alloc((128, 8, 1280), BF16)
            Wt = AR.alloc((128, 8, 776), BF16)
            NK = past + T
            kTs = AR.alloc((64, 4, NK), BF16)
            nkt = (NK + 127) // 128
            Vs = AR.alloc((128, nkt, 256), BF16)
            xb = AR.alloc((128, 8, ntok), BF16)
            ext = AR.alloc((128, 6, ntok + 3))
            yv = AR.alloc((128, 6, ntok))
            ys = AR.alloc((128, 6, ntok))
            vb16 = AR.alloc((128, 2, ntok), BF16)
            sqb = AR.alloc((128, 4, ntok), BF16)
            rn = AR.alloc((128, 4, ntok))
            qT2 = [AR.alloc((64, 4, ntok), BF16) for _ in range(2)]
            kT2 = [AR.alloc((64, 4, ntok), BF16) for _ in range(2)]
            qsb2 = [AR.alloc((64, 4, ntok), BF16) for _ in range(2)]
            kvout2 = [AR.alloc((128, 512)) for _ in range(2)]
            gg2 = [AR.alloc((64, nch, 256)) for _ in range(2)]
            sc8 = AR.alloc((64, nch, 8))
            bet2 = [AR.alloc((64, nch, 4)) for _ in range(2)]
            tg = AR.alloc((64, nch, 4))
            gvec2 = [AR.alloc((64, nch, 4)) for _ in range(2)]
            kvtm2 = [AR.alloc((64, nch, 512), BF16) for _ in range(2)]
            arg = AR.alloc((64, nch, 12))
            ex = AR.alloc((64, nch, 12))
            bg = AR.alloc((64, nch, 4))
            vbt = AR.alloc((64, nch, 4, 64), BF16)
            kbg = AR.alloc((64, nch, 4, 64), BF16)
            kdec = AR.alloc((64, nch, 4, 64), BF16)
            kb = AR.alloc((64, nch, 4, 64), BF16)
            kbT = AR.alloc((64, nch, 4, 64), BF16)
            absD = AR.alloc((64, nch * 256))
            G = AR.alloc((64, nch, 4, 64))
            eB = AR.alloc((64, nch, 4, 64))
            qg = AR.alloc((64, nch, 4, 64), BF16)
            Gu = AR.alloc((64, nch, 4, 64))
            nGs = AR.alloc((64, nch, 4, 64))
            nGl = AR.alloc((64, nch, 4, 64))
            PP = AR.alloc((64, 2 * nch, 4, 64)).rearrange("p (a c) h t -> p a c h t", a=2)
            ATb = AR.alloc((64, nch, 4, 64), BF16)
            TTm = AR.alloc((64, nch, 4, 64))
            TTb = AR.alloc((64, nch, 4, 64), BF16)
            Uf = AR.alloc((64, nch, 4, 64))
            WTb = AR.alloc((64, nch, 4, 64), BF16)
            vnew = AR.alloc((64, 4, 64), BF16)
            Sg = AR.alloc((64, 4, 64))
            Sgb = AR.alloc((64, 4, 64), BF16)
            O = AR.alloc((64, nch, 8, 64))
            Osq = AR.alloc((64, nch, 8, 64))
            ssq = AR.alloc((64, nch * 8))
            sgate = AR.alloc((64, nch, 256))
            Ob = AR.alloc((64, nch, 512), BF16)
            zs2 = [AR.alloc((128, 4, ntok)) for _ in range(2)]
            et = AR.alloc((128, 4, ntok))
            spb2 = [AR.alloc((128, 4, ntok), BF16) for _ in range(2)]
            Rb2 = [AR.alloc((128, 4, ntok), BF16) for _ in range(2)]
            tf = AR.alloc((128, 4, ntok))
            af = AR.alloc((128, 4, ntok))
            ab2 = [AR.alloc((128, 4, ntok), BF16) for _ in range(2)]
            if sq["sample"]:
                ktm = [AR.alloc((128, 256)) for _ in range(2)]

            load_w(Wf, wfa_bf[l], l, "Wf", "wA0")
            load_w(Wt, wta_bf[l], l, "Wt", "wA1")
            cw = convw_t[:, l, :].rearrange("p (b j) -> p b j", b=6)
            if sq["sample"]:
                S.dma("sp", Sg[:], gS_in[l].rearrange("h k v -> k h v"), w=["Sg"], chan="stA")
                for b in range(6):
                    S.dma("sp", ext[:, b, 0:3], sconv[l, :, b * 128:(b + 1) * 128].rearrange("r p -> p r"), w=["ext"], chan="stB", nc_ok=True)
                S.dma("pool", Vs[:, 0:PAST // 128, :], cv[l].rearrange("(t p) c -> p t c", p=128), w=[f"Vs{kt_}" for kt_ in range(PAST // 128)], chan="stC")
                for kt in range(PAST // 128):
                    a = kt % 2
                    S.dma("sp", ktm[a][:], ck[l, kt * 128:(kt + 1) * 128, :], w=[f"ktm{a}"], chan=f"stK{a}")
                    for h in range(4):
                        TR(PB(2 + a, 0, 64, h * 128, (h + 1) * 128), ktm[a][:, h * 64:(h + 1) * 64], ID128, [f"ktm{a}", "c128"], [f"pb{2 + a}"])
                    CP("act" if a == 0 else "dve", kTs[:, :, kt * 128:(kt + 1) * 128],
                       pbank[2 + a][0:64, :].rearrange("p (h t) -> p h t", h=4), [f"pb{2 + a}"], [f"kTs{kt}"])
            else:
                MSET("pool", Sg[:], 0.0, ["Sg"])
                MSET("pool", ext[:, :, 0:3], 0.0, ["ext"])
            CP("dve", Sgb[:], Sg[:], ["Sg"], ["Sgb"])

            stop_at("A0a")
            def pre_stream(ti):
                p = ti % 2
                pos0 = ti * ntok
                kt_new = (past + pos0) // 128
                pr0 = (past + pos0) % 128
                pos0 = ti * ntok
                S.dma("pool", xb[:], xscr[l % 2, :, :, pos0:pos0 + ntok], r=[f"xscr{l % 2}"], w=["xb"], chan="xbA")
                stop_at("A0b")
                nblk_bank = 512 // ntok
                groups = [(0, 4), (4, 8), (8, 10)]
                for gi, (b0, b1) in enumerate(groups):
                    bank = gi % 2
                    for b in range(b0, b1):
                        for k in range(8):
                            MM(PB(bank, 0, 128, (b - b0) * ntok, (b - b0 + 1) * ntok), Wf[:, k, b * 128:(b + 1) * 128], xb[:, k, :],
                               ["Wf", "xb"], [f"pb{bank}"], start=(k == 0), stop=(k == 7))
                    pv = pbank[bank][:, 0:(b1 - b0) * ntok].rearrange("p (b t) -> p b t", t=ntok)
                    import os
                    if os.environ.get("KDBG", "") == "noevac":
                        continue
                    if os.environ.get("KDBG", "") == f"only{gi}":
                        stop_at("A0c")
                    if gi == 0:
                        CP("act", ext[:, 0:4, 3:3 + ntok], pv, [f"pb{bank}"], ["ext"])
                    elif gi == 1:
                        dbg = os.environ.get("KDBG2", "abc")
                        if "a" in dbg:
                            CP("act", ext[:, 4:6, 3:3 + ntok], pv[:, 0:2, :], [f"pb{bank}"], ["ext"])
                        q4 = qsb2[p][:].rearrange("p (b h) t -> p b h t", b=2)
                        if "b" in dbg:
                            TS("dve", q4[:, :, 0, :], pv[0:64, 2:4, :], 0.125, ALU.mult, [f"pb{bank}"], [f"qsb{p}"])
                        if "c" in dbg:
                            TS("dve", q4[:, :, 1, :], pv[64:128, 2:4, :], 0.125, ALU.mult, [f"pb{bank}"], [f"qsb{p}"])
                    else:
                        k4 = kTs[:, :, past + pos0:past + pos0 + ntok].rearrange("p (b h) t -> p b h t", b=2)
                        CP("act", k4[:, :, 0, :], pv[0:64, 0:2, :], [f"pb{bank}"], [f"kTs{kt_new}a"])
                        CP("dve", k4[:, :, 1, :], pv[64:128, 0:2, :], [f"pb{bank}"], [f"kTs{kt_new}b"])
                stop_at("A0c")
                for k in range(8):
                    MM(PB(1, 0, ntok, 0, 512), xb[:, k, :], Wt[:, k, 0:512], ["Wt", "xb"], ["pb1"], start=(k == 0), stop=(k == 7))
                CP("act", kvout2[p][0:ntok, :], PB(1, 0, ntok, 0, 512), ["pb1"], [f"kvout{p}"])
                kt_new = (past + pos0) // 128
                pr0 = (past + pos0) % 128
                CP("dve", Vs[pr0:pr0 + ntok, kt_new, :], PB(1, 0, ntok, 256, 512), ["pb1"], [f"Vs{kt_new}"])
                S.dma("sp", sq["k"](l)[pos0:pos0 + ntok, :], kvout2[p][0:ntok, 0:256], r=[f"kvout{p}"], chan=f"okA{p}", final=True)
                S.dma("sp", sq["v"](l)[pos0:pos0 + ntok, :], kvout2[p][0:ntok, 256:512], r=[f"kvout{p}"], chan=f"ovA{p}", final=True)
                for k in range(8):
                    MM(PB(0, 0, ntok, 0, 264), xb[:, k, :], Wt[:, k, 512:776], ["Wt", "xb"], ["pb0"], start=(k == 0), stop=(k == 7))
                for c in range(nch):
                    CP("act", gg2[p][:, c, :], PB(0, 64 * c, 64 * c + 64, 0, 256), ["pb0"], [f"gg{p}"])
                    CP("dve", sc8[:, c, :], PB(0, 64 * c, 64 * c + 64, 256, 264), ["pb0"], ["sc8"])
                    yield
                stop_at("A1")
                for b in range(6):
                    TS("dve", yv[:, b, :], ext[:, b, 0:ntok], cw[:, b, 0:1], ALU.mult, ["ext", "convw"], [f"yv{b}"])
                for j in range(1, 4):
                    for b in range(6):
                        STT(yv[:, b, :], ext[:, b, j:j + ntok], cw[:, b, j:j + 1], yv[:, b, :], ALU.mult, ALU.add,
                            ["ext", "convw", f"yv{b}"], [f"yv{b}"])
                if ti == ntile - 1:
                    for b in range(6):
                        S.dma("sp", sq["conv"](l)[:, b * 128:(b + 1) * 128].rearrange("r p -> p r"), ext[:, b, ntok:ntok + 3], r=["ext"],
                              chan="ocA", final=True, nc_ok=True)
                else:
                    CP("pool", ext[:, :, 0:3], ext[:, :, ntok:ntok + 3], ["ext"] + [f"yv{b}" for b in range(6)], ["ext"])
                    yield
                SIGM(ys[:], yv[:], [f"yv{b}" for b in range(6)], ["ys"])
                TT("dve", ys[:], ys[:], yv[:], ALU.mult, ["ys"] + [f"yv{b}" for b in range(6)], ["ys"])
                CP("pool", vb16[:], ys[:, 4:6, :], ["ys"], ["vb16"])
                ACT(sqb[:], ys[:, 0:4, :], AF.Square, ["ys"], ["sqb"])
                yield
                for b in range(4):
                    MM(PB(1, 0, 128, b * ntok, (b + 1) * ntok), BLKONES_B, sqb[:, b, :], ["sqb", "c128b"], ["pb1"])
                RSQRT(rn[:], pbank[1][:, 0:4 * ntok].rearrange("p (b t) -> p b t", b=4), EPS6, ["pb1"], ["rn"])
                yield
                q4 = qT2[p][:].rearrange("p (b h) t -> p b h t", b=2)
                k4 = kT2[p][:].rearrange("p (b h) t -> p b h t", b=2)
                for hf in range(2):
                    STT(q4[:, :, hf, :], ys[64 * hf:64 * hf + 64, 0:2, :], 0.125, rn[64 * hf:64 * hf + 64, 0:2, :], ALU.mult, ALU.mult,
                        ["ys", "rn"], [f"qT{p}"])
                    TT("dve", k4[:, :, hf, :], ys[64 * hf:64 * hf + 64, 2:4, :], rn[64 * hf:64 * hf + 64, 2:4, :], ALU.mult, ["ys", "rn"], [f"kT{p}"])
                SIGM(bet2[p][:], sc8[:, :, 0:4], ["sc8"], [f"bet{p}"])
                TT("dve", tg[:], sc8[:, :, 4:8], bc_mid(hbc_t[:, l, 4:8], nch), ALU.add, ["sc8", "hbc"], ["tg"])
                ACT(tg[:], tg[:], AF.Exp, ["tg"], ["tg"])
                ACT(tg[:], tg[:], AF.Ln, ["tg", "cst"], ["tg"], bias=ONE[0:64], scale=1.0)
                TT("dve", gvec2[p][:], tg[:], bc_mid(negA_t[:, l, :], nch), ALU.mult, ["tg", "negA"], [f"gvec{p}"])
                yield
                for c in range(nch):
                    cs = slice(64 * c, 64 * c + 64)
                    pT = PB16(3, 0, 64, 0, 256)
                    for h in range(4):
                        TR(pT[:, h * 64:(h + 1) * 64], kT2[p][:, h, cs], c64b[:, 4, :], [f"kT{p}", "c64b"], ["pb3a"])
                    for bv in range(2):
                        TR(pT[:, 256 + bv * 128:256 + (bv + 1) * 128], vb16[:, bv, cs], ID128B, ["vb16", "c128b"], ["pb3a"])
                    CP("act", kvtm2[p][:, c, :], pT, ["pb3a"], [f"kvtm{p}_{c}"])
                    yield
                yield
            def gdn_stream(ti):
                p = ti % 2
                NH_ = nch * 4
                W_ = NH_ * 64
                fl = lambda ap: ap.rearrange("p c h t -> p (c h t)")
                kv = kvtm2[p]
                ktm4 = kv[:, :, 0:256].rearrange("p c (h d) -> p c h d", h=4)
                vtm4 = kv[:, :, 256:512].rearrange("p c (h d) -> p c h d", h=4)
                gv = gvec2[p]
                for c in range(nch):
                    MM(PB(4, 0, 64, 8 * c, 8 * c + 4), C_TRIU, gv[:, c, :], ["c64", f"gvec{p}"], ["pb4"])
                    MM(PB(4, 0, 64, 8 * c + 4, 8 * c + 8), C_ONES64, gv[:, c, :], ["c64", f"gvec{p}"], ["pb4"])
                CP("act", arg[:, :, 0:8], pbank[4][0:64, 0:8 * nch].rearrange("p (c e) -> p c e", e=8), ["pb4"], ["arg"])
                TT("dve", arg[:, :, 8:12], arg[:, :, 4:8], arg[:, :, 0:4], ALU.subtract, ["arg"], ["arg"])
                ACT(ex[:], arg[:], AF.Exp, ["arg"], ["ex"])
                yield
                TT("dve", bg[:], bet2[p][:], ex[:, :, 0:4], ALU.mult, [f"bet{p}", "ex"], ["bg"])
                b3 = lambda a3: a3.unsqueeze(3).to_broadcast([64, nch, 4, 64])
                TT("pool", vbt[:], vtm4, b3(bet2[p][:]), ALU.mult, [f"kvtm{p}_{c_}" for c_ in range(nch)] + [f"bet{p}"], ["vbt"])
                TT("dve", kb[:], ktm4, b3(bet2[p][:]), ALU.mult, [f"kvtm{p}_{c_}" for c_ in range(nch)] + [f"bet{p}"], ["kb"])
                TT("dve", kbg[:], ktm4, b3(bg[:]), ALU.mult, [f"kvtm{p}_{c_}" for c_ in range(nch)] + ["bg"], ["kbg"])
                TT("pool", kdec[:], ktm4, b3(ex[:, :, 8:12]), ALU.mult, [f"kvtm{p}_{c_}" for c_ in range(nch)] + ["ex"], ["kdec"])
                yield
                pT2 = PB16(3, 0, 64, 256, 256 + W_ // 2)
                for c in range(nch):
                    for h in range(4):
                        j = c * 4 + h
                        TR(pT2[:, j * 64:(j + 1) * 64], kb[:, c, h, :], c64b[:, 4, :], ["kb", "c64b"], ["pb3b"])
                CP("dve", fl(kbT[:]), pT2, ["pb3b"], ["kbT"])
                yield
                for c in range(nch):
                    for h in range(4):
                        j = c * 4 + h
                        gbc = gv[:, c, h:h + 1].to_broadcast([64, 64])
                        MM(PB(4, 0, 64, j * 64, (j + 1) * 64), gbc, C_TRIU, [f"gvec{p}", "c64"], ["pb4"])
                        MM(PB(5, 0, 64, j * 64, (j + 1) * 64), gbc, C_TRIU, [f"gvec{p}", "c64"], ["pb5"], start=True, stop=False)
                        MM(PB(5, 0, 64, j * 64, (j + 1) * 64), C_NTRIU, gbc, [f"gvec{p}", "c64"], ["pb5"], start=False, stop=True)
                CP("dve", absD[:], PB(5, 0, 64, 0, W_), ["pb5"], ["absD"])
                STT(absD[:], absD[:], -1.0, absD[:], ALU.mult, ALU.min, ["absD"], ["absD"])
                yield
                ACT(fl(G[:]), absD[:], AF.Exp, ["absD"], ["G"])
                ACT(fl(eB[:]), PB(4, 0, 64, 0, W_), AF.Exp, ["pb4"], ["eB"])
                qTv = qT2[p][:].rearrange("p h (c t) -> p c h t", t=64)
                TT("dve", qg[:], qTv, eB[:], ALU.mult, [f"qT{p}", "eB"], ["qg"])
                yield
                G3 = G[:].rearrange("p c h t -> p (c h) t")
                TT("dve", Gu[:].rearrange("p c h t -> p (c h) t"), G3, bc_mid(C_TRIU, NH_), ALU.mult, ["G", "c64"], ["Gu"])
                TT("dve", nGs[:].rearrange("p c h t -> p (c h) t"), G3, bc_mid(c64[:, 2, :], NH_), ALU.mult, ["G", "c64"], ["nGs"])
                TT("dve", nGl[:].rearrange("p c h t -> p (c h) t"), G3, bc_mid(c64[:, 3, :], NH_), ALU.mult, ["G", "c64"], ["nGl"])
                yield
                for c in range(nch):
                    cs = slice(64 * c, 64 * c + 64)
                    for h in range(4):
                        j = c * 4 + h
                        MM(PB(6, 0, 64, j * 64, (j + 1) * 64), kT2[p][:, h, cs], kbT[:, c, h, :], [f"kT{p}", "kbT"], ["pb6"])
                        MM(PB(7, 0, 64, j * 64, (j + 1) * 64), kbT[:, c, h, :], kT2[p][:, h, cs], [f"kT{p}", "kbT"], ["pb7"])
                        MM(PB(4, 0, 64, j * 64, (j + 1) * 64), kT2[p][:, h, cs], qT2[p][:, h, cs], [f"kT{p}", f"qT{p}"], ["pb4"])
                TT("dve", fl(PP[:, 1]), PB(6, 0, 64, 0, W_), fl(nGs[:]), ALU.mult, ["pb6", "nGs"], ["PP1"])
                TT("dve", fl(PP[:, 0]), PB(7, 0, 64, 0, W_), fl(nGl[:]), ALU.mult, ["pb7", "nGl"], ["PP0"])
                TT("dve", fl(ATb[:]), PB(4, 0, 64, 0, W_), fl(Gu[:]), ALU.mult, ["pb4", "Gu"], ["ATb"])
                yield
                TT("dve", TTm[:].rearrange("p c h t -> p (c h) t"), PP[:, 1].rearrange("p c h t -> p (c h) t"), bc_mid(C_ID64, NH_), ALU.add,
                   ["PP1", "c64"], ["TTm"])
                yield
                for lev in range(1, 6):
                    for c in range(nch):
                        for h in range(4):
                            j = c * 4 + h
                            MM(PB(5, 0, 64, j * 64, (j + 1) * 64), PP[:, 1, c, h, :], PP[:, 0, c, h, :], ["PP0", "PP1"], ["pb5"])
                            if lev < 5:
                                MM(PB(6, 0, 64, j * 64, (j + 1) * 64), PP[:, 0, c, h, :], PP[:, 1, c, h, :], ["PP0", "PP1"], ["pb6"])
                    CP("act", fl(PP[:, 0]), PB(5, 0, 64, 0, W_), ["pb5"], ["PP0"])
                    if lev < 5:
                        CP("dve", fl(PP[:, 1]), PB(6, 0, 64, 0, W_), ["pb6"], ["PP1"])
                    yield
                    for c in range(nch):
                        for h in range(4):
                            j = c * 4 + h
                            MM(PB(7, 0, 64, j * 64, (j + 1) * 64), PP[:, 0, c, h, :], TTm[:, c, h, :], ["PP0", "TTm"], ["pb7"])
                    TT("dve", fl(TTm[:]), fl(TTm[:]), PB(7, 0, 64, 0, W_), ALU.add, ["TTm", "pb7"], ["TTm"])
                    yield
                CP("act", TTb[:], TTm[:], ["TTm"], ["TTb"])
                yield
                for c in range(nch):
                    for h in range(4):
                        j = c * 4 + h
                        MM(PB(4, 0, 64, j * 64, (j + 1) * 64), TTb[:, c, h, :], vbt[:, c, h, :], ["TTb", "vbt"], ["pb4"])
                        MM(PB(5, 0, 64, j * 64, (j + 1) * 64), kbg[:, c, h, :], TTb[:, c, h, :], ["TTb", "kbg"], ["pb5"])
                CP("act", fl(Uf[:]), PB(4, 0, 64, 0, W_), ["pb4"], ["Uf"])
                CP("dve", fl(WTb[:]), PB(5, 0, 64, 0, W_), ["pb5"], ["WTb"])
                yield
                f3 = lambda ap: ap.rearrange("p h t -> p (h t)")
                for c in range(nch):
                    for h in range(4):
                        MM(PB(6, 0, 64, h * 64, (h + 1) * 64), WTb[:, c, h, :], Sgb[:, h, :], ["WTb", "Sgb"], ["pb6"])
                    TT("dve", f3(vnew[:]), f3(Uf[:, c]), PB(6, 0, 64, 0, 256), ALU.subtract, ["Uf", "pb6"], ["vnew"])
                    yield
                    for h in range(4):
                        MM(PB(7, 0, 64, h * 64, (h + 1) * 64), ATb[:, c, h, :], vnew[:, h, :], ["ATb", "vnew"], ["pb7"], start=True, stop=False)
                        MM(PB(7, 0, 64, h * 64, (h + 1) * 64), qg[:, c, h, :], Sgb[:, h, :], ["qg", "Sgb"], ["pb7"], start=False, stop=True)
                    for h in range(4):
                        MM(PB(6, 0, 64, 256 + h * 64, 256 + (h + 1) * 64), kdec[:, c, h, :], vnew[:, h, :], ["kdec", "vnew"], ["pb6"])
                    CP("act", f3(O[:, c, 0:4, :]), PB(7, 0, 64, 0, 256), ["pb7"], ["O"])
                    TT("dve", Sg[:], Sg[:], bc_last(ex[:, c, 4:8], 64), ALU.mult, ["Sg", "ex"], ["Sg"])
                    TT("dve", f3(Sg[:]), f3(Sg[:]), PB(6, 0, 64, 256, 512), ALU.add, ["Sg", "pb6"], ["Sg"])
                    CP("act", Sgb[:], Sg[:], ["Sg"], ["Sgb"])
                    yield
            def attn_stream(ti):
                p = ti % 2
                pos0 = ti * ntok
                kt_new = (past + pos0) // 128
                pr0 = (past + pos0) % 128
                kts = [(kt_new, ntok, True)] + [(kt, 128, False) for kt in range(kt_new - 1, -1, -1)]
                nkts = len(kts)
                W4 = 4 * ntok

                def stage1(i):
                    kt, nk, diag = kts[i]
                    a = i % 2
                    for h in range(4):
                        MM(PB(0, 0, nk, h * ntok, (h + 1) * ntok), kTs[:, h, kt * 128:kt * 128 + nk], qsb2[p][:, h, :], [f"kTs{kt}", f"kTs{kt}a", f"kTs{kt}b", f"qsb{p}"], ["pb0"])
                    pz = pbank[0][0:nk, 0:W4].rearrange("p (h t) -> p h t", h=4)
                    CP("dve", zs2[a][0:nk], pz, ["pb0"], [f"zs{a}"])
                    ACT(et[0:nk], pz, AF.Exp, ["pb0"], ["et"])
                    ACT(spb2[a][0:nk], et[0:nk], AF.Ln, ["et", "cst"], [f"spb{a}"], bias=ONE[0:nk], scale=1.0)
                    if diag:
                        TT("pool", spb2[a][0:nk], spb2[a][0:nk], bc_mid(AMASK[0:nk, 0:ntok], 4), ALU.mult, [f"spb{a}", "c128"], [f"spb{a}"])

                def stage2(i):
                    kt, nk, diag = kts[i]
                    a = i % 2
                    first = i == 0
                    last = i == nkts - 1
                    pa = pbank[1][0:nk, 0:W4]
                    MM(pa, TRIINC_B[0:nk, 0:nk], spb2[a][0:nk].rearrange("p h t -> p (h t)"), [f"spb{a}", "c128b"], ["pb1"], start=True, stop=first)
                    if not first:
                        MM(pa, ones128b[:, 0:nk], Rb2[a][:].rearrange("p h t -> p (h t)"), [f"Rb{a}", "ones128b"], ["pb1"], start=False, stop=True)
                    TT("dve", tf[0:nk].rearrange("p h t -> p (h t)"), zs2[a][0:nk].rearrange("p h t -> p (h t)"), pa, ALU.subtract,
                       [f"zs{a}", "pb1"], ["tf"])
                    if diag:
                        ACT(af[0:nk], tf[0:nk], AF.Exp, ["tf"], ["af"])
                        TT("pool", ab2[a][0:nk], af[0:nk], bc_mid(AMASK[0:nk, 0:ntok], 4), ALU.mult, ["af", "c128"], [f"ab{a}"])
                    else:
                        ACT(ab2[a][0:nk], tf[0:nk], AF.Exp, ["tf"], [f"ab{a}"])
                    if not last:
                        nb_ = (i + 1) % 2
                        if first:
                            if nk < 128:
                                MSET("pool", Rb2[nb_][:], 0.0, [f"Rb{nb_}"])
                            CP("dve", Rb2[nb_][0:nk], spb2[a][0:nk], [f"spb{a}", f"Rb{nb_}"], [f"Rb{nb_}"])
                        else:
                            TT("dve", Rb2[nb_][:], Rb2[a][:], spb2[a][:], ALU.add, [f"spb{a}", f"Rb{a}"], [f"Rb{nb_}"])
                    for h in range(4):
                        S.add("pe", lambda e, h=h, kt=kt, nk=nk, a=a, st_=(first and h == 0), sp_=last: e.matmul(
                            PB(2, 0, ntok, h * 64, (h + 1) * 64), lhsT=ab2[a][0:nk, h, :], rhs=Vs[0:nk, kt, h * 64:(h + 1) * 64],
                            start=st_, stop=sp_, skip_group_check=True), r=[f"ab{a}", f"Vs{kt}"], w=["pb2"])

                stage1(0)
                yield
                for i in range(nkts):
                    if i + 1 < nkts:
                        stage1(i + 1)
                        yield
                    stage2(i)
                    yield
                for c in range(nch):
                    CP("act", O[:, c, 4:8, :].rearrange("p h t -> p (h t)"), PB(2, 64 * c, 64 * c + 64, 0, 256), ["pb2"], ["O"])
            def merge_tile(ti):
                p = ti % 2
                pos0 = ti * ntok
                kt_new = (past + pos0) // 128
                pr0 = (past + pos0) % 128
                merge(O, Osq, ssq, Ob, nch, l, 0, pos0,
                      gate0=("silu", gg2[p], sgate, f"gg{p}", 0), gate1=None)
            for _ in pre_stream(0):
                pass
            for ti in range(ntile):
                interleave(gdn_stream(ti), attn_stream(ti))
                merge_tile(ti)
                if ti + 1 < ntile:
                    for _ in pre_stream(ti + 1):
                        pass
            S.dma("sp", sq["gdn"](l).rearrange("h k v -> k h v"), Sg[:], r=["Sg"], chan="ogA", final=True)

        def merge(O, Osq, ssq, Ob, nch, l, blk0, pos0, gate0, gate1):
            Of = O[:].rearrange("p c h d -> p (c h d)")
            ACT(Osq[:].rearrange("p c h d -> p (c h d)"), Of, AF.Square, ["O"], ["Osq"])
            S.add("dve", lambda e: e.tensor_reduce(out=ssq[:], in_=Osq[:].rearrange("p c h d -> p (c h) d"), axis=AX.X, op=ALU.add),
                  r=["Osq"], w=["ssq"])
            RSQRT(ssq[:], ssq[:], EPS5[0:64], ["ssq"], ["ssq"], scale=1.0 / 64.0)
            O3 = O[:].rearrange("p c h d -> p (c h) d")
            TT("dve", O3, O3, bc_last(ssq[:], 64), ALU.mult, ["O", "ssq"], ["O"])
            O2 = O[:].rearrange("p c h d -> p c (h d)")
            TT("pool", O2, O2, bc_mid(outg_t[:, l, blk0 * 128:blk0 * 128 + 512], nch), ALU.mult, ["O", "outg"], ["O"])
            for gi, gate in enumerate((gate0, gate1)):
                sl = slice(256 * gi, 256 * gi + 256)
                if gate is None:
                    CP("pool", Ob[:, :, sl], O2[:, :, sl], ["O"], [f"Ob{gi}"])
                else:
                    kind, gsrc, gdst, gkey, goff = gate
                    SIGM(gdst[:], gsrc[:, :, goff:goff + 256], [gkey], ["sgate"])
                    if kind == "silu":
                        TT("pool", gdst[:], gdst[:], gsrc[:, :, goff:goff + 256], ALU.mult, [gkey, "sgate"], ["sgate"])
                    TT("dve", Ob[:, :, sl], O2[:, :, sl], gdst[:], ALU.mult, ["O", "sgate"], [f"Ob{gi}"])
            for c in range(nch):
                pT = PB16(3, 0, 128, 0, 128)
                for b in range(4):
                    TR(pT[:, b * 64:(b + 1) * 64], Ob[:, c, b * 128:(b + 1) * 128], c64b[:, 4, :], ["Ob0", "Ob1", "c64b"], ["pb3a"])
                CP("act", oT[:, blk0:blk0 + 4, pos0 + 64 * c:pos0 + 64 * c + 64], pT.rearrange("p (b t) -> p b t", b=4), ["pb3a"], ["oT"])

        def passB(sq, l):
            T = sq["T"]
            ntok = min(128, T)
            nch = ntok // 64
            ntile = T // ntok
            AR.reset()
            S.barrier(dummy[:])
            Wf = AR.alloc((128, 8, 1024), BF16)
            Wr = AR.alloc((128, 8, 8), BF16)
            Wt = AR.alloc((128, 8, 1280), BF16)
            xb = AR.alloc((128, 8, ntok), BF16)
            mqT = AR.alloc((64, 4, ntok), BF16)
            mkT = AR.alloc((64, 4, ntok), BF16)
            hqf = AR.alloc((64, 2, 4, ntok))
            ri = AR.alloc((4, ntok))
            rf = AR.alloc((4, ntok))
            cl = AR.alloc((4, ntok))
            wv = AR.alloc((4, ntok))
            wsr = AR.alloc((4, ntok))
            gfr = AR.alloc((4, ntok))
            wmax = AR.alloc((4, nch))
            blast = AR.alloc((4, nch))
            mcur = AR.alloc((4, nch))
            Mc = AR.alloc((4, nch))
            mprev = AR.alloc((4, nch))
            decr = AR.alloc((4, nch))
            decd = AR.alloc((4, nch, 4))
            mstate = AR.alloc((4, 1))
            wsg = AR.alloc((64, nch, 8))
            decbc = AR.alloc((64, nch, 4))
            mktm = AR.alloc((64, nch, 256), BF16)
            V1 = AR.alloc((64, nch, 4, 65), BF16)
            V2 = AR.alloc((64, 4, 65), BF16)
            hgv = AR.alloc((64, nch, 256), BF16)
            gmo = AR.alloc((64, nch, 256))
            ghg = AR.alloc((64, nch, 256))
            sgo = AR.alloc((64, nch, 256))
            STb = AR.alloc((64, 4, 64), BF16)
            Cm = AR.alloc((64, 4, 65))
            Cmb = AR.alloc((64, 4, 65), BF16)
            aden = AR.alloc((64, 4))
            rec = AR.alloc((64, 4))
            hC = AR.alloc((64, 4, 64))
            qsil = AR.alloc((64, 4, ntok))
            sig = AR.alloc((64, 4, ntok))
            logf = AR.alloc((64, 4, ntok))
            kk = AR.alloc((64, 4, ntok))
            lcum = AR.alloc((64, 4, ntok))
            t1 = AR.alloc((64, 4, ntok))
            t2 = AR.alloc((64, 4, ntok))
            qs_ = AR.alloc((64, 4, ntok), BF16)
            qe_ = AR.alloc((64, 4, ntok), BF16)
            ks_ = AR.alloc((64, 4, ntok), BF16)
            kdT = AR.alloc((64, 4, ntok), BF16)
            kdtm = AR.alloc((64, nch, 4, 64), BF16)
            el = AR.alloc((64, 4, nch))
            aTb = AR.alloc((64, 4, 64), BF16)
            Sh = AR.alloc((64, 4, 64))
            Shb = AR.alloc((64, 4, 64), BF16)
            O = AR.alloc((64, nch, 8, 64))
            Osq = AR.alloc((64, nch, 8, 64))
            ssq = AR.alloc((64, nch * 8))
            sgate = AR.alloc((64, nch, 256))
            Ob = AR.alloc((64, nch, 512), BF16)

            load_w(Wf, wfb_bf[l], l, "Wf", "wB0")
            load_w(Wr, wrb_bf[l], l, "Wr", "wB2")
            load_w(Wt, wtb_bf[l], l, "Wt", "wB1")
            if sq["sample"]:
                S.dma("sp", Cm[:, :, 0:64], mC_in[l].rearrange("h k v -> k h v"), w=["Cm"], chan="stA")
                S.dma("sp", Cm[:, :, 64:65], mn_in[l].rearrange("h (k o) -> k h o", o=1), w=["Cm"], chan="stB", nc_ok=True)
                S.dma("sp", mstate[:], mm_in[l].rearrange("(h o) -> h o", o=1), w=["mstate"], chan="stC", nc_ok=True)
                S.dma("sp", Sh[:], hS_in[l].rearrange("h k v -> k h v"), w=["Sh"], chan="stD")
            else:
                MSET("pool", Cm[:], 0.0, ["Cm"])
                MSET("pool", mstate[:], 0.0, ["mstate"])
                MSET("pool", Sh[:], 0.0, ["Sh"])
            CP("dve", Shb[:], Sh[:], ["Sh"], ["Shb"])
            MSET("pool", V1[:], 1.0, ["V1"])
            bi_col = hrow_t[:, l, 2:3]
            nbf_col = nbf_t[:, l, :]

            for ti in range(ntile):
                pos0 = ti * ntok
                S.dma("pool", xb[:], xscr[l % 2, :, :, pos0:pos0 + ntok], r=[f"xscr{l % 2}"], w=["xb"], chan="xbA")
                for gi in range(2):
                    bank = gi
                    for bb in range(4):
                        b = gi * 4 + bb
                        for k in range(8):
                            MM(PB(bank, 0, 128, bb * ntok, (bb + 1) * ntok), Wf[:, k, b * 128:(b + 1) * 128], xb[:, k, :],
                               ["Wf", "xb"], [f"pb{bank}"], start=(k == 0), stop=(k == 7))
                    pv = pbank[bank][:, 0:4 * ntok].rearrange("p (b t) -> p b t", t=ntok)
                    if gi == 0:
                        q4 = mqT[:].rearrange("p (b h) t -> p b h t", b=2)
                        k4 = mkT[:].rearrange("p (b h) t -> p b h t", b=2)
                        for hf in range(2):
                            CP("act", q4[:, :, hf, :], pv[64 * hf:64 * hf + 64, 0:2, :], ["pb0"], ["mqT"])
                            TS("dve", k4[:, :, hf, :], pv[64 * hf:64 * hf + 64, 2:4, :], 0.125, ALU.mult, ["pb0"], ["mkT"])
                    else:
                        for a in range(2):
                            d4 = hqf[:, a].rearrange("p (b h) t -> p b h t", b=2)
                            for hf in range(2):
                                CP("act" if hf == 0 else "dve", d4[:, :, hf, :], pv[64 * hf:64 * hf + 64, 2 * a:2 * a + 2, :], ["pb1"], ["hqf"])
                for a in range(2):
                    for k in range(8):
                        MM(PB(2, 0, 4, a * ntok, (a + 1) * ntok), Wr[:, k, 4 * a:4 * a + 4], xb[:, k, :], ["Wr", "xb"], ["pb2"],
                           start=(k == 0), stop=(k == 7))
                CP("act", ri[:], PB(2, 0, 4, 0, ntok), ["pb2"], ["ri"])
                ACT(rf[:], PB(2, 0, 4, ntok, 2 * ntok), AF.Exp, ["pb2", "nbf"], ["rf"], bias=nbf_col, scale=-1.0)
                ACT(rf[:], rf[:], AF.Ln, ["rf", "cst"], ["rf"], bias=ONE[0:4], scale=1.0)
                for k in range(8):
                    MM(PB(0, 0, ntok, 0, 512), xb[:, k, :], Wt[:, k, 0:512], ["Wt", "xb"], ["pb0"], start=(k == 0), stop=(k == 7))
                for c in range(nch):
                    TS("dve", mktm[:, c, :], PB(0, 64 * c, 64 * c + 64, 0, 256), 0.125, ALU.mult, ["pb0"], ["mktm"])
                    CP("act", V1[:, c, :, 0:64], PB(0, 64 * c, 64 * c + 64, 256, 512).rearrange("p (h d) -> p h d", h=4), ["pb0"], ["V1"])
                for k in range(8):
                    MM(PB(1, 0, ntok, 0, 512), xb[:, k, :], Wt[:, k, 512:1024], ["Wt", "xb"], ["pb1"], start=(k == 0), stop=(k == 7))
                for c in range(nch):
                    CP("dve", hgv[:, c, :], PB(1, 64 * c, 64 * c + 64, 0, 256), ["pb1"], ["hgv"])
                    CP("act", gmo[:, c, :], PB(1, 64 * c, 64 * c + 64, 256, 512), ["pb1"], ["gmo"])
                for k in range(8):
                    MM(PB(0, 0, ntok, 0, 256), xb[:, k, :], Wt[:, k, 1024:1280], ["Wt", "xb"], ["pb0"], start=(k == 0), stop=(k == 7))
                for c in range(nch):
                    CP("act", ghg[:, c, :], PB(0, 64 * c, 64 * c + 64, 0, 256), ["pb0"], ["ghg"])
                SIGM(sgo[:], gmo[:], ["gmo"], ["sgo"])
                def rows_stream():
                    S.add("dve", lambda e: e.tensor_tensor_scan(out=cl[:], data0=rmask[0:4, 0:ntok], data1=rf[:], initial=0.0, op0=ALU.mult, op1=ALU.add),
                          r=["rf", "rmask"], w=["cl"])
                    yield
                    STT(wv[:], ri[:], bi_col, cl[:], ALU.add, ALU.add, ["ri", "cl", "hrow"], ["wv"])
                    yield
                    S.add("dve", lambda e: e.tensor_reduce(out=wmax[:], in_=wv[:].rearrange("p (c t) -> p c t", t=64), axis=AX.X, op=ALU.max),
                          r=["wv"], w=["wmax"])
                    yield
                    TS("dve", blast[:], cl[:].rearrange("p (c t) -> p c t", t=64)[:, :, 63], -1.0, ALU.mult, ["cl"], ["blast"])
                    yield
                    S.add("dve", lambda e: e.tensor_tensor_scan(out=mcur[:], data0=wmax[:], data1=blast[:], initial=mstate[:], op0=ALU.max, op1=ALU.add),
                          r=["wmax", "blast", "mstate"], w=["mcur"])
                    yield
                    TT("dve", Mc[:], mcur[:], blast[:], ALU.subtract, ["mcur", "blast"], ["Mc"])
                    yield
                    CP("dve", mprev[:, 0:1], mstate[:], ["mstate"], ["mprev"])
                    yield
                    if nch > 1:
                        CP("dve", mprev[:, 1:nch], mcur[:, 0:nch - 1], ["mcur", "mprev"], ["mprev"])
                        yield
                    CP("dve", mstate[:], mcur[:, nch - 1:nch], ["mcur", "mprev"], ["mstate"])
                    yield
                    TT("dve", decr[:], mprev[:], Mc[:], ALU.subtract, ["mprev", "Mc"], ["decr"])
                    yield
                    ACT(decr[:], decr[:], AF.Exp, ["decr"], ["decr"])
                    yield
                    TT("dve", wsr[:].rearrange("p (c t) -> p c t", t=64), wv[:].rearrange("p (c t) -> p c t", t=64), bc_last(Mc[:], 64), ALU.subtract,
                       ["wv", "Mc"], ["wsr"])
                    yield
                    ACT(wsr[:], wsr[:], AF.Exp, ["wsr"], ["wsr"])
                    yield
                    TT("dve", gfr[:].rearrange("p (c t) -> p c t", t=64), cl[:].rearrange("p (c t) -> p c t", t=64), bc_last(Mc[:], 64), ALU.subtract,
                       ["cl", "Mc"], ["gfr"])
                    yield
                    ACT(gfr[:], gfr[:], AF.Exp, ["gfr"], ["gfr"])
                    yield
                    for c in range(nch):
                        TR(PB(2, 0, 64, 8 * c, 8 * c + 4), wsr[:, 64 * c:64 * c + 64], i4[:], ["wsr", "i4"], ["pb2"])
                        yield
                        TR(PB(2, 0, 64, 8 * c + 4, 8 * c + 8), gfr[:, 64 * c:64 * c + 64], i4[:], ["gfr", "i4"], ["pb2"])
                        yield
                    CP("act", wsg[:].rearrange("p c e -> p (c e)"), PB(2, 0, 64, 0, 8 * nch), ["pb2"], ["wsg"])
                    yield
                    TT("dve", decd[:], bc_last(decr[:], 4), bc_mid(i4[:], nch), ALU.mult, ["decr", "i4"], ["decd"])
                    yield
                    MM(PB(2, 0, 64, 64, 64 + 4 * nch), c64[0:4, 5, :], decd[:].rearrange("p c h -> p (c h)"), ["decd", "c64"], ["pb2b"])
                    yield
                    CP("act", decbc[:].rearrange("p c h -> p (c h)"), PB(2, 0, 64, 64, 64 + 4 * nch), ["pb2b"], ["decbc"])
                    yield
                def hpre_stream():
                    SIGM(qsil[:], hqf[:, 0], ["hqf"], ["qsil"])
                    TT("dve", qsil[:], qsil[:], hqf[:, 0], ALU.mult, ["hqf", "qsil"], ["qsil"])
                    yield
                    SIGM(sig[:], hqf[:, 1], ["hqf"], ["sig"])
                    yield
                    for h in range(4):
                        ACT(logf[:, h, :], sig[:, h, :], AF.Ln, ["sig", "oml", "lb"], ["logf"], bias=lb_t[:, h, l:l + 1], scale=oml_t[:, h, l:l + 1])
                        yield
                        TS("pool", kk[:, h, :], sig[:, h, :], noml_t[:, h, l:l + 1], ALU.mult, ["sig", "noml", "oml"], ["kk"],
                           s2=oml_t[:, h, l:l + 1], op1=ALU.add)
                        yield
                    for h in range(4):
                        S.add("dve", lambda e, h=h: e.tensor_tensor_scan(out=lcum[:, h, :], data0=rmask[0:64, 0:ntok], data1=logf[:, h, :], initial=0.0,
                                                                         op0=ALU.mult, op1=ALU.add), r=["logf", "rmask"], w=["lcum"])
                        yield
                    lc4 = lcum[:].rearrange("p h (c t) -> p (h c) t", t=64)
                    yield
                    nhc = 4 * nch
                    yield
                    TT("dve", t1[:].rearrange("p h (c t) -> p (h c) t", t=64), lc4, bc_last(lc4[:, :, 31], 64), ALU.subtract, ["lcum"], ["t1"])
                    yield
                    ACT(t2[:], t1[:], AF.Exp, ["t1"], ["t2"])
                    yield
                    TT("dve", qs_[:], qsil[:], t2[:], ALU.mult, ["qsil", "t2"], ["qs_"])
                    yield
                    ACT(t2[:], t1[:], AF.Exp, ["t1", "qs_"], ["t2"], scale=-1.0)
                    yield
                    TT("dve", ks_[:], kk[:], t2[:], ALU.mult, ["kk", "t2"], ["ks_"])
                    yield
                    ACT(t2[:], lcum[:], AF.Exp, ["lcum", "ks_"], ["t2"])
                    yield
                    TT("dve", qe_[:], qsil[:], t2[:], ALU.mult, ["qsil", "t2"], ["qe_"])
                    yield
                    TT("dve", t1[:].rearrange("p h (c t) -> p (h c) t", t=64), bc_last(lc4[:, :, 63], 64), lc4, ALU.subtract, ["lcum", "t2"], ["t1"])
                    yield
                    ACT(t2[:], t1[:], AF.Exp, ["t1", "qe_"], ["t2"])
                    yield
                    TT("dve", kdT[:], kk[:], t2[:], ALU.mult, ["kk", "t2"], ["kdT"])
                    yield
                    ACT(el[:].rearrange("p h c -> p (h c)"), lc4[:, :, 63], AF.Exp, ["lcum"], ["el"])
                    yield
                    for c in range(nch):
                        cs = slice(64 * c, 64 * c + 64)
                        yield
                        pT = PB16(3, 0, 64, 0, 128)
                        yield
                        for h in range(4):
                            TR(pT[:, h * 64:(h + 1) * 64], kdT[:, h, cs], c64b[:, 4, :], ["kdT", "c64b"], ["pb3a"])
                            yield
                        CP("act", kdtm[:, c].rearrange("p h t -> p (h t)"), pT, ["pb3a"], ["kdtm"])
                        yield
                interleave(rows_stream(), hpre_stream())
                def mlstm_stream():
                    for c in range(nch):
                        cs = slice(64 * c, 64 * c + 64)
                        TT("pool", V2[:], V1[:, c], bc_last(wsg[:, c, 0:4], 65), ALU.mult, ["V1", "wsg"], ["V2"])
                        yield
                        for h in range(4):
                            MM(PB(4, 0, 64, h * 64, (h + 1) * 64), mkT[:, h, cs], mqT[:, h, cs], ["mkT", "mqT"], ["pb4a"])
                        TT("dve", STb[:], pbank[4][0:64, 0:256].rearrange("p (h t) -> p h t", h=4), bc_mid(C_TRIU, 4), ALU.mult, ["pb4a", "c64"], ["STb"])
                        yield
                        TT("dve", Cm[:], Cm[:], bc_last(decbc[:, c, :], 65), ALU.mult, ["Cm", "decbc"], ["Cm"])
                        CP("act", Cmb[:], Cm[:], ["Cm"], ["Cmb"])
                        yield
                        for h in range(4):
                            MM(PB(5, 0, 64, h * 65, (h + 1) * 65), STb[:, h, :], V2[:, h, :], ["STb", "V2"], ["pb5"], start=True, stop=False)
                            MM(PB(5, 0, 64, h * 65, (h + 1) * 65), mqT[:, h, cs], Cmb[:, h, :], ["mqT", "Cmb"], ["pb5"], start=False, stop=True)
                        pn = pbank[5][0:64, 0:260].rearrange("p (h e) -> p h e", h=4)
                        CP("act", hC[:], pn[:, :, 0:64], ["pb5"], ["hC"])
                        CP("act", aden[:], pn[:, :, 64], ["pb5"], ["aden"])
                        yield
                        STT(rec[:], aden[:], -1.0, aden[:], ALU.mult, ALU.max, ["aden"], ["rec"])
                        TT("dve", rec[:], rec[:], wsg[:, c, 4:8], ALU.max, ["rec", "wsg"], ["rec"])
                        S.add("dve", lambda e: e.reciprocal(out=rec[:], in_=rec[:]), r=["rec"], w=["rec"])
                        TT("dve", hC[:], hC[:], bc_last(rec[:], 64), ALU.mult, ["hC", "rec"], ["hC"])
                        TT("pool", O[:, c, 0:4, :], hC[:], sgo[:, c, :].rearrange("p (h d) -> p h d", h=4), ALU.mult, ["hC", "sgo"], ["O"])
                        yield
                        for h in range(4):
                            MM(PB(6, 0, 64, h * 65, (h + 1) * 65), mktm[:, c, h * 64:(h + 1) * 64], V2[:, h, :], ["mktm", "V2"], ["pb6"])
                        TT("dve", Cm[:], Cm[:], pbank[6][0:64, 0:260].rearrange("p (h e) -> p h e", h=4), ALU.add, ["Cm", "pb6", "Cmb"], ["Cm"])
                        yield
                def hgrn_stream():
                    for c in range(nch):
                        cs = slice(64 * c, 64 * c + 64)
                        for h in range(4):
                            MM(PB(2, 0, 64, h * 64, (h + 1) * 64), ks_[:, h, cs], qs_[:, h, cs], ["ks_", "qs_"], ["pb2"])
                        TT("dve", aTb[:], pbank[2][0:64, 0:256].rearrange("p (h t) -> p h t", h=4), bc_mid(C_TRIU, 4), ALU.mult, ["pb2", "c64"], ["aTb"])
                        yield
                        for h in range(4):
                            MM(PB(7, 0, 64, h * 64, (h + 1) * 64), aTb[:, h, :], hgv[:, c, h * 64:(h + 1) * 64], ["aTb", "hgv"], ["pb7a"], start=True, stop=False)
                            MM(PB(7, 0, 64, h * 64, (h + 1) * 64), qe_[:, h, cs], Shb[:, h, :], ["qe_", "Shb"], ["pb7a"], start=False, stop=True)
                        CP("act", O[:, c, 4:8, :].rearrange("p h t -> p (h t)"), PB(7, 0, 64, 0, 256), ["pb7a"], ["O"])
                        yield
                        for h in range(4):
                            MM(PB(7, 0, 64, 256 + h * 64, 256 + (h + 1) * 64), kdtm[:, c, h, :], hgv[:, c, h * 64:(h + 1) * 64], ["kdtm", "hgv"], ["pb7b"])
                        TT("dve", Sh[:], Sh[:], bc_last(el[:, :, c], 64), ALU.mult, ["Sh", "el", "Shb"], ["Sh"])
                        TT("dve", Sh[:].rearrange("p h t -> p (h t)"), Sh[:].rearrange("p h t -> p (h t)"), PB(7, 0, 64, 256, 512), ALU.add, ["Sh", "pb7b"], ["Sh"])
                        CP("act", Shb[:], Sh[:], ["Sh"], ["Shb"])
                        yield
                interleave(mlstm_stream(), hgrn_stream())
                merge(O, Osq, ssq, Ob, nch, l, 4, pos0, gate0=None, gate1=("sigmoid", ghg, sgate, "ghg", 0))
            S.dma("sp", sq["c"](l).rearrange("h k v -> k h v"), Cm[:, :, 0:64], r=["Cm"], chan="osA", final=True)
            S.dma("sp", sq["n"](l).rearrange("h (k o) -> k h o", o=1), Cm[:, :, 64:65], r=["Cm"], chan="osB", final=True, nc_ok=True)
            S.dma("sp", sq["m"](l).rearrange("(h o) -> h o", o=1), mstate[:], r=["mstate"], chan="osC", final=True, nc_ok=True)
            S.dma("sp", sq["hg"](l).rearrange("h k v -> k h v"), Sh[:], r=["Sh"], chan="osD", final=True)

        NW = 8

        def phaseD(sq, l, last_layer):
            T = sq["T"]
            nd = min(512, T)
            ntile = T // nd
            AR.reset()
            S.barrier(dummy[:])
            ring = [AR.alloc((128, 4096), BF16) for _ in range(NW)]
            rx = AR.alloc((128, 8, nd))
            x1b = AR.alloc((128, 8, nd), BF16)
            sqb = AR.alloc((128, 8, nd), BF16)
            hT = AR.alloc((128, 32, nd), BF16)
            tmp = [AR.alloc((128, nd)) for _ in range(2)]
            mean_sb = AR.alloc((128, nd))
            rstd = AR.alloc((128, nd))
            ytm = AR.alloc((128, 1024)) if last_layer else None
            vec = vec_t[:, l, :].rearrange("p (a b) -> p a b", a=5)
            total = ntile * 18
            state = dict(next=0)

            def fetch():
                i = state["next"]
                if i >= total:
                    return
                slot = i % NW
                ci = i % 18
                S.dma("sp", ring[slot][:], wdn_bf[l, ci], r=[f"wbfD{l}"], w=[f"ring{slot}"], chan=f"rg{slot}")
                state["next"] = i + 1

            for _ in range(NW):
                fetch()
            used = dict(n=0)

            def chunk():
                i = used["n"]
                used["n"] = i + 1
                return i % NW

            def layernorm(gi, bi_, need_bf=True):
                CP("act", x1b[:], rx[:], ["rx"], ["x1b"])
                ACT(sqb[:], rx[:], AF.Square, ["rx"], ["sqb"])
                for m in range(8):
                    MM(PB(2, 0, 128, 0, nd), odiv[:], x1b[:, m, :], ["odiv", "x1b"], ["pb2"], start=(m == 0), stop=(m == 7))
                for m in range(8):
                    MM(PB(3, 0, 128, 0, nd), odiv[:], sqb[:, m, :], ["odiv", "sqb"], ["pb3"], start=(m == 0), stop=(m == 7))
                CP("act", mean_sb[:], PB(2, 0, 128, 0, nd), ["pb2"], ["mean"])
                ACT(rstd[:], PB(2, 0, 128, 0, nd), AF.Square, ["pb2"], ["rstd"])
                TT("dve", rstd[:], PB(3, 0, 128, 0, nd), rstd[:], ALU.subtract, ["pb3", "rstd"], ["rstd"])
                RSQRT(rstd[:], rstd[:], EPS5, ["rstd"], ["rstd"])
                for m in range(8):
                    t = tmp[m % 2]
                    tk = f"tmp{m % 2}"
                    TT("dve", t[:], rx[:, m, :], mean_sb[:], ALU.subtract, ["rx", "mean"], [tk])
                    TT("dve", t[:], t[:], rstd[:], ALU.mult, [tk, "rstd"], [tk])
                    ACT(rx[:, m, :], t[:], AF.Identity, [tk, "vec"], ["rx"], bias=vec[:, bi_, m:m + 1], scale=vec[:, gi, m:m + 1])
                    if need_bf:
                        CP("pool", x1b[:, m, :], rx[:, m, :], ["rx"], ["x1b"])

            for ti in range(ntile):
                pos0 = ti * nd
                S.dma("sp", rx[:], xscr[l % 2, :, :, pos0:pos0 + nd], r=[f"xscr{l % 2}"], w=["rx"], chan="rxD")
                for j in range(2):
                    slot = chunk()
                    wv_ = ring[slot][:].rearrange("p (k c) -> p k c", k=8)
                    for mm_ in range(4):
                        m = 4 * j + mm_
                        bank = m % 2
                        for k in range(8):
                            MM(PB(bank, 0, 128, 0, nd), wv_[:, k, mm_ * 128:(mm_ + 1) * 128], oT[:, k, pos0:pos0 + nd], [f"ring{slot}", "oT"], [f"pb{bank}"],
                               start=(k == 0), stop=(k == 7))
                        STT(rx[:, m, :], rx[:, m, :], DN_ALPHA, PB(bank, 0, 128, 0, nd), ALU.mult, ALU.add, ["rx", f"pb{bank}"], ["rx"])
                    fetch()
                layernorm(1, 2)
                for j in range(8):
                    slot = chunk()
                    wv_ = ring[slot][:].rearrange("p (k c) -> p k c", k=8)
                    for mm_ in range(4):
                        m = 4 * j + mm_
                        bank = m % 2
                        for k in range(8):
                            MM(PB(bank, 0, 128, 0, nd), wv_[:, k, mm_ * 128:(mm_ + 1) * 128], x1b[:, k, :], [f"ring{slot}", "x1b"], [f"pb{bank}"],
                               start=(k == 0), stop=(k == 7))
                        t = tmp[m % 2]
                        tk = f"tmp{m % 2}"
                        ACT(t[:], PB(bank, 0, 128, 0, nd), AF.Relu, [f"pb{bank}", "b1"], [tk], bias=b1_t[:, l, m:m + 1], scale=1.0)
                        TT("pool", hT[:, m, :], t[:], t[:], ALU.mult, [tk], ["hT"])
                    fetch()
                for m in range(8):
                    slot = chunk()
                    wv_ = ring[slot][:].rearrange("p (k c) -> p k c", k=32)
                    bank = m % 2
                    for k in range(32):
                        MM(PB(bank, 0, 128, 0, nd), wv_[:, k, :], hT[:, k, :], [f"ring{slot}", "hT"], [f"pb{bank}"], start=(k == 0), stop=(k == 31))
                    t = tmp[m % 2]
                    tk = f"tmp{m % 2}"
                    ACT(t[:], PB(bank, 0, 128, 0, nd), AF.Identity, [f"pb{bank}", "vec"], [tk], bias=vec[:, 0, m:m + 1], scale=1.0)
                    STT(rx[:, m, :], rx[:, m, :], DN_ALPHA, t[:], ALU.mult, ALU.add, ["rx", tk], ["rx"])
                    fetch()
                layernorm(3, 4, need_bf=False)
                if not last_layer:
                    S.dma("sp", xscr[(l + 1) % 2, :, :, pos0:pos0 + nd], rx[:], r=["rx"], w=[f"xscr{(l + 1) % 2}"], chan="xoD")
                else:
                    nsub = (nd + 127) // 128
                    for su in range(nsub):
                        ns = min(128, nd - su * 128)
                        for m in range(8):
                            bank = 4 + m // 4
                            TR(PB(bank, 0, ns, (m % 4) * 128, (m % 4 + 1) * 128), rx[:, m, su * 128:su * 128 + ns], ID128, ["rx", "c128"], [f"pb{bank}"])
                        CP("act", ytm[0:ns, 0:512], PB(4, 0, ns, 0, 512), ["pb4"], ["ytm"])
                        CP("dve", ytm[0:ns, 512:1024], PB(5, 0, ns, 0, 512), ["pb5"], ["ytm"])
                        S.dma("sp", sq["y"][pos0 + su * 128:pos0 + su * 128 + ns, :], ytm[0:ns, :], r=["ytm"], chan="yD", final=True)

        import os
        kstop = os.environ.get("KSTOP", "")
        try:
            for sq in seqs:
                phase0(sq)
                stop_at("p0")
                for l in range(L):
                    passA(sq, l)
                    stop_at("A")
                    passB(sq, l)
                    if KDEBUG and l == 0 and sq is seqs[0]:
                        S.dma("sp", dbg_oT, oT[:], r=["oT"], chan="dbgo", final=True)
                    stop_at("B")
                    phaseD(sq, l, l == L - 1)
                    stop_at("D")
        except StopBuild as e:
            print("build stopped at", e)
        print('arena high water', AR.hi, 'of', AW)
        S.emit()
    return nc


def host_layout_weights(inp, L):
    w_in = np.asarray(inp["w_in"], np.float32)[:L]

    def fm(cols):
        w = w_in[:, :, cols]
        return np.ascontiguousarray(w.reshape(L, 8, 128, len(cols)).transpose(0, 2, 1, 3))

    out = dict(wfa=fm(FA_COLS), wta=fm(TA_COLS), wfb=fm(FB_COLS), wrb=fm(RB_COLS), wtb=fm(TB_COLS))
    w_out = np.asarray(inp["w_out"], np.float32)[:L]
    w1 = np.asarray(inp["w1"], np.float32)[:L]
    w2 = np.asarray(inp["w2"], np.float32)[:L]
    wdn = np.empty((L, 18, 128, 4096), np.float32)
    wo = w_out.reshape(L, 8, 128, 2, 512).transpose(0, 3, 2, 1, 4)
    wdn[:, 0:2] = wo.reshape(L, 2, 128, 4096)
    w1r = w1.reshape(L, 8, 128, 8, 512).transpose(0, 3, 2, 1, 4)
    wdn[:, 2:10] = w1r.reshape(L, 8, 128, 4096)
    w2r = w2.reshape(L, 32, 128, 8, 128).transpose(0, 3, 2, 1, 4)
    wdn[:, 10:18] = w2r.reshape(L, 8, 128, 4096)
    out["wdn"] = wdn
    cw = np.asarray(inp["gdn_conv_w"], np.float32)[:L]
    out["convw"] = np.ascontiguousarray(cw.reshape(L, 4, 6, 128).transpose(0, 3, 2, 1))
    out["b1c"] = np.ascontiguousarray(np.asarray(inp["b1"], np.float32)[:L].reshape(L, 32, 128).transpose(0, 2, 1))
    vs = np.stack([np.asarray(inp[k], np.float32)[:L].reshape(L, 8, 128).transpose(0, 2, 1)
                   for k in ("b2", "ln1_g", "ln1_b", "ln2_g", "ln2_b")], axis=2)
    out["vecs"] = np.ascontiguousarray(vs)
    out["outg"] = np.ascontiguousarray(np.asarray(inp["out_norm_g"], np.float32)[:L])
    out["hvec"] = np.ascontiguousarray(np.stack([np.asarray(inp[k], np.float32)[:L] for k in
                                                 ("gdn_A_log", "gdn_dt_bias", "mlstm_b_i", "mlstm_b_f")], axis=1))
    out["lbl"] = np.ascontiguousarray(np.asarray(inp["hgrn_lb_logits"], np.float32)[:L])
    out.update(make_consts())
    return out


_PROG_CACHE = {}


def run(inp, n_cores, NPS, TP, L, SAMPLE):
    key = (NPS, TP, L, SAMPLE)
    if key not in _PROG_CACHE:
        _PROG_CACHE[key] = build(NPS, TP, L, SAMPLE)
    nc = _PROG_CACHE[key]
    shared = host_layout_weights(inp, L)
    xpr = np.asarray(inp["x_prompt"], np.float32)
    in_maps = []
    for c in range(n_cores):
        m = dict(shared)
        m["xp"] = np.ascontiguousarray(xpr[c * NPS:(c + 1) * NPS])
        if SAMPLE:
            m["xs"] = np.ascontiguousarray(np.asarray(inp["x_sample"], np.float32)[c])
            m["ck"] = np.ascontiguousarray(np.asarray(inp["cache_sb_k"], np.float32)[:L, c].reshape(L, PAST, 256))
            m["cv"] = np.ascontiguousarray(np.asarray(inp["cache_sb_v"], np.float32)[:L, c].reshape(L, PAST, 256))
            m["sconv"] = np.ascontiguousarray(np.asarray(inp["state_gdn_conv"], np.float32)[:L, c])
            m["gS"] = np.ascontiguousarray(np.asarray(inp["state_gdn_S"], np.float32)[:L, c])
            m["mC"] = np.ascontiguousarray(np.asarray(inp["state_mlstm_C"], np.float32)[:L, c])
            m["mn"] = np.ascontiguousarray(np.asarray(inp["state_mlstm_n"], np.float32)[:L, c])
            m["mm"] = np.ascontiguousarray(np.asarray(inp["state_mlstm_m"], np.float32)[:L, c])
            m["hS"] = np.ascontiguousarray(np.asarray(inp["state_hgrn_S"], np.float32)[:L, c])
        in_maps.append(m)
    res = run_bass_kernel_spmd(nc, in_maps, core_ids=list(range(n_cores)))
    R = res.results
    global LAST_RESULTS
    LAST_RESULTS = R

    def cat(name, axis):
        return np.concatenate([np.asarray(r[name], np.float32) for r in R], axis=axis)

    y_p = cat("y_p", 0)
    outs = [y_p]
    if SAMPLE:
        outs.append(np.stack([np.asarray(r["y_s"], np.float32) for r in R], 0))
    else:
        outs.append(None)
    B = n_cores * NPS
    outs += [cat("p_k", 1).reshape(L, B, TP, 4, 64), cat("p_v", 1).reshape(L, B, TP, 4, 64), cat("p_conv", 1), cat("p_gdn", 1),
             cat("p_c", 1), cat("p_n", 1), cat("p_m", 1), cat("p_hg", 1)]
    if SAMPLE:
        outs += [cat("s_k", 1).reshape(L, n_cores, 64, 4, 64), cat("s_v", 1).reshape(L, n_cores, 64, 4, 64), cat("s_conv", 1),
                 cat("s_gdn", 1), cat("s_c", 1), cat("s_n", 1), cat("s_m", 1), cat("s_hg", 1)]
    return tuple(outs)


def kernel(**inputs):
    return run(inputs, 8, 2, 2048, DEPTH, True)
```
